# Optimizing a Trainium2 kernel written in Bass

```python
import jax, jax.numpy as jnp
from jax import lax
import numpy as np

D_MODEL = 1024
BATCH = 2
SEQ = 8192
DEPTH = 1

N_META = 16
CHUNK = 128
META_PAD = CHUNK - N_META
RET_HEADS = 4
RET_DK = 128
RET_DV = 256
ATT_HEADS = 8
ATT_GROUPS = 2
ATT_HD = 128
WINDOW = 128
ROPE_THETA = 10000.0
EPS = 1e-6
NEG_INF = -1e30
RET_QK = RET_HEADS * RET_DK
RET_V = RET_HEADS * RET_DV
ATT_Q = ATT_HEADS * ATT_HD
ATT_KV = ATT_GROUPS * ATT_HD
IN_SPLIT = (RET_QK, RET_QK, RET_V, RET_V, ATT_Q, ATT_KV, ATT_KV, ATT_Q, D_MODEL, D_MODEL)
D_IN = 2 * RET_QK + 2 * RET_V + 2 * ATT_Q + 2 * ATT_KV + 2 * D_MODEL

kernel_name = "hybrid_retention_swa_encoder_block"


def _rms_norm(x, w):
    xf = x.astype(jnp.float32)
    y = xf * lax.rsqrt(jnp.mean(xf * xf, axis=-1, keepdims=True) + EPS)
    return (y * w.astype(jnp.float32)).astype(x.dtype)


def _split_in(z):
    parts = []
    start = 0
    for size in IN_SPLIT:
        parts.append(z[..., start:start + size])
        start += size
    return parts


def _rotary(t, pos):
    d = t.shape[-1]
    half = d // 2
    inv = ROPE_THETA ** (-jnp.arange(half, dtype=jnp.float32) * 2.0 / d)
    ang = pos[:, None] * inv[None, :]
    cos = jnp.cos(ang)[None, :, None, :]
    sin = jnp.sin(ang)[None, :, None, :]
    tf = t.astype(jnp.float32)
    t1, t2 = tf[..., :half], tf[..., half:]
    return jnp.concatenate([t1 * cos - t2 * sin, t2 * cos + t1 * sin], axis=-1)


def _pad_front(t):
    return jnp.pad(t, ((0, 0), (META_PAD, 0)) + ((0, 0),) * (t.ndim - 2))


def _retention_direction(q, k, v, log_gamma, strict):
    B, L, H, DK = q.shape
    DV = v.shape[-1]
    nc = L // CHUNK
    q = q.reshape(B, nc, CHUNK, H, DK)
    k = k.reshape(B, nc, CHUNK, H, DK)
    v = v.reshape(B, nc, CHUNK, H, DV)
    idx = jnp.arange(CHUNK, dtype=jnp.float32)
    rel = idx[:, None] - idx[None, :]
    vis = (rel > 0) if strict else (rel >= 0)
    decay = jnp.where(vis[None], jnp.exp(jnp.where(vis, rel, 0.0)[None] * log_gamma[:, None, None]), 0.0)
    s = jnp.einsum("bnihd,bnjhd->bnhij", q, k) * decay[None, None]
    inner = jnp.einsum("bnhij,bnjhv->bnihv", s, v)
    k_decay = jnp.exp((CHUNK - 1 - idx)[:, None] * log_gamma[None, :])
    kv = jnp.einsum("bnjhd,bnjhv->bnhdv", k * k_decay[None, None, :, :, None], v)
    chunk_decay = jnp.exp(CHUNK * log_gamma)[None, :, None, None]

    def step(state, kv_c):
        return chunk_decay * state + kv_c, state

    _, s_prev = lax.scan(step, jnp.zeros((B, H, DK, DV), jnp.float32), jnp.moveaxis(kv, 1, 0))
    s_prev = jnp.moveaxis(s_prev, 0, 1)
    q_decay = jnp.exp((idx + 1.0)[:, None] * log_gamma[None, :])
    cross = jnp.einsum("bnihd,bnhdv->bnihv", q * q_decay[None, None, :, :, None], s_prev)
    return (inner + cross).reshape(B, L, H, DV)


def _bidirectional_retention(q, k, v, log_gamma_f, log_gamma_b):
    fwd = _retention_direction(q, k, v, log_gamma_f, False)
    flip = lambda t: jnp.flip(t, axis=1)
    bwd = flip(_retention_direction(flip(q), flip(k), flip(v), log_gamma_b, True))
    return fwd + bwd


def _windowed_sink_attention(q, k, v, sink):
    B, Lp, H, hd = q.shape
    G = k.shape[2]
    R = H // G
    nb = Lp // CHUNK
    qb = q.reshape(B, nb, CHUNK, G, R, hd)

    def band(t):
        tp = jnp.pad(t, ((0, 0), (CHUNK, CHUNK), (0, 0), (0, 0))).reshape(B, nb + 2, CHUNK, G, hd)
        return jnp.concatenate([tp[:, :-2], tp[:, 1:-1], tp[:, 2:]], axis=2)

    kb, vb = band(k), band(v)
    km, vm = k[:, META_PAD:CHUNK], v[:, META_PAD:CHUNK]
    qpos = jnp.arange(Lp).reshape(nb, CHUNK)
    kpos = (jnp.arange(nb)[:, None] - 1) * CHUNK + jnp.arange(3 * CHUNK)[None, :]
    kp = kpos[:, None, :]
    ok = (jnp.abs(qpos[:, :, None] - kp) <= WINDOW) & (kp >= CHUNK) & (kp < Lp)
    scale = ATT_HD ** -0.5
    s_band = jnp.einsum("bnqgrd,bnkgd->bngrqk", qb, kb) * scale
    s_band = jnp.where(ok[None, :, None, None], s_band, NEG_INF)
    s_meta = jnp.einsum("bnqgrd,bmgd->bngrqm", qb, km) * scale
    s = jnp.concatenate([s_band, s_meta], axis=-1)
    sink_l = sink.astype(jnp.float32).reshape(G, R)[None, None, :, :, None, None]
    m = jnp.maximum(jnp.max(s, axis=-1, keepdims=True), sink_l)
    p = jnp.exp(s - m)
    p = p / (jnp.sum(p, axis=-1, keepdims=True) + jnp.exp(sink_l - m))
    o = (jnp.einsum("bngrqk,bnkgd->bnqgrd", p[..., :3 * CHUNK], vb)
         + jnp.einsum("bngrqm,bmgd->bnqgrd", p[..., 3 * CHUNK:], vm))
    return o.reshape(B, Lp, H, hd)


def _hybrid_layer(h, pre_w, w_in, dec_f, dec_b, ret_nw, w_rb, sink, w_ab, w_o, post_w):
    B, T, _ = h.shape
    u = _rms_norm(h, pre_w)
    z = u @ w_in
    rq, rk, rv, rg, aq, ak, av, ag, gr, ga = _split_in(z)
    pos = jnp.arange(T, dtype=jnp.float32)

    rq = _pad_front(_rotary(rq.reshape(B, T, RET_HEADS, RET_DK), pos))
    rk = _pad_front(_rotary(rk.reshape(B, T, RET_HEADS, RET_DK), pos) * (RET_DK ** -0.5))
    rv = _pad_front(rv.reshape(B, T, RET_HEADS, RET_DV).astype(jnp.float32))
    lg_f = jax.nn.log_sigmoid(dec_f.astype(jnp.float32))
    lg_b = jax.nn.log_sigmoid(dec_b.astype(jnp.float32))
    o_r = _bidirectional_retention(rq, rk, rv, lg_f, lg_b)[:, META_PAD:]
    mu = jnp.mean(o_r, axis=-1, keepdims=True)
    var = jnp.mean(jnp.square(o_r - mu), axis=-1, keepdims=True)
    o_r = ((o_r - mu) * lax.rsqrt(var + EPS)).reshape(B, T, RET_V)
    o_r = o_r * ret_nw.astype(jnp.float32) * jax.nn.silu(rg.astype(jnp.float32))
    y_r = o_r.astype(h.dtype) @ w_rb

    aq = _pad_front(_rotary(aq.reshape(B, T, ATT_HEADS, ATT_HD), pos))
    ak = _pad_front(_rotary(ak.reshape(B, T, ATT_GROUPS, ATT_HD), pos))
    av = _pad_front(av.reshape(B, T, ATT_GROUPS, ATT_HD).astype(jnp.float32))
    o_a = _windowed_sink_attention(aq, ak, av, sink)[:, META_PAD:].reshape(B, T, ATT_Q)
    o_a = o_a * jax.nn.silu(ag.astype(jnp.float32))
    y_a = o_a.astype(h.dtype) @ w_ab

    mix = jax.nn.sigmoid(gr) * y_r + jax.nn.sigmoid(ga) * y_a
    out = mix @ w_o
    return h + _rms_norm(out, post_w)


def setup_inputs(seed: int = 0) -> dict:
    key = jax.random.key(seed)
    ks = jax.random.split(key, 12)
    f32 = jnp.float32
    nrm = jax.random.normal
    base_decay = jnp.log(2.0 ** (5.0 + jnp.arange(RET_HEADS, dtype=f32)) - 1.0)
    return {
        "x": nrm(ks[0], (BATCH, SEQ, D_MODEL), f32),
        "meta_tokens": nrm(ks[1], (N_META, D_MODEL), f32),
        "pre_norm_w": 1.0 + 0.02 * nrm(ks[2], (DEPTH, D_MODEL), f32),
        "w_in": nrm(ks[3], (DEPTH, D_MODEL, D_IN), f32) * D_MODEL ** -0.5,
        "ret_decay_fwd": base_decay[None] + 0.1 * nrm(ks[4], (DEPTH, RET_HEADS), f32),
        "ret_decay_bwd": base_decay[None] + 0.1 * nrm(ks[5], (DEPTH, RET_HEADS), f32),
        "ret_norm_w": 1.0 + 0.02 * nrm(ks[6], (DEPTH, RET_V), f32),
        "w_ret_branch": nrm(ks[7], (DEPTH, RET_V, D_MODEL), f32) * RET_V ** -0.5,
        "attn_sink": 0.5 * nrm(ks[8], (DEPTH, ATT_HEADS), f32),
        "w_attn_branch": nrm(ks[9], (DEPTH, ATT_Q, D_MODEL), f32) * ATT_Q ** -0.5,
        "w_out": nrm(ks[10], (DEPTH, D_MODEL, D_MODEL), f32) * D_MODEL ** -0.5,
        "post_norm_w": 1.0 + 0.02 * nrm(ks[11], (DEPTH, D_MODEL), f32),
    }


def reference(x, meta_tokens, pre_norm_w, w_in, ret_decay_fwd, ret_decay_bwd, ret_norm_w,
              w_ret_branch, attn_sink, w_attn_branch, w_out, post_norm_w):
    B = x.shape[0]
    meta = jnp.broadcast_to(meta_tokens.astype(x.dtype)[None], (B, N_META, x.shape[-1]))
    h = jnp.concatenate([meta, x], axis=1)
    for layer in range(DEPTH):
        h = _hybrid_layer(h, pre_norm_w[layer], w_in[layer], ret_decay_fwd[layer], ret_decay_bwd[layer],
                          ret_norm_w[layer], w_ret_branch[layer], attn_sink[layer], w_attn_branch[layer],
                          w_out[layer], post_norm_w[layer])
    return h[:, N_META:]
```

```python
import contextlib
import numpy as np
import ml_dtypes
import concourse.bass as bass
import concourse.mybir as mybir
from concourse.bass_utils import run_bass_kernel_spmd

F32 = mybir.dt.float32
BF16 = mybir.dt.bfloat16
AF = mybir.ActivationFunctionType
ALU = mybir.AluOpType

SAME_ENGINE_SYNC = True
DEFER_W = True
PIPE_A = True
ENGS = ['pe', 'dve', 'act', 'pool', 'sp']

D_MODEL = 1024
SEQ = 8192
N_META = 16
NSLOT = 64
NOWN = 16
EPS = 1e-6
SCALE = 128 ** -0.5
LN_SCALE = float(-0.5 * np.log(128.0))
NEG = -30000.0


class Res:
    __slots__ = ('name', 'last_w', 'readers', 'excl')

    def __init__(self, name):
        self.name = name
        self.last_w = None
        self.readers = []
        self.excl = False


class Sched:
    def __init__(self, nc):
        self.nc = nc
        self.ops = []

    def add(self, eng, fn, reads=(), writes=(), dma=None, extra=()):
        self.ops.append((eng, fn, tuple(reads), tuple(writes), dma, tuple(extra)))
        return len(self.ops) - 1

    def build(self):
        nc = self.nc
        last = {}
        for i, op in enumerate(self.ops):
            if op[4] is not None:
                last[op[4]] = i
        self.add('sp', None, extra=sorted(last.values()))
        ops = self.ops
        n = len(ops)
        deps = [None] * n
        signaled = [False] * n
        for i, (eng, fn, reads, writes, dk, extra) in enumerate(ops):
            d = set(extra)
            for r in reads:
                if r.last_w is not None:
                    d.add(r.last_w)
                if r.excl:
                    d.update(j for j in r.readers if ops[j][0] != eng)
            for w in writes:
                if w.last_w is not None:
                    d.add(w.last_w)
                d.update(w.readers)
            d.discard(i)
            dd = set()
            latest = {}
            for j in d:
                ej = ops[j][0]
                dmaj = ops[j][4] is not None
                if dmaj and dk is not None and ops[j][4] == dk and dk.startswith('T:'):
                    continue
                if (not dmaj) and ej == eng and dk is None and (eng == 'pe' or not SAME_ENGINE_SYNC):
                    continue
                if dmaj:
                    dd.add(j)
                else:
                    latest[ej] = max(latest.get(ej, -1), j)
            dd.update(latest.values())
            deps[i] = dd
            for j in dd:
                signaled[j] = True
            for r in reads:
                r.readers.append(i)
            for w in writes:
                w.last_w = i
                w.readers = []
        cnt = {e: 0 for e in ENGS}
        val = [0] * n
        dmacnt = {}
        for i, op in enumerate(ops):
            if op[4] is not None:
                dmacnt[op[4]] = dmacnt.get(op[4], 0) + 1
                val[i] = 16 * dmacnt[op[4]]
            elif signaled[i]:
                cnt[op[0]] += 1
                val[i] = cnt[op[0]]
        for i, op in enumerate(ops):
            if op[4] is not None and op[4].startswith('T:'):
                val[i] = 16 * dmacnt[op[4]]
        self.stats = dict(n_ops=n, milestones=dict(cnt), dma_keys=len(dmacnt))
        with contextlib.ExitStack() as st:
            sem_eng = {e: st.enter_context(nc.semaphore('s_' + e)) for e in ENGS}
            dma_sem = {k: st.enter_context(nc.semaphore('d_' + k.replace(':', '_'))) for k in dmacnt}

            def emit(engname, e):
                waited = {}
                for i, op in enumerate(ops):
                    if op[0] != engname:
                        continue
                    need = {}
                    for j in deps[i]:
                        if ops[j][4] is not None:
                            key = ('d', ops[j][4])
                            sem = dma_sem[ops[j][4]]
                        else:
                            key = ('e', ops[j][0])
                            sem = sem_eng[ops[j][0]]
                        if val[j] > need.get(key, (None, 0))[1]:
                            need[key] = (sem, val[j])
                    for key, (sem, v) in need.items():
                        if waited.get(key, 0) >= v:
                            continue
                        e.wait_ge(sem, v)
                        waited[key] = v
                    if op[1] is None:
                        continue
                    inst = op[1](e)
                    if op[4] is not None:
                        inst.then_inc(dma_sem[op[4]], 16)
                    elif signaled[i]:
                        inst.then_inc(sem_eng[engname], 1)

            with nc.Block() as block:
                @block.tensor
                def _(e):
                    emit('pe', e)

                @block.vector
                def _(e):
                    emit('dve', e)

                @block.scalar
                def _(e):
                    emit('act', e)

                @block.gpsimd
                def _(e):
                    emit('pool', e)

                @block.sync
                def _(e):
                    emit('sp', e)


def I(name, *a, **kw):
    return lambda e: getattr(e, name)(*a, **kw)


def build_nc(debug=False, stop_after=None, a_steps=19):
    nc = bass.Bass("TRN2", target_bir_lowering=False)

    def din(name, shape, dt=F32):
        return nc.dram_tensor(name, shape, dt, kind="ExternalInput").ap()

    xs = din("xs", [NSLOT * 128, 1024])
    css = din("css", [NSLOT * 128, 128])
    xm = din("xm", [19 * 128, 1024])
    csm = din("csm", [19 * 128, 128])
    w_in = din("w_in", [1024, 7680])
    w_rb = din("w_rb", [1024, 1024])
    w_ab = din("w_ab", [1024, 1024])
    w_o = din("w_o", [1024, 1024])
    prew = din("prew", [128, 1024])
    postw = din("postw", [128, 1024])
    nwb = din("nwb", [128, 1024])
    dec8 = din("dec8", [128, 8])
    sink8 = din("sink8", [128, 8])
    rtab = din("rtab", [128, 512])
    ctab = din("ctab", [128, 256])
    jtab = din("jtab", [128, 2])
    acttab = din("acttab", [128, 128])
    masks = din("masks", [128, 4 * 512], BF16)
    identd = din("ident", [128, 128], BF16)
    out = nc.dram_tensor("out", [NOWN * 128, 1024], F32, kind="ExternalOutput").ap()
    sbst = nc.dram_tensor("sbst", [NOWN * 128, 1024], BF16, kind="Internal").ap()
    mixrd = nc.dram_tensor("mixrd", [NOWN * 128, 1024], BF16, kind="Internal").ap()
    kspill = nc.dram_tensor("kspill", [NOWN * 128, 512], BF16, kind="Internal").ap()
    vspill = nc.dram_tensor("vspill", [NOWN * 128, 1024], BF16, kind="Internal").ap()
    wbf = nc.dram_tensor("wbf", [1024, 3584], BF16, kind="Internal").ap()
    wabf = nc.dram_tensor("wabf", [1024, 1024], BF16, kind="Internal").ap()
    dbg = {}
    if debug:
        dbg['sf'] = nc.dram_tensor("dbg_sf", [128, 1024], F32, kind="ExternalOutput").ap()
        dbg['sb'] = nc.dram_tensor("dbg_sb", [128, 1024], F32, kind="ExternalOutput").ap()
        dbg['mixr'] = nc.dram_tensor("dbg_mixr", [NOWN * 128, 1024], BF16, kind="ExternalOutput").ap()

    with contextlib.ExitStack() as st:
        RES = {}

        def R(name):
            if name not in RES:
                RES[name] = Res(name)
            return RES[name]

        def sb(name, shape, dt):
            t = st.enter_context(nc.sbuf_tensor(name, shape, dt))
            R(name)
            return t

        def ps(name, shape, dt):
            return st.enter_context(nc.psum_tensor(name, shape, dt))

        S = Sched(nc)

        WW = sb("WW", [128, 8, 4096], BF16)
        W2 = sb("W2", [128, 8, 1024], BF16)
        W3 = sb("W3", [128, 8, 1024], BF16)
        prew_t = sb("prew_t", [128, 1024], F32)
        nw_t = sb("nw_t", [128, 1024], F32)
        ident = sb("ident_t", [128, 128], BF16)
        mask_t = sb("mask_t", [128, 4 * 512], BF16)
        dec_t = sb("dec_t", [128, 8], F32)
        sink_t = sb("sink_t", [128, 8], F32)
        jtab_t = sb("jtab_t", [128, 2], F32)
        act_t = sb("act_t", [128, 128], F32)
        e8 = sb("e8", [128, 8], F32)
        lg8 = sb("lg8", [128, 8], F32)
        lg128 = sb("lg128", [128, 8], F32)
        DT = sb("DT", [128, 512], BF16)
        QDF = sb("QDF", [128, 512], F32)
        QDB = sb("QDB", [128, 512], F32)
        KD8 = sb("KD8", [128, 8], F32)
        CD8 = sb("CD8", [128, 8], F32)
        KDSf = sb("KDSf", [128, 256], F32)
        KDSb = sb("KDSb", [128, 256], F32)
        AFt = sb("AFt", [128, 256], F32)
        ABt = sb("ABt", [128, 256], F32)
        zrow = sb("zrow", [128, 128], F32)
        S_f = sb("S_f", [128, 1024], F32)
        S_b = sb("S_b", [128, 1024], F32)
        S_fbf1 = sb("S_fbf0", [128, 1024], BF16)
        S_fbf = [S_fbf1, S_fbf1]
        sbst2 = sb("sbst2", [128, 2048], BF16)
        sbstg = [sbst2[:, 0:1024], sbst2[:, 1024:2048]]
        R("sbstg0")
        R("sbstg1")
        XB = [sb("XB%d" % i, [128, 1024], F32) for i in range(3)]
        XB.append(sbst2.bitcast(F32))
        R("XB3")
        CS = [sb("CS%d" % i, [128, 128], F32) for i in range(4)]
        ssq = [sb("ssq%d" % i, [128, 1], F32) for i in range(2)]
        vv = [sb("vv%d" % i, [128, 1], F32) for i in range(2)]
        rstd = [sb("rstd%d" % i, [128, 1], F32) for i in range(2)]
        mhalf = sb("mhalf", [128, 8], F32)
        u0 = sb("u0", [128, 1024], F32)
        ub = [sb("ub%d" % i, [128, 1024], BF16) for i in range(2)]
        uT = [sb("uT%d" % i, [128, 8, 128], BF16) for i in range(2)]
        rotA = [sb("rotA%d" % i, [128, 512], F32) for i in range(2)]
        rotB = [sb("rotB%d" % i, [128, 512], F32) for i in range(2)]
        k_rot = [sb("k_rot%d" % i, [128, 512], BF16) for i in range(2)]
        q_rot1 = sb("q_rot", [128, 1024], BF16)
        q_rot = [q_rot1, q_rot1]
        kdf = [sb("kdf%d" % i, [128, 512], BF16) for i in range(2)]
        kdb = [sb("kdb%d" % i, [128, 512], BF16) for i in range(2)]
        v_tok = [sb("v_tok%d" % i, [128, 1024], BF16) for i in range(2)]
        qT = [sb("qT%d" % i, [128, 512], BF16) for i in range(2)]
        kT1 = sb("kT", [128, 512], BF16)
        qdf1 = sb("qdf", [128, 512], BF16)
        qdb1 = sb("qdb", [128, 512], BF16)
        sd1 = sb("sd", [128, 512], BF16)
        kT2 = sb("kT2", [128, 512], BF16)
        kT, qdf, qdb, sd = [kT1, kT2], [qdf1] * 2, [qdb1] * 2, [sd1] * 2
        kTR = ["kT", "kT2"]
        g_r = [sb("g_r%d" % i, [128, 1024], BF16) for i in range(2)]
        sg_r = [sb("sg_r%d" % i, [128, 1024], BF16) for i in range(2)]
        gnst = sb("gnst", [128, 24], F32)
        gnmv = sb("gnmv", [128, 8], F32)
        gnve = sb("gnve", [128, 4], F32)
        gnrs = sb("gnrs", [128, 4], F32)
        gnnm = sb("gnnm", [128, 4], F32)
        o_n = sb("o_n", [128, 1024], BF16)
        o_g = sb("o_g", [128, 1024], BF16)
        o_gT = sb("o_gT", [128, 8, 128], BF16)
        mixr1 = sb("mixr", [128, 1024], BF16)
        mixr = [mixr1, mixr1]
        KT = [sb("KT%d" % i, [128, 2, 128], BF16) for i in range(4)]
        VA = [sb("VA%d" % i, [128, 2, 130], BF16) for i in range(4)]
        KTm = sb("KTm", [128, 2, 128], BF16)
        Vm = sb("Vm", [128, 2, 130], BF16)
        PTb = [[kdf[0], kdf[1], kdb[0]], [kdb[1], qT[0], qT[1]]]
        PTbR = [["kdf0", "kdf1", "kdb0"], ["kdb1", "qT0", "qT1"]]
        PTm = [[kT1, qdf1], [qdb1, sd1]]
        PTmR = [["kT", "qdf"], ["qdb", "sd"]]
        th_t = QDF
        rl = sb("rl", [128, 4], F32)
        mixa = S_f
        mixb = o_n
        mixT = S_fbf[0][:].rearrange("p (a b) -> p a b", a=8, b=128)
        tfin = S_b

        rtab_t = rotA[0]
        RES["rtab_t"] = RES["rotA0"]
        ctab_t = u0[:, 256:512]
        tmpA = u0[:, 0:256]
        tmp1 = u0[:, 512:640]
        tmp2 = u0[:, 640:768]
        for nm_ in ("ctab_t", "tmpA", "tmp1", "tmp2"):
            RES[nm_] = RES["u0"]
        print("sbuf remaining", nc.sbuf_bytes_remaining() if callable(getattr(nc, 'sbuf_bytes_remaining', None)) else nc.sbuf_bytes_remaining, flush=True)
        PA = ps("PA", [128, 1024], F32)
        PTr = ps("PTr", [128, 2048], BF16)
        PS = ps("PS", [128, 1024], F32)
        PO = ps("PO", [128, 1024], F32)
        for nm in ["PA0", "PA1", "PT0", "PT1", "PS0", "PS1", "PO0", "PO1", "sbst_d", "mixr_d"]:
            R(nm)
        for nm in ["PA0", "PA1", "PT0", "PT1", "PS0", "PS1", "PO0", "PO1"]:
            R(nm).excl = True
        PTf = PTr.bitcast(F32)
        PA_all = {
            'state': [(PA[:, 0:512], R("PA0")), (PA[:, 512:1024], R("PA1")), (PTf[:, 512:1024], R("PT1"))],
            'main': [(PA[:, 0:512], R("PA0")), (PA[:, 512:1024], R("PA1"))],
        }
        PT_all = {
            'state': [(PTr[:, 0:1024], R("PT0"))],
            'main': [(PTr[:, 0:1024], R("PT0")), (PTr[:, 1024:2048], R("PT1"))],
        }
        mode = ['state']
        cnt = dict(pa=0, pt=0, x=0, xa=0, par=0)

        def nxt(key, mod):
            v = cnt[key]
            cnt[key] = v + 1
            return v % mod

        def ld(dst, src, rname, q='sp', key='T:setup'):
            S.add(q, I('dma_start', out=dst, in_=src), writes=[R(rname)], dma=key)

        ld(dec_t[:], dec8, "dec_t")
        ld(sink_t[:], sink8, "sink_t")
        ld(rtab_t[:], rtab, "rtab_t")
        ld(ctab_t, ctab, "ctab_t")
        ld(jtab_t[:], jtab, "jtab_t")
        ld(act_t[:], acttab, "act_t")
        ld(ident[:], identd, "ident_t")
        ld(mask_t[:], masks, "mask_t")
        ld(prew_t[:], prew, "prew_t")
        ld(nw_t[:], nwb, "nw_t")

        def wload(dst_tile, rname, src, c0, n, d0, key):
            rn = rname if isinstance(rname, (list, tuple)) else [rname]
            for kt in range(8):
                S.add('pool', I('dma_start', out=dst_tile[:, kt, d0:d0 + n], in_=src[kt * 128:(kt + 1) * 128, c0:c0 + n]),
                      writes=[R(r_) for r_ in rn], dma=key)

        wload(WW, "WW", w_in, 512, 512, 512, 'T:w1')
        wload(WW, "WW", w_in, 1024, 1024, 1024, 'T:w1')

        S.add('pool', I('memset', mhalf[:], -0.5), writes=[R("mhalf")])
        S.add('pool', I('memset', zrow[:], 0.0), writes=[R("zrow")])
        S.add('pool', I('memset', S_f[:], 0.0), writes=[R("S_f")])
        S.add('pool', I('memset', S_b[:], 0.0), writes=[R("S_b")])
        for i in range(4):
            S.add('pool', I('memset', VA[i][:], 1.0), writes=[R("VA%d" % i)])
        S.add('pool', I('memset', Vm[:], 0.0), writes=[R("Vm")])
        S.add('pool', I('memset', Vm[0:16, :, 128:129], 1.0), writes=[R("Vm")])
        S.add('pool', I('memset', Vm[32:33, :, 128:129], 1.0), writes=[R("Vm")])
        S.add('act', I('activation', out=e8[:], in_=dec_t[:], func=AF.Exp, scale=-1.0), reads=[R("dec_t")], writes=[R("e8")])
        S.add('act', I('activation', out=e8[:], in_=e8[:], func=AF.Ln, bias=1.0), reads=[R("e8")], writes=[R("e8")])
        S.add('dve', I('tensor_scalar', out=lg8[:], in0=e8[:], scalar1=-1.0, scalar2=None, op0=ALU.mult), reads=[R("e8")], writes=[R("lg8")])
        S.add('dve', I('tensor_scalar', out=lg128[:], in0=e8[:], scalar1=-128.0, scalar2=None, op0=ALU.mult), reads=[R("e8")], writes=[R("lg128")])
        for h in range(4):
            hs = slice(h * 128, (h + 1) * 128)
            S.add('act', I('activation', out=tmp1, in_=rtab_t[:, 0:128], func=AF.Exp, scale=lg8[:, h:h + 1]),
                  reads=[R("rtab_t"), R("lg8")], writes=[R("tmp1")])
            S.add('act', I('activation', out=tmp2, in_=rtab_t[:, 256:384], func=AF.Exp, scale=lg8[:, 4 + h:5 + h]),
                  reads=[R("rtab_t"), R("lg8")], writes=[R("tmp2")])
            S.add('dve', I('tensor_tensor', out=tmp1, in0=tmp1, in1=rtab_t[:, 128:256], op=ALU.mult), reads=[R("tmp1"), R("rtab_t")], writes=[R("tmp1")])
            S.add('dve', I('tensor_tensor', out=tmp2, in0=tmp2, in1=rtab_t[:, 384:512], op=ALU.mult), reads=[R("tmp2"), R("rtab_t")], writes=[R("tmp2")])
            S.add('dve', I('tensor_tensor', out=DT[:, hs], in0=tmp1, in1=tmp2, op=ALU.add), reads=[R("tmp1"), R("tmp2")], writes=[R("DT")])
            S.add('act', I('activation', out=QDF[:, hs], in_=ctab_t[:, 0:128], func=AF.Exp, scale=lg8[:, h:h + 1]),
                  reads=[R("ctab_t"), R("lg8")], writes=[R("QDF")])
            S.add('act', I('activation', out=QDB[:, hs], in_=ctab_t[:, 128:256], func=AF.Exp, scale=lg8[:, 4 + h:5 + h]),
                  reads=[R("ctab_t"), R("lg8")], writes=[R("QDB")])
            S.add('act', I('activation', out=KD8[:, h:h + 1], in_=jtab_t[:, 0:1], func=AF.Exp, scale=lg8[:, h:h + 1], bias=LN_SCALE),
                  reads=[R("jtab_t"), R("lg8")], writes=[R("KD8")])
            S.add('act', I('activation', out=KD8[:, 4 + h:5 + h], in_=jtab_t[:, 1:2], func=AF.Exp, scale=lg8[:, 4 + h:5 + h], bias=LN_SCALE),
                  reads=[R("jtab_t"), R("lg8")], writes=[R("KD8")])
        S.add('act', I('activation', out=CD8[:], in_=lg128[:], func=AF.Exp), reads=[R("lg128")], writes=[R("CD8")])

        def bc_slot(tile, off):
            return bass.AP(tile, off, [[tile.shape[1], 128], [0, NSLOT], [1, 4]])

        def bc_head(tile, off):
            return bass.AP(tile, off, [[tile.shape[1], 128], [1, NSLOT], [0, 4]])

        def v3(tile):
            ap_ = tile if isinstance(tile, bass.AP) else tile[:]
            return ap_.rearrange("p (s h) -> p s h", s=NSLOT, h=4)

        S.add('dve', I('tensor_tensor', out=v3(KDSf), in0=bc_slot(KD8, 0), in1=bc_head(act_t, 0), op=ALU.mult),
              reads=[R("KD8"), R("act_t")], writes=[R("KDSf")])
        S.add('dve', I('tensor_tensor', out=v3(KDSb), in0=bc_slot(KD8, 4), in1=bc_head(act_t, 64), op=ALU.mult),
              reads=[R("KD8"), R("act_t")], writes=[R("KDSb")])
        S.add('dve', I('tensor_tensor', out=v3(tmpA), in0=bc_slot(lg128, 0), in1=bc_head(act_t, 0), op=ALU.mult),
              reads=[R("lg128"), R("act_t")], writes=[R("tmpA")])
        S.add('act', I('activation', out=AFt[:], in_=tmpA, func=AF.Exp), reads=[R("tmpA")], writes=[R("AFt")])
        S.add('dve', I('tensor_tensor', out=v3(tmpA), in0=bc_slot(lg128, 4), in1=bc_head(act_t, 64), op=ALU.mult),
              reads=[R("lg128"), R("act_t"), R("AFt")], writes=[R("tmpA")])
        S.add('act', I('activation', out=ABt[:], in_=tmpA, func=AF.Exp), reads=[R("tmpA")], writes=[R("ABt")])
        deferred = []

        def wload_def(dst_tile, rname, src, c0, n, d0, key):
            for kt in range(8):
                deferred.append(lambda kt=kt: S.add('pool', I('dma_start', out=dst_tile[:, kt, d0:d0 + n], in_=src[kt * 128:(kt + 1) * 128, c0:c0 + n]),
                                                   writes=[R(rname)], dma=key))

        wload_def(WW, "WWq", w_in, 0, 512, 0, 'T:w2')
        wload_def(WW, "WWq", w_in, 2048, 1024, 2048, 'T:w2')
        wload_def(WW, "WWq", w_in, 5632, 1024, 3072, 'T:w2')
        wload_def(W2, "W2", w_rb, 0, 1024, 0, 'T:w2')
        wload_def(W3, "W3", w_o, 0, 1024, 0, 'T:w3')

        def precast_def(dst, rname, src, c0, n, d0, key):
            for kt in range(8):
                deferred.append(lambda kt=kt: S.add('pool', I('dma_start', out=dst[kt * 128:(kt + 1) * 128, d0:d0 + n], in_=src[kt * 128:(kt + 1) * 128, c0:c0 + n]),
                                                   writes=[R(rname)], dma=key))

        precast_def(wbf, "wbf_kv", w_in, 4096, 512, 1024, 'T:pc1')
        precast_def(wbf, "wbf_q", w_in, 3072, 1024, 0, 'T:pc2')
        precast_def(wbf, "wbf_g", w_in, 4608, 1024, 1536, 'T:pc3')
        precast_def(wbf, "wbf_s", w_in, 6656, 1024, 2560, 'T:pc4')
        precast_def(wabf, "wabf", w_ab, 0, 1024, 0, 'T:pc5')
        if not DEFER_W:
            while deferred:
                deferred.pop(0)()

        def interleave(tail, filler, k=1, lead=0):
            t_done = tail is None
            f_done = filler is None

            def step(gen):
                try:
                    next(gen)
                    return False
                except StopIteration:
                    return True
            for _ in range(lead):
                if not f_done:
                    f_done = step(filler)
            while not (t_done and f_done):
                if not t_done:
                    t_done = step(tail)
                for _ in range(k):
                    if not f_done:
                        f_done = step(filler)

        def front_a(xsrc, cssrc, ring=3):
            xi = nxt('x' if ring == 3 else 'xa', ring)
            par = nxt('par', 2)
            xb, cs = XB[xi], CS[xi]
            S.add('sp', I('dma_start', out=xb[:], in_=xsrc), writes=[R("XB%d" % xi)] + ([R("sbstg0"), R("sbstg1")] if xi == 3 else []), dma='x%d' % xi)
            S.add('sp', I('dma_start', out=cs[:], in_=cssrc), writes=[R("CS%d" % xi)], dma='c%d' % xi)
            S.add('act', I('activation', out=ub[par][:], in_=xb[:], func=AF.Square, accum_out=ssq[par][:]),
                  reads=[R("XB%d" % xi)], writes=[R("ub%d" % par), R("ssq%d" % par)])
            S.add('dve', I('tensor_scalar', out=vv[par][:], in0=ssq[par][:], scalar1=1.0 / D_MODEL, scalar2=EPS, op0=ALU.mult, op1=ALU.add),
                  reads=[R("ssq%d" % par)], writes=[R("vv%d" % par)])
            S.add('pool', I('tensor_tensor', out=rstd[par][:], in0=vv[par][:], in1=mhalf[:, 0:1], op=ALU.pow),
                  reads=[R("vv%d" % par), R("mhalf")], writes=[R("rstd%d" % par)])
            S.add('pool', I('tensor_tensor', out=u0[:], in0=xb[:], in1=prew_t[:], op=ALU.mult),
                  reads=[R("XB%d" % xi), R("prew_t")], writes=[R("u0")])
            S.add('act', I('activation', out=ub[par][:], in_=u0[:], func=AF.Copy, scale=rstd[par][:]),
                  reads=[R("u0"), R("rstd%d" % par)], writes=[R("ub%d" % par)])
            return dict(xi=xi, par=par)

        pend = []

        def pe_pop():
            if pend:
                ops_, post = pend[0]
                ops_.pop(0)()
                if not ops_:
                    post()
                    pend.pop(0)

        def pe_flush():
            while pend:
                pe_pop()

        def front_b(ctx, mixed=False):
            par = ctx['par']
            tr_to(ub[par], "ub%d" % par, 8, uT[par][:].rearrange("p a b -> p (a b)"), "uT%d" % par, mixed=mixed)

        def tr_to(src_tile, src_res, ntile, dst_ap, dst_res, scale=None, src_off=0, mixed=False):
            pts = PT_all[mode[0]]
            pt, ptr = pts[nxt('pt', len(pts))]
            adders = [(lambda t=t: S.add('pe', I('transpose', out=pt[:, t * 128:(t + 1) * 128],
                                                 in_=src_tile[:, src_off + t * 128:src_off + (t + 1) * 128], identity=ident[:]),
                                         reads=[R(src_res), R("ident_t")], writes=[ptr])) for t in range(ntile)]

            def post():
                if scale is None:
                    S.add('act', I('activation', out=dst_ap, in_=pt[:, 0:ntile * 128], func=AF.Copy), reads=[ptr], writes=[R(dst_res)])
                else:
                    S.add('act', I('activation', out=dst_ap, in_=pt[:, 0:ntile * 128], func=AF.Copy, scale=scale), reads=[ptr], writes=[R(dst_res)])
            if mixed:
                pend.append([adders, post])
            else:
                pe_flush()
                for a_ in adders:
                    a_()
                post()

        def proj(par, wt, wres, c0, n=512):
            pas = PA_all[mode[0]]
            bank, bres = pas[nxt('pa', len(pas))]
            for kt in range(8):
                S.add('pe', I('matmul', out=bank[:, 0:n], lhsT=uT[par][:, kt, :], rhs=wt[:, kt, c0:c0 + n], start=(kt == 0), stop=(kt == 7)),
                      reads=[R("uT%d" % par)] + [R(r_) for r_ in (wres if isinstance(wres, (list, tuple)) else [wres])], writes=[bres])
                if kt % 2 == 1:
                    pe_pop()
            return bank[:, 0:n], bres

        def rotary(zb, zres, cs, csres, nh, dst_ap, dst_res, ri):
            n = nh * 128
            zv = zb.rearrange("p (h t f) -> p h t f", h=nh, t=2, f=64)
            A = rotA[ri][:, 0:n].rearrange("p (h t f) -> p h t f", h=nh, t=2, f=64)
            B = rotB[ri][:, 0:n].rearrange("p (h t f) -> p h t f", h=nh, t=2, f=64)
            dv = dst_ap.rearrange("p (h t f) -> p h t f", h=nh, t=2, f=64)
            cosb = bass.AP(cs, 0, [[128, 128], [0, nh], [0, 2], [1, 64]])
            sinb = bass.AP(cs, 64, [[128, 128], [0, nh], [1, 64]])
            rA, rB = R("rotA%d" % ri), R("rotB%d" % ri)
            S.add('dve', I('tensor_tensor', out=A, in0=zv, in1=cosb, op=ALU.mult), reads=[zres, R(csres)], writes=[rA])
            S.add('dve', I('tensor_tensor', out=B[:, :, 0, :], in0=zv[:, :, 1, :], in1=sinb, op=ALU.mult), reads=[zres, R(csres)], writes=[rB])
            S.add('dve', I('tensor_tensor', out=B[:, :, 1, :], in0=zv[:, :, 0, :], in1=sinb, op=ALU.mult), reads=[zres, R(csres)], writes=[rB])
            S.add('pool', I('tensor_tensor', out=dv[:, :, 0, :], in0=A[:, :, 0, :], in1=B[:, :, 0, :], op=ALU.subtract), reads=[rA, rB], writes=[R(dst_res)])
            S.add('pool', I('tensor_tensor', out=dv[:, :, 1, :], in0=A[:, :, 1, :], in1=B[:, :, 1, :], op=ALU.add), reads=[rA, rB], writes=[R(dst_res)])

        def bc_d(tile, off):
            return bass.AP(tile, off, [[tile.shape[1], 128], [1, 4], [0, 128]])

        def h3(ap):
            return ap.rearrange("p (h d) -> p h d", h=4, d=128)

        def kv_update(par, kd_tile, kd_res, bank_tile, bres, Stile, Sres, coef_tile, coef_off):
            for h in range(4):
                S.add('pe', I('matmul', out=bank_tile[:, h * 256:(h + 1) * 256], lhsT=kd_tile[:, h * 128:(h + 1) * 128],
                              rhs=v_tok[par][:, h * 256:(h + 1) * 256], start=True, stop=True),
                      reads=[R(kd_res), R("v_tok%d" % par)], writes=[bres[h // 2]])
            for h in range(4):
                S.add('dve', I('scalar_tensor_tensor',
                               out=Stile[:, h * 256:(h + 1) * 256], in0=Stile[:, h * 256:(h + 1) * 256], scalar=coef_tile[:, coef_off + h:coef_off + h + 1],
                               in1=bank_tile[:, h * 256:(h + 1) * 256], op0=ALU.mult, op1=ALU.add),
                      reads=[R(Sres), bres[h // 2], R(coef_tile.name)], writes=[R(Sres)])

        PSr = [R("PS0"), R("PS1")]
        POr = [R("PO0"), R("PO1")]
        final_dma = []

        nstg = [0]

        def st_head(t, ctx):
            xi, par = ctx['xi'], ctx['par']
            zk, zkr = proj(par, WW, "WW", 512)
            rotary(zk, zkr, CS[xi], "CS%d" % xi, 4, k_rot[par][:], "k_rot%d" % par, par)
            yield
            for hb in range(2):
                zv_, zvr = proj(par, WW, "WW", 1024 + hb * 512)
                S.add('act', I('activation', out=v_tok[par][:, hb * 512:(hb + 1) * 512], in_=zv_, func=AF.Copy),
                      reads=[zvr], writes=[R("v_tok%d" % par)])
                if hb == 0:
                    S.add('dve', I('tensor_tensor', out=h3(kdf[par][:]), in0=h3(k_rot[par][:]), in1=bc_d(KDSf, t * 4), op=ALU.mult),
                          reads=[R("k_rot%d" % par), R("KDSf")], writes=[R("kdf%d" % par)])
                    S.add('pool', I('tensor_tensor', out=h3(kdb[par][:]), in0=h3(k_rot[par][:]), in1=bc_d(KDSb, t * 4), op=ALU.mult),
                          reads=[R("k_rot%d" % par), R("KDSb")], writes=[R("kdb%d" % par)])
                yield
            if t >= 49:
                c = NSLOT - t
                S.add('sp', I('dma_start', out=kspill[c * 128:(c + 1) * 128, :], in_=k_rot[par][:]), reads=[R("k_rot%d" % par)], writes=[R("kspill_d")], dma='ksp%d' % par)
                S.add('sp', I('dma_start', out=vspill[c * 128:(c + 1) * 128, :], in_=v_tok[par][:]), reads=[R("v_tok%d" % par)], writes=[R("vspill_d")], dma='vsp%d' % par)

        def st_tail(t, ctx):
            par = ctx['par']
            if t >= 49:
                c = NSLOT - t
                sg = sbstg[nstg[0] % 2]
                sgr = "sbstg%d" % (nstg[0] % 2)
                nstg[0] += 1
                S.add('act', I('activation', out=sg[:], in_=S_b[:], func=AF.Copy), reads=[R("S_b")], writes=[R(sgr)])
                S.add('sp', I('dma_start', out=sbst[c * 128:(c + 1) * 128, :], in_=sg[:]), reads=[R(sgr)], writes=[R("sbst_d")], dma=sgr)
            kv_update(par, kdf[par], "kdf%d" % par, PS, PSr, S_f, "S_f", AFt, t * 4)
            yield
            kv_update(par, kdb[par], "kdb%d" % par, PO, POr, S_b, "S_b", ABt, t * 4)
            yield

        ctxs = {0: front_a(xs[0:128, :], css[0:128, :])}
        front_b(ctxs[0])
        ctxs[1] = front_a(xs[128:256, :], css[128:256, :])
        prev_tail = None
        for t in range(NSLOT):
            if t + 2 < NSLOT:
                ctxs[t + 2] = front_a(xs[(t + 2) * 128:(t + 3) * 128, :], css[(t + 2) * 128:(t + 3) * 128, :])
            if t + 1 < NSLOT:
                front_b(ctxs[t + 1], mixed=True)
            for _ in range(2):
                if deferred and DEFER_W:
                    deferred.pop(0)()
            interleave(prev_tail, st_head(t, ctxs[t]), k=2)
            pe_flush()
            prev_tail = st_tail(t, ctxs[t])
        interleave(prev_tail, None)
        while deferred:
            deferred.pop(0)()
        sg = sbstg[nstg[0] % 2]
        sgr = "sbstg%d" % (nstg[0] % 2)
        S.add('act', I('activation', out=sg[:], in_=S_b[:], func=AF.Copy), reads=[R("S_b")], writes=[R(sgr)])
        S.add('sp', I('dma_start', out=sbst[0:128, :], in_=sg[:]), reads=[R(sgr)], writes=[R("sbst_d")], dma=sgr)
        S.add('act', I('activation', out=S_fbf[0][:], in_=S_f[:], func=AF.Copy), reads=[R("S_f")], writes=[R("S_fbf0")])
        if debug:
            final_dma.append(S.add('sp', I('dma_start', out=dbg['sf'], in_=S_f[:]), reads=[R("S_f")], dma='dbgsf'))
            final_dma.append(S.add('sp', I('dma_start', out=dbg['sb'], in_=S_b[:]), reads=[R("S_b")], dma='dbgsb'))

        if stop_after == 'state':
            S.add('sp', None, extra=final_dma)
            S.build()
            return nc
        mode[0] = 'main'

        def r_front(c):
            return front_a(xm[(2 + c) * 128:(3 + c) * 128, :], csm[(2 + c) * 128:(3 + c) * 128, :])

        def load_sb(c):
            sbc = sbstg[c % 2]
            sbr = "sbstg%d" % (c % 2)
            S.add('sp', I('dma_start', out=sbc[:], in_=sbst[c * 128:(c + 1) * 128, :]), reads=[R("sbst_d")], writes=[R(sbr)], dma=sbr)

        def r_head(c, ctx):
            xi, par = ctx['xi'], ctx['par']
            zq, zqr = proj(par, WW, "WWq", 0)
            rotary(zq, zqr, CS[xi], "CS%d" % xi, 4, q_rot[par][:, 0:512], "q_rot", 0)
            yield
            if c == 0:
                zk, zkr = proj(par, WW, "WW", 512)
                rotary(zk, zkr, CS[xi], "CS%d" % xi, 4, k_rot[par][:], "k_rot%d" % par, 1)
            tr_to(q_rot[par], "q_rot", 4, qT[par][:], "qT%d" % par)
            yield
            for hb in range(2):
                if c == 0:
                    zv_, zvr = proj(par, WW, "WW", 1024 + hb * 512)
                    S.add('act', I('activation', out=v_tok[par][:, hb * 512:(hb + 1) * 512], in_=zv_, func=AF.Copy),
                          reads=[zvr], writes=[R("v_tok%d" % par)])
                if hb == 0:
                    tr_to(k_rot[par], "k_rot%d" % par, 4, kT[par][:], kTR[par])
                    S.add('dve', I('tensor_tensor', out=h3(kdf[par][:]), in0=h3(k_rot[par][:]), in1=bc_d(KD8, 0), op=ALU.mult),
                          reads=[R("k_rot%d" % par), R("KD8")], writes=[R("kdf%d" % par)])
                if c == 0:
                    yield
            for hb in range(2):
                zg, zgr = proj(par, WW, "WWq", 2048 + hb * 512)
                S.add('act', I('activation', out=g_r[par][:, hb * 512:(hb + 1) * 512], in_=zg, func=AF.Silu),
                      reads=[zgr], writes=[R("g_r%d" % par)])
                yield
            S.add('pool', I('tensor_tensor', out=g_r[par][:], in0=g_r[par][:], in1=nw_t[:], op=ALU.mult),
                  reads=[R("g_r%d" % par), R("nw_t")], writes=[R("g_r%d" % par)])
            for hb in range(2):
                zg, zgr = proj(par, WW, "WWq", 3072 + hb * 512)
                S.add('act', I('activation', out=sg_r[par][:, hb * 512:(hb + 1) * 512], in_=zg, func=AF.Tanh, scale=0.5),
                      reads=[zgr], writes=[R("sg_r%d" % par)])
                yield

        def r_tail_a(c, ctx):
            par = ctx['par']
            sbc = sbstg[c % 2]
            sbr = "sbstg%d" % (c % 2)
            for h in range(4):
                hs = slice(h * 128, (h + 1) * 128)
                S.add('pe', I('matmul', out=PS[:, hs], lhsT=kT[par][:, hs], rhs=qT[par][:, hs], start=True, stop=True),
                      reads=[R(kTR[par]), R("qT%d" % par)], writes=[PSr[0]])
            S.add('dve', I('tensor_tensor', out=sd[par][:], in0=PS[:, 0:512], in1=DT[:], op=ALU.mult), reads=[PSr[0], R("DT")], writes=[R("sd")])
            S.add('pool', I('tensor_tensor', out=qdf[par][:], in0=qT[par][:], in1=QDF[:], op=ALU.mult), reads=[R("qT%d" % par), R("QDF")], writes=[R("qdf")])
            S.add('pool', I('tensor_tensor', out=qdb[par][:], in0=qT[par][:], in1=QDB[:], op=ALU.mult), reads=[R("qT%d" % par), R("QDB")], writes=[R("qdb")])
            yield
            sfb = S_fbf[0]
            sfr = "S_fbf0"
            for h in range(4):
                hs = slice(h * 128, (h + 1) * 128)
                vs = slice(h * 256, (h + 1) * 256)
                S.add('pe', I('matmul', out=PO[:, vs], lhsT=qdf[par][:, hs], rhs=sfb[:, vs], start=True, stop=False),
                      reads=[R("qdf"), R(sfr)], writes=[POr[h // 2]])
                S.add('pe', I('matmul', out=PO[:, vs], lhsT=qdb[par][:, hs], rhs=sbc[:, vs], start=False, stop=False),
                      reads=[R("qdb"), R(sbr)], writes=[POr[h // 2]])
                S.add('pe', I('matmul', out=PO[:, vs], lhsT=sd[par][:, hs], rhs=v_tok[par][:, vs], start=False, stop=True),
                      reads=[R("sd"), R("v_tok%d" % par)], writes=[POr[h // 2]])
            if c < NOWN - 1:
                kv_update(par, kdf[par], "kdf%d" % par, PS, PSr, S_f, "S_f", CD8, 0)
                S.add('pool', I('tensor_copy', out=S_fbf[0][:], in_=S_f[:]), reads=[R("S_f")], writes=[R("S_fbf0")])
            for h in range(4):
                S.add('dve', I('bn_stats', out=gnst[:, h * 6:(h + 1) * 6], in_=PO[:, h * 256:(h + 1) * 256]), reads=[POr[h // 2]], writes=[R("gnst")])
            for h in range(4):
                S.add('dve', I('bn_aggr', out=gnmv[:, h * 2:(h + 1) * 2], in_=gnst[:, h * 6:(h + 1) * 6]), reads=[R("gnst")], writes=[R("gnmv")])
            mvv = gnmv[:].rearrange("p (h t) -> p h t", h=4, t=2)
            S.add('dve', I('tensor_scalar', out=gnve[:], in0=mvv[:, :, 1], scalar1=EPS, scalar2=None, op0=ALU.add), reads=[R("gnmv")], writes=[R("gnve")])
            S.add('pool', I('tensor_tensor', out=gnrs[:], in0=gnve[:], in1=mhalf[:, 0:4], op=ALU.pow), reads=[R("gnve"), R("mhalf")], writes=[R("gnrs")])
            S.add('dve', I('scalar_tensor_tensor', out=gnnm[:], in0=mvv[:, :, 0], scalar=-1.0, in1=gnrs[:], op0=ALU.mult, op1=ALU.mult),
                  reads=[R("gnmv"), R("gnrs")], writes=[R("gnnm")])
            for h in range(4):
                vs = slice(h * 256, (h + 1) * 256)
                S.add('dve', I('tensor_scalar', out=o_n[:, vs], in0=PO[:, vs], scalar1=gnrs[:, h:h + 1], scalar2=gnnm[:, h:h + 1], op0=ALU.mult, op1=ALU.add),
                      reads=[POr[h // 2], R("gnrs"), R("gnnm")], writes=[R("o_n")])
            S.add('pool', I('tensor_tensor', out=o_g[:], in0=o_n[:], in1=g_r[par][:], op=ALU.mult), reads=[R("o_n"), R("g_r%d" % par)], writes=[R("o_g")])
            yield

        def r_tail_b(c, ctx):
            par = ctx['par']
            tr_to(o_g, "o_g", 8, o_gT[:].rearrange("p a b -> p (a b)"), "o_gT")
            yield
            for nb_ in range(2):
                for kt in range(8):
                    S.add('pe', I('matmul', out=PO[:, nb_ * 512:(nb_ + 1) * 512], lhsT=o_gT[:, kt, :], rhs=W2[:, kt, nb_ * 512:(nb_ + 1) * 512],
                                  start=(kt == 0), stop=(kt == 7)),
                          reads=[R("o_gT"), R("W2")], writes=[POr[nb_]])
            mr = mixr[0]
            mrr = "mixr"
            for nb_ in range(2):
                S.add('dve', I('scalar_tensor_tensor', out=mr[:, nb_ * 512:(nb_ + 1) * 512], in0=sg_r[par][:, nb_ * 512:(nb_ + 1) * 512], scalar=1.0,
                               in1=PO[:, nb_ * 512:(nb_ + 1) * 512], op0=ALU.add, op1=ALU.mult),
                      reads=[R("sg_r%d" % par), POr[nb_]], writes=[R(mrr)])
            S.add('sp', I('dma_start', out=mixrd[c * 128:(c + 1) * 128, :], in_=mr[:]), reads=[R(mrr)], writes=[R("mixr_d")], dma=mrr)
            if debug:
                final_dma.append(S.add('sp', I('dma_start', out=dbg['mixr'][c * 128:(c + 1) * 128, :], in_=mr[:]), reads=[R(mrr)], dma='dbgm'))
            yield

        ctxs = {0: r_front(0)}
        front_b(ctxs[0])
        def chain(*gens):
            for g_ in gens:
                if g_ is not None:
                    yield from g_

        ctxs[1] = r_front(1)
        for c in range(NOWN + 2):
            if c < NOWN:
                load_sb(c)
                if c >= 1:
                    pc = ctxs[c]['par']
                    S.add('sp', I('dma_start', out=k_rot[pc][:], in_=kspill[c * 128:(c + 1) * 128, :]), reads=[R("kspill_d")], writes=[R("k_rot%d" % pc)], dma='ksp%d' % pc)
                    S.add('sp', I('dma_start', out=v_tok[pc][:], in_=vspill[c * 128:(c + 1) * 128, :]), reads=[R("vspill_d")], writes=[R("v_tok%d" % pc)], dma='vsp%d' % pc)
            if c + 2 < NOWN:
                ctxs[c + 2] = r_front(c + 2)
            if c + 1 < NOWN:
                front_b(ctxs[c + 1])
            if c == NOWN:
                ld(nw_t[:], postw, "nw_t", key='pw')
                def reload(rn, sres, d0, n, key):
                    S.add('sp', I('dma_start', out=WW[:, :, d0:d0 + n], in_=wbf[:, d0:d0 + n].rearrange("(kt p) c -> p kt c", p=128)),
                          reads=[R(sres)], writes=[R("WW"), R("WWq"), R(rn)], dma=key)
                reload("WAkv", "wbf_kv", 1024, 512, 'w4a')
                reload("WAq", "wbf_q", 0, 1024, 'w4b')
                reload("WAg", "wbf_g", 1536, 1024, 'w4c')
                reload("WAs", "wbf_s", 2560, 1024, 'w4d')
            tb = r_tail_b(c - 2, ctxs[c - 2]) if 0 <= c - 2 < NOWN else None
            ta = r_tail_a(c - 1, ctxs[c - 1]) if 0 <= c - 1 < NOWN else None
            hd = r_head(c, ctxs[c]) if c < NOWN else None
            interleave(chain(tb, ta), hd, k=2, lead=1)

        if stop_after == 'R':
            S.add('sp', None, extra=final_dma)
            S.build()
            return nc
        S.add('sp', I('dma_start', out=W2[:], in_=wabf.rearrange("(kt p) c -> p kt c", p=128)), reads=[R("wabf")], writes=[R("W2")], dma='w6')
        while deferred:
            deferred.pop(0)()

        for p in range(2):
            for g in range(2):
                S.add('pool', I('memset', PTm[p][g][:], 0.0), writes=[R(PTmR[p][g])])
        for p in range(2):
            for g in range(2):
                for hh in range(4):
                    h = g * 4 + hh
                    S.add('act', I('activation', out=PTm[p][g][32:33, hh * 128:(hh + 1) * 128], in_=zrow[32:33, :], func=AF.Exp,
                                   bias=sink_t[32:33, h:h + 1], scale=1.0),
                          reads=[R("zrow"), R("sink_t")], writes=[R(PTmR[p][g])])

        def a_head(idx, ctx, nctx=None):
            xi, par = ctx['xi'], ctx['par']
            zkv, zkvr = proj(par, WW, "WAkv", 1024)
            if idx == 0:
                ktile, kres, vtile, vres = KTm, "KTm", Vm, "Vm"
            else:
                ktile, kres, vtile, vres = KT[idx % 4], "KT%d" % (idx % 4), VA[idx % 4], "VA%d" % (idx % 4)
            if idx == 0:
                S.add('act', I('activation', out=Vm[0:16, :, 0:128], in_=zkv[0:16, 256:512].rearrange("p (g d) -> p g d", g=2, d=128), func=AF.Copy),
                      reads=[zkvr], writes=[R("Vm")])
            else:
                S.add('act', I('activation', out=vtile[:, :, 0:128], in_=zkv[:, 256:512].rearrange("p (g d) -> p g d", g=2, d=128), func=AF.Copy),
                      reads=[zkvr], writes=[R(vres)])
            rotary(zkv[:, 0:256], zkvr, CS[xi], "CS%d" % xi, 2, k_rot[par][:, 0:256], "k_rot%d" % par, 0)
            yield
            if 2 <= idx <= 17:
                c = idx - 2
                pa_c = c % 2
                own_info[c] = (xi, pa_c)
                for hb in range(2):
                    zq, zqr = proj(par, WW, "WAq", hb * 512)
                    rotary(zq, zqr, CS[xi], "CS%d" % xi, 4, q_rot[par][:, hb * 512:(hb + 1) * 512], "q_rot", 1)
                    if hb == 0:
                        tr_to(k_rot[par], "k_rot%d" % par, 2, ktile[:].rearrange("p a b -> p (a b)"), kres)
                    yield
                for hb in range(2):
                    zg, zgr = proj(par, WW, "WAg", 1536 + hb * 512)
                    S.add('act', I('activation', out=th_t[:], in_=zg, func=AF.Tanh, scale=0.5), reads=[zgr], writes=[R("QDF")])
                    S.add('dve', I('scalar_tensor_tensor', out=g_r[pa_c][:, hb * 512:(hb + 1) * 512], in0=th_t[:], scalar=1.0, in1=zg, op0=ALU.add, op1=ALU.mult),
                          reads=[R("QDF"), zgr], writes=[R("g_r%d" % pa_c)])
                    if hb == 0:
                        tr_to(q_rot[par], "q_rot", 8, v_tok[pa_c][:], "v_tok%d" % pa_c)
                    yield
                for hb in range(2):
                    zg, zgr = proj(par, WW, "WAs", 2560 + hb * 512)
                    S.add('act', I('activation', out=sg_r[pa_c][:, hb * 512:(hb + 1) * 512], in_=zg, func=AF.Tanh, scale=0.5),
                          reads=[zgr], writes=[R("sg_r%d" % pa_c)])
                    yield
            else:
                tr_to(k_rot[par], "k_rot%d" % par, 2, ktile[:].rearrange("p a b -> p (a b)"), kres)
                yield
            if nctx is not None:
                front_b(nctx)
                yield

        def a_post(c, xi_c):
            q = c % 2
            S.add('act', I('activation', out=o_n[:], in_=PO[:, :], func=AF.Square, accum_out=ssq[q][:]), reads=POr, writes=[R("o_n"), R("ssq%d" % q)])
            S.add('dve', I('tensor_scalar', out=vv[q][:], in0=ssq[q][:], scalar1=1.0 / D_MODEL, scalar2=EPS, op0=ALU.mult, op1=ALU.add),
                  reads=[R("ssq%d" % q)], writes=[R("vv%d" % q)])
            S.add('pool', I('tensor_tensor', out=rstd[q][:], in0=vv[q][:], in1=mhalf[:, 0:1], op=ALU.pow), reads=[R("vv%d" % q), R("mhalf")], writes=[R("rstd%d" % q)])
            for nb_ in range(2):
                cs_ = slice(nb_ * 512, (nb_ + 1) * 512)
                S.add('dve', I('scalar_tensor_tensor', out=tfin[:, cs_], in0=PO[:, cs_], scalar=rstd[q][:], in1=nw_t[:, cs_], op0=ALU.mult, op1=ALU.mult),
                      reads=[POr[nb_], R("rstd%d" % q), R("nw_t")], writes=[R("S_b")])
            S.add('pool', I('tensor_tensor', out=tfin[:], in0=tfin[:], in1=XB[xi_c][:], op=ALU.add), reads=[R("S_b"), R("XB%d" % xi_c)], writes=[R("S_b")])
            final_dma.append(S.add('sp', I('dma_start', out=out[c * 128:(c + 1) * 128, :], in_=tfin[:]), reads=[R("S_b")], dma="resb"))

        def a_tail(c, xi_c, pa_c, post=None):
            if post is not None:
                a_post(*post)
            blocks = []
            for bi, idx in enumerate((c + 1, c + 2, c + 3)):
                if bi == 0:
                    mk = 2 if c == 0 else 0
                elif bi == 2:
                    mk = 3 if c == NOWN - 1 else 1
                else:
                    mk = None
                blocks.append((idx % 4, mk))
            pm = c % 2
            def tr_group(g):
                tr_to(o_g, "o_g", 4, o_gT[:, g * 4:(g + 1) * 4, :].rearrange("p a b -> p (a b)"), "o_gT", src_off=g * 512)

            for g in range(2):
                qg = v_tok[pa_c][:, g * 512:(g + 1) * 512]
                for bi, (sl, mk) in enumerate(blocks):
                    bank = PS[:, (bi % 2) * 512:(bi % 2 + 1) * 512]
                    br = PSr[bi % 2]
                    S.add('pe', I('matmul', out=bank, lhsT=KT[sl][:, g, :], rhs=qg, start=True, stop=(mk is None)),
                          reads=[R("KT%d" % sl), R("v_tok%d" % pa_c)], writes=[br])
                    if mk is not None:
                        S.add('pe', I('matmul', out=bank, lhsT=ident[:], rhs=mask_t[:, mk * 512:(mk + 1) * 512], start=False, stop=True),
                              reads=[R("ident_t"), R("mask_t")], writes=[br])
                    S.add('act', I('activation', out=PTb[g][bi][:], in_=bank, func=AF.Exp, scale=SCALE),
                          reads=[br], writes=[R(PTbR[g][bi])])
                bank = PS[0:16, 512:1024]
                S.add('pe', I('matmul', out=bank, lhsT=KTm[:, g, 0:16], rhs=qg, start=True, stop=True),
                      reads=[R("KTm"), R("v_tok%d" % pa_c)], writes=[PSr[1]])
                S.add('act', I('activation', out=PTm[pm][g][0:16, :], in_=bank, func=AF.Exp, scale=SCALE),
                      reads=[PSr[1]], writes=[R(PTmR[pm][g])])
                if g == 1:
                    tr_group(0)
                yield
                for hh in range(4):
                    dst = PO[:, hh * 129:(hh + 1) * 129] if hh < 3 else PO[:, 512:641]
                    dr = POr[0] if hh < 3 else POr[1]
                    for bi, (sl, mk) in enumerate(blocks):
                        S.add('pe', I('matmul', out=dst, lhsT=PTb[g][bi][:, hh * 128:(hh + 1) * 128], rhs=VA[sl][:, g, 0:129],
                                      start=(bi == 0), stop=False),
                              reads=[R(PTbR[g][bi]), R("VA%d" % sl)], writes=[dr])
                    S.add('pe', I('matmul', out=dst, lhsT=PTm[pm][g][0:33, hh * 128:(hh + 1) * 128], rhs=Vm[0:33, g, 0:129], start=False, stop=True),
                          reads=[R(PTmR[pm][g]), R("Vm")], writes=[dr])
                l3 = PO[:, 0:387].rearrange("p (h c) -> p h c", h=3, c=129)[:, :, 128]
                S.add('dve', I('reciprocal', out=rl[:, 0:3], in_=l3), reads=[POr[0]], writes=[R("rl")])
                S.add('dve', I('reciprocal', out=rl[:, 3:4], in_=PO[:, 640:641]), reads=[POr[1], R("rl")], writes=[R("rl")])
                S.add('dve', I('tensor_scalar', out=rl[:], in0=rl[:], scalar1=0.5, scalar2=None, op0=ALU.mult), reads=[R("rl")], writes=[R("rl")])
                for hh in range(4):
                    h = g * 4 + hh
                    src_ = PO[:, hh * 129:hh * 129 + 128] if hh < 3 else PO[:, 512:640]
                    dr = POr[0] if hh < 3 else POr[1]
                    S.add('dve', I('scalar_tensor_tensor', out=o_g[:, h * 128:(h + 1) * 128], in0=src_, scalar=rl[:, hh:hh + 1],
                                   in1=g_r[pa_c][:, h * 128:(h + 1) * 128], op0=ALU.mult, op1=ALU.mult),
                          reads=[dr, R("rl"), R("g_r%d" % pa_c)], writes=[R("o_g")])
                if g == 0:
                    S.add('sp', I('dma_start', out=mixr[0][:], in_=mixrd[c * 128:(c + 1) * 128, :]), reads=[R("mixr_d")], writes=[R("mixr")], dma="mixr")
                yield
            tr_group(1)
            for nb_ in range(2):
                for kt in range(8):
                    S.add('pe', I('matmul', out=PO[:, nb_ * 512:(nb_ + 1) * 512], lhsT=o_gT[:, kt, :], rhs=W2[:, kt, nb_ * 512:(nb_ + 1) * 512],
                                  start=(kt == 0), stop=(kt == 7)),
                          reads=[R("o_gT"), R("W2")], writes=[POr[nb_]])
            for nb_ in range(2):
                cs_ = slice(nb_ * 512, (nb_ + 1) * 512)
                S.add('dve', I('scalar_tensor_tensor', out=mixa[:, cs_], in0=sg_r[pa_c][:, cs_], scalar=1.0, in1=PO[:, cs_], op0=ALU.add, op1=ALU.mult),
                      reads=[R("sg_r%d" % pa_c), POr[nb_]], writes=[R("S_f")])
            S.add('pool', I('tensor_tensor', out=mixb[:], in0=mixa[:], in1=mixr[0][:], op=ALU.add), reads=[R("S_f"), R("mixr")], writes=[R("o_n")])
            yield
            tr_to(mixb, "o_n", 8, S_fbf[0][:], "S_fbf0", scale=0.5)
            yield
            for nb_ in range(2):
                for kt in range(8):
                    S.add('pe', I('matmul', out=PO[:, nb_ * 512:(nb_ + 1) * 512], lhsT=mixT[:, kt, :], rhs=W3[:, kt, nb_ * 512:(nb_ + 1) * 512],
                                  start=(kt == 0), stop=(kt == 7)),
                          reads=[R("S_fbf0"), R("W3")], writes=[POr[nb_]])
            yield

        own_info = {}
        ctxs = {0: front_a(xm[0:128, :], csm[0:128, :], ring=4)}
        front_b(ctxs[0])
        for idx in range(a_steps):
            if idx + 1 < 19:
                ctxs[idx + 1] = front_a(xm[(idx + 1) * 128:(idx + 2) * 128, :], csm[(idx + 1) * 128:(idx + 2) * 128, :], ring=4)
            tail = None
            if idx >= 3:
                c_ = idx - 3
                post = (c_ - 1, own_info[c_ - 1][0]) if c_ >= 1 else None
                tail = a_tail(c_, *own_info[c_], post=post)
            hd = a_head(idx, ctxs[idx], ctxs.get(idx + 1))
            if PIPE_A:
                interleave(tail, hd, k=1, lead=2)
            else:
                interleave(None, hd)
                interleave(tail, None)

        if a_steps == 19:
            a_post(NOWN - 1, own_info[NOWN - 1][0])
        S.add('sp', None, extra=final_dma)
        S.build()
        print("sched stats", S.stats, flush=True)
    return nc


def _host_prep(inputs):
    x = np.asarray(inputs["x"], np.float32)
    meta = np.asarray(inputs["meta_tokens"], np.float32)
    import jax
    import jax.numpy as jnp
    _cpu = jax.devices("cpu")[0]
    with jax.default_device(_cpu):
        inv_j = 10000.0 ** (-jnp.arange(64, dtype=jnp.float32) * 2.0 / 128)

    def chunk_data(b, n):
        if n == 0:
            z = np.zeros((128, 1024), np.float32)
            z[112:] = meta
            return z
        if n > 64:
            return np.zeros((128, 1024), np.float32)
        return x[b, (n - 1) * 128:n * 128]

    def cs_of(pos):
        with jax.default_device(_cpu):
            ang = jnp.asarray(np.asarray(pos, np.float32))[:, None] * inv_j[None, :]
            return np.concatenate([np.asarray(jnp.cos(ang), np.float32), np.asarray(jnp.sin(ang), np.float32)], axis=1)

    def chunk_pos(n):
        return np.maximum(n * 128 + np.arange(128) - 112, 0)

    jj = np.arange(128)[:, None].astype(np.float64)
    ii = np.arange(128)[None, :].astype(np.float64)
    sc = 128.0 ** -0.5
    rtab = np.concatenate([np.maximum(ii - jj, 0), (ii >= jj) * sc, np.maximum(jj - ii, 0), (jj > ii) * sc], axis=1).astype(np.float32)
    ctab = np.concatenate([np.broadcast_to(ii + 1, (128, 128)), np.broadcast_to(128 - ii, (128, 128))], axis=1).astype(np.float32)
    jtab = np.concatenate([127 - jj, jj], axis=1).astype(np.float32)
    bf = ml_dtypes.bfloat16
    m_prev = np.where(jj >= ii, 0.0, NEG).astype(np.float32)
    m_next = np.where(jj <= ii, 0.0, NEG).astype(np.float32)
    m_all = np.full((128, 128), NEG, np.float32)
    ident = np.eye(128, dtype=np.float32).astype(bf)

    def bc(v, n):
        return np.ascontiguousarray(np.broadcast_to(np.asarray(v, np.float32).reshape(1, n), (128, n)))

    common = dict(
        w_in=np.ascontiguousarray(inputs["w_in"][0], dtype=np.float32),
        w_rb=np.ascontiguousarray(inputs["w_ret_branch"][0], dtype=np.float32),
        w_ab=np.ascontiguousarray(inputs["w_attn_branch"][0], dtype=np.float32),
        w_o=np.ascontiguousarray(inputs["w_out"][0], dtype=np.float32),
        prew=bc(inputs["pre_norm_w"][0], 1024), postw=bc(inputs["post_norm_w"][0], 1024), nwb=bc(inputs["ret_norm_w"][0], 1024),
        dec8=bc(np.concatenate([np.asarray(inputs["ret_decay_fwd"][0]), np.asarray(inputs["ret_decay_bwd"][0])]), 8),
        sink8=bc(inputs["attn_sink"][0], 8),
        rtab=rtab, ctab=ctab, jtab=jtab, ident=ident,
    )
    in_maps = []
    for core in range(8):
        b, s = divmod(core, 4)
        fwd = list(range(0, 16 * s + 1))
        bwd = list(range(64, 16 * s + 16, -1))
        own = [16 * s + 1 + c for c in range(15, 0, -1)]
        slots = fwd + bwd + own
        assert len(slots) == NSLOT
        xs = np.concatenate([chunk_data(b, n) for n in slots], axis=0)
        css = np.concatenate([cs_of(chunk_pos(n)) for n in slots], axis=0)
        actf = np.array([1.0] * len(fwd) + [0.0] * (NSLOT - len(fwd)), np.float32)
        acttab = bc(np.concatenate([actf, 1.0 - actf]), 128)
        metac = np.zeros((128, 1024), np.float32)
        metac[:16] = meta
        mpos = np.zeros(128)
        mpos[:16] = np.arange(16)
        mchunks = [16 * s] + [16 * s + 1 + c for c in range(16)] + [16 * s + 17]
        xm = np.concatenate([metac] + [chunk_data(b, n) for n in mchunks], axis=0)
        csm = np.concatenate([cs_of(mpos)] + [cs_of(chunk_pos(n)) for n in mchunks], axis=0)
        mk = [m_prev, m_next, m_all if s == 0 else m_prev, m_all if s == 3 else m_next]
        masks = np.concatenate([np.tile(m, (1, 4)) for m in mk], axis=1).astype(bf)
        d = dict(common)
        d.update(xs=xs, css=css, xm=xm, csm=csm, acttab=acttab, masks=masks)
        in_maps.append(d)
    return in_maps


_NC_CACHE = {}


def kernel(**inputs):
    in_maps = _host_prep(inputs)
    if 'nc' not in _NC_CACHE:
        _NC_CACHE['nc'] = build_nc()
    nc = _NC_CACHE['nc']
    res = run_bass_kernel_spmd(nc, in_maps, core_ids=list(range(8)))
    out = np.empty((2, SEQ, D_MODEL), np.float32)
    for core in range(8):
        b, s = divmod(core, 4)
        out[b, s * 2048:(s + 1) * 2048] = np.asarray(res.results[core]["out"], np.float32)
    return out
```

```python
import contextlib
import numpy as np
import ml_dtypes
import concourse.bass as bass
import concourse.mybir as mybir
from concourse.bass_utils import run_bass_kernel_spmd

F32 = mybir.dt.float32
BF16 = mybir.dt.bfloat16
AF = mybir.ActivationFunctionType
ALU = mybir.AluOpType

SAME_ENGINE_SYNC = True
DEFER_W = True
PIPE_A = True
ENGS = ['pe', 'dve', 'act', 'pool', 'sp']

D_MODEL = 1024
SEQ = 8192
N_META = 16
NSLOT = 64
NOWN = 16
EPS = 1e-6
SCALE = 128 ** -0.5
LN_SCALE = float(-0.5 * np.log(128.0))
NEG = -30000.0


class Res:
    __slots__ = ('name', 'last_w', 'readers', 'excl')

    def __init__(self, name):
        self.name = name
        self.last_w = None
        self.readers = []
        self.excl = False


class Sched:
    def __init__(self, nc):
        self.nc = nc
        self.ops = []

    def add(self, eng, fn, reads=(), writes=(), dma=None, extra=()):
        self.ops.append((eng, fn, tuple(reads), tuple(writes), dma, tuple(extra)))
        return len(self.ops) - 1

    def build(self):
        nc = self.nc
        last = {}
        for i, op in enumerate(self.ops):
            if op[4] is not None:
                last[op[4]] = i
        self.add('sp', None, extra=sorted(last.values()))
        ops = self.ops
        n = len(ops)
        deps = [None] * n
        signaled = [False] * n
        for i, (eng, fn, reads, writes, dk, extra) in enumerate(ops):
            d = set(extra)
            for r in reads:
                if r.last_w is not None:
                    d.add(r.last_w)
                if r.excl:
                    d.update(j for j in r.readers if ops[j][0] != eng)
            for w in writes:
                if w.last_w is not None:
                    d.add(w.last_w)
                d.update(w.readers)
            d.discard(i)
            dd = set()
            latest = {}
            for j in d:
                ej = ops[j][0]
                dmaj = ops[j][4] is not None
                if dmaj and dk is not None and ops[j][4] == dk and dk.startswith('T:'):
                    continue
                if (not dmaj) and ej == eng and dk is None and (eng == 'pe' or not SAME_ENGINE_SYNC):
                    continue
                if dmaj:
                    dd.add(j)
                else:
                    latest[ej] = max(latest.get(ej, -1), j)
            dd.update(latest.values())
            deps[i] = dd
            for j in dd:
                signaled[j] = True
            for r in reads:
                r.readers.append(i)
            for w in writes:
                w.last_w = i
                w.readers = []
        cnt = {e: 0 for e in ENGS}
        val = [0] * n
        dmacnt = {}
        for i, op in enumerate(ops):
            if op[4] is not None:
                dmacnt[op[4]] = dmacnt.get(op[4], 0) + 1
                val[i] = 16 * dmacnt[op[4]]
            elif signaled[i]:
                cnt[op[0]] += 1
                val[i] = cnt[op[0]]
        for i, op in enumerate(ops):
            if op[4] is not None and op[4].startswith('T:'):
                val[i] = 16 * dmacnt[op[4]]
        self.stats = dict(n_ops=n, milestones=dict(cnt), dma_keys=len(dmacnt))
        with contextlib.ExitStack() as st:
            sem_eng = {e: st.enter_context(nc.semaphore('s_' + e)) for e in ENGS}
            dma_sem = {k: st.enter_context(nc.semaphore('d_' + k.replace(':', '_'))) for k in dmacnt}

            def emit(engname, e):
                waited = {}
                for i, op in enumerate(ops):
                    if op[0] != engname:
                        continue
                    need = {}
                    for j in deps[i]:
                        if ops[j][4] is not None:
                            key = ('d', ops[j][4])
                            sem = dma_sem[ops[j][4]]
                        else:
                            key = ('e', ops[j][0])
                            sem = sem_eng[ops[j][0]]
                        if val[j] > need.get(key, (None, 0))[1]:
                            need[key] = (sem, val[j])
                    for key, (sem, v) in need.items():
                        if waited.get(key, 0) >= v:
                            continue
                        e.wait_ge(sem, v)
                        waited[key] = v
                    if op[1] is None:
                        continue
                    inst = op[1](e)
                    if op[4] is not None:
                        inst.then_inc(dma_sem[op[4]], 16)
                    elif signaled[i]:
                        inst.then_inc(sem_eng[engname], 1)

            with nc.Block() as block:
                @block.tensor
                def _(e):
                    emit('pe', e)

                @block.vector
                def _(e):
                    emit('dve', e)

                @block.scalar
                def _(e):
                    emit('act', e)

                @block.gpsimd
                def _(e):
                    emit('pool', e)

                @block.sync
                def _(e):
                    emit('sp', e)


def I(name, *a, **kw):
    return lambda e: getattr(e, name)(*a, **kw)


def build_nc(debug=False, stop_after=None, a_steps=19):
    nc = bass.Bass("TRN2", target_bir_lowering=False)

    def din(name, shape, dt=F32):
        return nc.dram_tensor(name, shape, dt, kind="ExternalInput").ap()

    xs = din("xs", [NSLOT * 128, 1024])
    css = din("css", [NSLOT * 128, 128])
    xm = din("xm", [19 * 128, 1024])
    csm = din("csm", [19 * 128, 128])
    w_in = din("w_in", [1024, 7680])
    w_rb = din("w_rb", [1024, 1024])
    w_ab = din("w_ab", [1024, 1024])
    w_o = din("w_o", [1024, 1024])
    prew = din("prew", [128, 1024])
    postw = din("postw", [128, 1024])
    nwb = din("nwb", [128, 1024])
    dec8 = din("dec8", [128, 8])
    sink8 = din("sink8", [128, 8])
    rtab = din("rtab", [128, 512])
    ctab = din("ctab", [128, 256])
    jtab = din("jtab", [128, 2])
    acttab = din("acttab", [128, 128])
    masks = din("masks", [128, 4 * 512], BF16)
    identd = din("ident", [128, 128], BF16)
    out = nc.dram_tensor("out", [NOWN * 128, 1024], F32, kind="ExternalOutput").ap()
    sbst = nc.dram_tensor("sbst", [NOWN * 128, 1024], BF16, kind="Internal").ap()
    mixrd = nc.dram_tensor("mixrd", [NOWN * 128, 1024], BF16, kind="Internal").ap()
    wbf = nc.dram_tensor("wbf", [1024, 3584], BF16, kind="Internal").ap()
    wabf = nc.dram_tensor("wabf", [1024, 1024], BF16, kind="Internal").ap()
    dbg = {}
    if debug:
        dbg['sf'] = nc.dram_tensor("dbg_sf", [128, 1024], F32, kind="ExternalOutput").ap()
        dbg['sb'] = nc.dram_tensor("dbg_sb", [128, 1024], F32, kind="ExternalOutput").ap()
        dbg['mixr'] = nc.dram_tensor("dbg_mixr", [NOWN * 128, 1024], BF16, kind="ExternalOutput").ap()

    with contextlib.ExitStack() as st:
        RES = {}

        def R(name):
            if name not in RES:
                RES[name] = Res(name)
            return RES[name]

        def sb(name, shape, dt):
            t = st.enter_context(nc.sbuf_tensor(name, shape, dt))
            R(name)
            return t

        def ps(name, shape, dt):
            return st.enter_context(nc.psum_tensor(name, shape, dt))

        S = Sched(nc)

        WW = sb("WW", [128, 8, 4096], BF16)
        W2 = sb("W2", [128, 8, 1024], BF16)
        W3 = sb("W3", [128, 8, 1024], BF16)
        prew_t = sb("prew_t", [128, 1024], F32)
        nw_t = sb("nw_t", [128, 1024], F32)
        ident = sb("ident_t", [128, 128], BF16)
        mask_t = sb("mask_t", [128, 4 * 512], BF16)
        dec_t = sb("dec_t", [128, 8], F32)
        sink_t = sb("sink_t", [128, 8], F32)
        jtab_t = sb("jtab_t", [128, 2], F32)
        act_t = sb("act_t", [128, 128], F32)
        e8 = sb("e8", [128, 8], F32)
        lg8 = sb("lg8", [128, 8], F32)
        lg128 = sb("lg128", [128, 8], F32)
        DT = sb("DT", [128, 512], BF16)
        QDF = sb("QDF", [128, 512], F32)
        QDB = sb("QDB", [128, 512], F32)
        KD8 = sb("KD8", [128, 8], F32)
        CD8 = sb("CD8", [128, 8], F32)
        KDSf = sb("KDSf", [128, 256], F32)
        KDSb = sb("KDSb", [128, 256], F32)
        AFt = sb("AFt", [128, 256], F32)
        ABt = sb("ABt", [128, 256], F32)
        zrow = sb("zrow", [128, 128], F32)
        S_f = sb("S_f", [128, 1024], F32)
        S_b = sb("S_b", [128, 1024], F32)
        S_fbf1 = sb("S_fbf0", [128, 1024], BF16)
        S_fbf = [S_fbf1, S_fbf1]
        sbst2 = sb("sbst2", [128, 2048], BF16)
        sbstg = [sbst2[:, 0:1024], sbst2[:, 1024:2048]]
        R("sbstg0")
        R("sbstg1")
        XB = [sb("XB%d" % i, [128, 1024], F32) for i in range(3)]
        XB.append(sbst2.bitcast(F32))
        R("XB3")
        CS = [sb("CS%d" % i, [128, 128], F32) for i in range(4)]
        ssq = [sb("ssq%d" % i, [128, 1], F32) for i in range(2)]
        vv = [sb("vv%d" % i, [128, 1], F32) for i in range(2)]
        rstd = [sb("rstd%d" % i, [128, 1], F32) for i in range(2)]
        mhalf = sb("mhalf", [128, 8], F32)
        u0 = sb("u0", [128, 1024], F32)
        ub = [sb("ub%d" % i, [128, 1024], BF16) for i in range(2)]
        uT = [sb("uT%d" % i, [128, 8, 128], BF16) for i in range(2)]
        rotA = [sb("rotA%d" % i, [128, 512], F32) for i in range(2)]
        rotB = [sb("rotB%d" % i, [128, 512], F32) for i in range(2)]
        k_rot = [sb("k_rot%d" % i, [128, 512], BF16) for i in range(2)]
        q_rot1 = sb("q_rot", [128, 1024], BF16)
        q_rot = [q_rot1, q_rot1]
        kdf = [sb("kdf%d" % i, [128, 512], BF16) for i in range(2)]
        kdb = [sb("kdb%d" % i, [128, 512], BF16) for i in range(2)]
        v_tok = [sb("v_tok%d" % i, [128, 1024], BF16) for i in range(2)]
        qT = [sb("qT%d" % i, [128, 512], BF16) for i in range(2)]
        kT1 = sb("kT", [128, 512], BF16)
        qdf1 = sb("qdf", [128, 512], BF16)
        qdb1 = sb("qdb", [128, 512], BF16)
        sd1 = sb("sd", [128, 512], BF16)
        kT2 = sb("kT2", [128, 512], BF16)
        kT, qdf, qdb, sd = [kT1, kT2], [qdf1] * 2, [qdb1] * 2, [sd1] * 2
        kTR = ["kT", "kT2"]
        g_r = [sb("g_r%d" % i, [128, 1024], BF16) for i in range(2)]
        sg_r = [sb("sg_r%d" % i, [128, 1024], BF16) for i in range(2)]
        gnst = sb("gnst", [128, 24], F32)
        gnmv = sb("gnmv", [128, 8], F32)
        gnve = sb("gnve", [128, 4], F32)
        gnrs = sb("gnrs", [128, 4], F32)
        gnnm = sb("gnnm", [128, 4], F32)
        o_n = sb("o_n", [128, 1024], BF16)
        o_g = sb("o_g", [128, 1024], BF16)
        o_gT = sb("o_gT", [128, 8, 128], BF16)
        mixr1 = sb("mixr", [128, 1024], BF16)
        mixr = [mixr1, mixr1]
        KT = [sb("KT%d" % i, [128, 2, 128], BF16) for i in range(4)]
        VA = [sb("VA%d" % i, [128, 2, 130], BF16) for i in range(4)]
        KTm = sb("KTm", [128, 2, 128], BF16)
        Vm = sb("Vm", [128, 2, 130], BF16)
        PTb = [[kdf[0], kdf[1], kdb[0]], [kdb[1], qT[0], qT[1]]]
        PTbR = [["kdf0", "kdf1", "kdb0"], ["kdb1", "qT0", "qT1"]]
        PTm = [[kT1, qdf1], [qdb1, sd1]]
        PTmR = [["kT", "qdf"], ["qdb", "sd"]]
        th_t = QDF
        rl = sb("rl", [128, 4], F32)
        mixa = S_f
        mixb = o_n
        mixT = S_fbf[0][:].rearrange("p (a b) -> p a b", a=8, b=128)
        tfin = S_b

        rtab_t = rotA[0]
        RES["rtab_t"] = RES["rotA0"]
        ctab_t = u0[:, 256:512]
        tmpA = u0[:, 0:256]
        tmp1 = u0[:, 512:640]
        tmp2 = u0[:, 640:768]
        for nm_ in ("ctab_t", "tmpA", "tmp1", "tmp2"):
            RES[nm_] = RES["u0"]
        print("sbuf remaining", nc.sbuf_bytes_remaining() if callable(getattr(nc, 'sbuf_bytes_remaining', None)) else nc.sbuf_bytes_remaining, flush=True)
        PA = ps("PA", [128, 1024], F32)
        PTr = ps("PTr", [128, 2048], BF16)
        PS = ps("PS", [128, 1024], F32)
        PO = ps("PO", [128, 1024], F32)
        for nm in ["PA0", "PA1", "PT0", "PT1", "PS0", "PS1", "PO0", "PO1", "sbst_d", "mixr_d"]:
            R(nm)
        for nm in ["PA0", "PA1", "PT0", "PT1", "PS0", "PS1", "PO0", "PO1"]:
            R(nm).excl = True
        PTf = PTr.bitcast(F32)
        PA_all = {
            'state': [(PA[:, 0:512], R("PA0")), (PA[:, 512:1024], R("PA1"))],
            'main': [(PA[:, 0:512], R("PA0")), (PA[:, 512:1024], R("PA1"))],
        }
        PT_all = {
            'state': [(PTr[:, 0:1024], R("PT0")), (PTr[:, 1024:2048], R("PT1"))],
            'main': [(PTr[:, 0:1024], R("PT0")), (PTr[:, 1024:2048], R("PT1"))],
        }
        mode = ['state']
        cnt = dict(pa=0, pt=0, x=0, xa=0, par=0)

        def nxt(key, mod):
            v = cnt[key]
            cnt[key] = v + 1
            return v % mod

        def ld(dst, src, rname, q='sp', key='T:setup'):
            S.add(q, I('dma_start', out=dst, in_=src), writes=[R(rname)], dma=key)

        ld(dec_t[:], dec8, "dec_t")
        ld(sink_t[:], sink8, "sink_t")
        ld(rtab_t[:], rtab, "rtab_t")
        ld(ctab_t, ctab, "ctab_t")
        ld(jtab_t[:], jtab, "jtab_t")
        ld(act_t[:], acttab, "act_t")
        ld(ident[:], identd, "ident_t")
        ld(mask_t[:], masks, "mask_t")
        ld(prew_t[:], prew, "prew_t")
        ld(nw_t[:], nwb, "nw_t")

        def wload(dst_tile, rname, src, c0, n, d0, key):
            rn = rname if isinstance(rname, (list, tuple)) else [rname]
            for kt in range(8):
                S.add('pool', I('dma_start', out=dst_tile[:, kt, d0:d0 + n], in_=src[kt * 128:(kt + 1) * 128, c0:c0 + n]),
                      writes=[R(r_) for r_ in rn], dma=key)

        wload(WW, "WW", w_in, 512, 512, 512, 'T:w1')
        wload(WW, "WW", w_in, 1024, 1024, 1024, 'T:w1')

        S.add('pool', I('memset', mhalf[:], -0.5), writes=[R("mhalf")])
        S.add('pool', I('memset', zrow[:], 0.0), writes=[R("zrow")])
        S.add('pool', I('memset', S_f[:], 0.0), writes=[R("S_f")])
        S.add('pool', I('memset', S_b[:], 0.0), writes=[R("S_b")])
        for i in range(4):
            S.add('pool', I('memset', VA[i][:], 1.0), writes=[R("VA%d" % i)])
        S.add('pool', I('memset', Vm[:], 0.0), writes=[R("Vm")])
        S.add('pool', I('memset', Vm[0:16, :, 128:129], 1.0), writes=[R("Vm")])
        S.add('pool', I('memset', Vm[32:33, :, 128:129], 1.0), writes=[R("Vm")])
        S.add('act', I('activation', out=e8[:], in_=dec_t[:], func=AF.Exp, scale=-1.0), reads=[R("dec_t")], writes=[R("e8")])
        S.add('act', I('activation', out=e8[:], in_=e8[:], func=AF.Ln, bias=1.0), reads=[R("e8")], writes=[R("e8")])
        S.add('dve', I('tensor_scalar', out=lg8[:], in0=e8[:], scalar1=-1.0, scalar2=None, op0=ALU.mult), reads=[R("e8")], writes=[R("lg8")])
        S.add('dve', I('tensor_scalar', out=lg128[:], in0=e8[:], scalar1=-128.0, scalar2=None, op0=ALU.mult), reads=[R("e8")], writes=[R("lg128")])
        for h in range(4):
            hs = slice(h * 128, (h + 1) * 128)
            S.add('act', I('activation', out=tmp1, in_=rtab_t[:, 0:128], func=AF.Exp, scale=lg8[:, h:h + 1]),
                  reads=[R("rtab_t"), R("lg8")], writes=[R("tmp1")])
            S.add('act', I('activation', out=tmp2, in_=rtab_t[:, 256:384], func=AF.Exp, scale=lg8[:, 4 + h:5 + h]),
                  reads=[R("rtab_t"), R("lg8")], writes=[R("tmp2")])
            S.add('dve', I('tensor_tensor', out=tmp1, in0=tmp1, in1=rtab_t[:, 128:256], op=ALU.mult), reads=[R("tmp1"), R("rtab_t")], writes=[R("tmp1")])
            S.add('dve', I('tensor_tensor', out=tmp2, in0=tmp2, in1=rtab_t[:, 384:512], op=ALU.mult), reads=[R("tmp2"), R("rtab_t")], writes=[R("tmp2")])
            S.add('dve', I('tensor_tensor', out=DT[:, hs], in0=tmp1, in1=tmp2, op=ALU.add), reads=[R("tmp1"), R("tmp2")], writes=[R("DT")])
            S.add('act', I('activation', out=QDF[:, hs], in_=ctab_t[:, 0:128], func=AF.Exp, scale=lg8[:, h:h + 1]),
                  reads=[R("ctab_t"), R("lg8")], writes=[R("QDF")])
            S.add('act', I('activation', out=QDB[:, hs], in_=ctab_t[:, 128:256], func=AF.Exp, scale=lg8[:, 4 + h:5 + h]),
                  reads=[R("ctab_t"), R("lg8")], writes=[R("QDB")])
            S.add('act', I('activation', out=KD8[:, h:h + 1], in_=jtab_t[:, 0:1], func=AF.Exp, scale=lg8[:, h:h + 1], bias=LN_SCALE),
                  reads=[R("jtab_t"), R("lg8")], writes=[R("KD8")])
            S.add('act', I('activation', out=KD8[:, 4 + h:5 + h], in_=jtab_t[:, 1:2], func=AF.Exp, scale=lg8[:, 4 + h:5 + h], bias=LN_SCALE),
                  reads=[R("jtab_t"), R("lg8")], writes=[R("KD8")])
        S.add('act', I('activation', out=CD8[:], in_=lg128[:], func=AF.Exp), reads=[R("lg128")], writes=[R("CD8")])

        def bc_slot(tile, off):
            return bass.AP(tile, off, [[tile.shape[1], 128], [0, NSLOT], [1, 4]])

        def bc_head(tile, off):
            return bass.AP(tile, off, [[tile.shape[1], 128], [1, NSLOT], [0, 4]])

        def v3(tile):
            ap_ = tile if isinstance(tile, bass.AP) else tile[:]
            return ap_.rearrange("p (s h) -> p s h", s=NSLOT, h=4)

        S.add('dve', I('tensor_tensor', out=v3(KDSf), in0=bc_slot(KD8, 0), in1=bc_head(act_t, 0), op=ALU.mult),
              reads=[R("KD8"), R("act_t")], writes=[R("KDSf")])
        S.add('dve', I('tensor_tensor', out=v3(KDSb), in0=bc_slot(KD8, 4), in1=bc_head(act_t, 64), op=ALU.mult),
              reads=[R("KD8"), R("act_t")], writes=[R("KDSb")])
        S.add('dve', I('tensor_tensor', out=v3(tmpA), in0=bc_slot(lg128, 0), in1=bc_head(act_t, 0), op=ALU.mult),
              reads=[R("lg128"), R("act_t")], writes=[R("tmpA")])
        S.add('act', I('activation', out=AFt[:], in_=tmpA, func=AF.Exp), reads=[R("tmpA")], writes=[R("AFt")])
        S.add('dve', I('tensor_tensor', out=v3(tmpA), in0=bc_slot(lg128, 4), in1=bc_head(act_t, 64), op=ALU.mult),
              reads=[R("lg128"), R("act_t"), R("AFt")], writes=[R("tmpA")])
        S.add('act', I('activation', out=ABt[:], in_=tmpA, func=AF.Exp), reads=[R("tmpA")], writes=[R("ABt")])
        deferred = []

        def wload_def(dst_tile, rname, src, c0, n, d0, key):
            for kt in range(8):
                deferred.append(lambda kt=kt: S.add('pool', I('dma_start', out=dst_tile[:, kt, d0:d0 + n], in_=src[kt * 128:(kt + 1) * 128, c0:c0 + n]),
                                                   writes=[R(rname)], dma=key))

        wload_def(WW, "WWq", w_in, 0, 512, 0, 'T:w2')
        wload_def(WW, "WWq", w_in, 2048, 1024, 2048, 'T:w2')
        wload_def(WW, "WWq", w_in, 5632, 1024, 3072, 'T:w2')
        wload_def(W2, "W2", w_rb, 0, 1024, 0, 'T:w2')
        wload_def(W3, "W3", w_o, 0, 1024, 0, 'T:w3')

        def precast_def(dst, rname, src, c0, n, d0, key):
            for kt in range(8):
                deferred.append(lambda kt=kt: S.add('pool', I('dma_start', out=dst[kt * 128:(kt + 1) * 128, d0:d0 + n], in_=src[kt * 128:(kt + 1) * 128, c0:c0 + n]),
                                                   writes=[R(rname)], dma=key))

        precast_def(wbf, "wbf_kv", w_in, 4096, 512, 1024, 'T:pc1')
        precast_def(wbf, "wbf_q", w_in, 3072, 1024, 0, 'T:pc2')
        precast_def(wbf, "wbf_g", w_in, 4608, 1024, 1536, 'T:pc3')
        precast_def(wbf, "wbf_s", w_in, 6656, 1024, 2560, 'T:pc4')
        precast_def(wabf, "wabf", w_ab, 0, 1024, 0, 'T:pc5')
        if not DEFER_W:
            while deferred:
                deferred.pop(0)()

        def interleave(tail, filler, k=1, lead=0):
            t_done = tail is None
            f_done = filler is None

            def step(gen):
                try:
                    next(gen)
                    return False
                except StopIteration:
                    return True
            for _ in range(lead):
                if not f_done:
                    f_done = step(filler)
            while not (t_done and f_done):
                if not t_done:
                    t_done = step(tail)
                for _ in range(k):
                    if not f_done:
                        f_done = step(filler)

        def front_a(xsrc, cssrc, ring=3):
            xi = nxt('x' if ring == 3 else 'xa', ring)
            par = nxt('par', 2)
            xb, cs = XB[xi], CS[xi]
            S.add('sp', I('dma_start', out=xb[:], in_=xsrc), writes=[R("XB%d" % xi)] + ([R("sbstg0"), R("sbstg1")] if xi == 3 else []), dma='x%d' % xi)
            S.add('sp', I('dma_start', out=cs[:], in_=cssrc), writes=[R("CS%d" % xi)], dma='c%d' % xi)
            S.add('act', I('activation', out=ub[par][:], in_=xb[:], func=AF.Square, accum_out=ssq[par][:]),
                  reads=[R("XB%d" % xi)], writes=[R("ub%d" % par), R("ssq%d" % par)])
            S.add('dve', I('tensor_scalar', out=vv[par][:], in0=ssq[par][:], scalar1=1.0 / D_MODEL, scalar2=EPS, op0=ALU.mult, op1=ALU.add),
                  reads=[R("ssq%d" % par)], writes=[R("vv%d" % par)])
            S.add('pool', I('tensor_tensor', out=rstd[par][:], in0=vv[par][:], in1=mhalf[:, 0:1], op=ALU.pow),
                  reads=[R("vv%d" % par), R("mhalf")], writes=[R("rstd%d" % par)])
            S.add('pool', I('tensor_tensor', out=u0[:], in0=xb[:], in1=prew_t[:], op=ALU.mult),
                  reads=[R("XB%d" % xi), R("prew_t")], writes=[R("u0")])
            S.add('act', I('activation', out=ub[par][:], in_=u0[:], func=AF.Copy, scale=rstd[par][:]),
                  reads=[R("u0"), R("rstd%d" % par)], writes=[R("ub%d" % par)])
            return dict(xi=xi, par=par)

        def front_b(ctx):
            par = ctx['par']
            tr_to(ub[par], "ub%d" % par, 8, uT[par][:].rearrange("p a b -> p (a b)"), "uT%d" % par)

        def tr_to(src_tile, src_res, ntile, dst_ap, dst_res, scale=None, src_off=0):
            pts = PT_all[mode[0]]
            pt, ptr = pts[nxt('pt', len(pts))]
            for t in range(ntile):
                S.add('pe', I('transpose', out=pt[:, t * 128:(t + 1) * 128], in_=src_tile[:, src_off + t * 128:src_off + (t + 1) * 128], identity=ident[:]),
                      reads=[R(src_res), R("ident_t")], writes=[ptr])
            if scale is None:
                S.add('act', I('activation', out=dst_ap, in_=pt[:, 0:ntile * 128], func=AF.Copy), reads=[ptr], writes=[R(dst_res)])
            else:
                S.add('act', I('activation', out=dst_ap, in_=pt[:, 0:ntile * 128], func=AF.Copy, scale=scale), reads=[ptr], writes=[R(dst_res)])

        def proj(par, wt, wres, c0, n=512):
            pas = PA_all[mode[0]]
            bank, bres = pas[nxt('pa', len(pas))]
            for kt in range(8):
                S.add('pe', I('matmul', out=bank[:, 0:n], lhsT=uT[par][:, kt, :], rhs=wt[:, kt, c0:c0 + n], start=(kt == 0), stop=(kt == 7)),
                      reads=[R("uT%d" % par)] + [R(r_) for r_ in (wres if isinstance(wres, (list, tuple)) else [wres])], writes=[bres])
            return bank[:, 0:n], bres

        def rotary(zb, zres, cs, csres, nh, dst_ap, dst_res, ri):
            n = nh * 128
            zv = zb.rearrange("p (h t f) -> p h t f", h=nh, t=2, f=64)
            A = rotA[ri][:, 0:n].rearrange("p (h t f) -> p h t f", h=nh, t=2, f=64)
            B = rotB[ri][:, 0:n].rearrange("p (h t f) -> p h t f", h=nh, t=2, f=64)
            dv = dst_ap.rearrange("p (h t f) -> p h t f", h=nh, t=2, f=64)
            cosb = bass.AP(cs, 0, [[128, 128], [0, nh], [0, 2], [1, 64]])
            sinb = bass.AP(cs, 64, [[128, 128], [0, nh], [1, 64]])
            rA, rB = R("rotA%d" % ri), R("rotB%d" % ri)
            S.add('dve', I('tensor_tensor', out=A, in0=zv, in1=cosb, op=ALU.mult), reads=[zres, R(csres)], writes=[rA])
            S.add('dve', I('tensor_tensor', out=B[:, :, 0, :], in0=zv[:, :, 1, :], in1=sinb, op=ALU.mult), reads=[zres, R(csres)], writes=[rB])
            S.add('dve', I('tensor_tensor', out=B[:, :, 1, :], in0=zv[:, :, 0, :], in1=sinb, op=ALU.mult), reads=[zres, R(csres)], writes=[rB])
            S.add('pool', I('tensor_tensor', out=dv[:, :, 0, :], in0=A[:, :, 0, :], in1=B[:, :, 0, :], op=ALU.subtract), reads=[rA, rB], writes=[R(dst_res)])
            S.add('pool', I('tensor_tensor', out=dv[:, :, 1, :], in0=A[:, :, 1, :], in1=B[:, :, 1, :], op=ALU.add), reads=[rA, rB], writes=[R(dst_res)])

        def bc_d(tile, off):
            return bass.AP(tile, off, [[tile.shape[1], 128], [1, 4], [0, 128]])

        def h3(ap):
            return ap.rearrange("p (h d) -> p h d", h=4, d=128)

        def kv_update(par, kd_tile, kd_res, bank_tile, bres, Stile, Sres, coef_tile, coef_off):
            for h in range(4):
                S.add('pe', I('matmul', out=bank_tile[:, h * 256:(h + 1) * 256], lhsT=kd_tile[:, h * 128:(h + 1) * 128],
                              rhs=v_tok[par][:, h * 256:(h + 1) * 256], start=True, stop=True),
                      reads=[R(kd_res), R("v_tok%d" % par)], writes=[bres[h // 2]])
            for h in range(4):
                S.add('dve', I('scalar_tensor_tensor',
                               out=Stile[:, h * 256:(h + 1) * 256], in0=Stile[:, h * 256:(h + 1) * 256], scalar=coef_tile[:, coef_off + h:coef_off + h + 1],
                               in1=bank_tile[:, h * 256:(h + 1) * 256], op0=ALU.mult, op1=ALU.add),
                      reads=[R(Sres), bres[h // 2], R(coef_tile.name)], writes=[R(Sres)])

        PSr = [R("PS0"), R("PS1")]
        POr = [R("PO0"), R("PO1")]
        final_dma = []

        nstg = [0]

        def st_head(t, ctx):
            xi, par = ctx['xi'], ctx['par']
            zk, zkr = proj(par, WW, "WW", 512)
            rotary(zk, zkr, CS[xi], "CS%d" % xi, 4, k_rot[par][:], "k_rot%d" % par, par)
            yield
            for hb in range(2):
                zv_, zvr = proj(par, WW, "WW", 1024 + hb * 512)
                S.add('act', I('activation', out=v_tok[par][:, hb * 512:(hb + 1) * 512], in_=zv_, func=AF.Copy),
                      reads=[zvr], writes=[R("v_tok%d" % par)])
                if hb == 0:
                    S.add('dve', I('tensor_tensor', out=h3(kdf[par][:]), in0=h3(k_rot[par][:]), in1=bc_d(KDSf, t * 4), op=ALU.mult),
                          reads=[R("k_rot%d" % par), R("KDSf")], writes=[R("kdf%d" % par)])
                    S.add('pool', I('tensor_tensor', out=h3(kdb[par][:]), in0=h3(k_rot[par][:]), in1=bc_d(KDSb, t * 4), op=ALU.mult),
                          reads=[R("k_rot%d" % par), R("KDSb")], writes=[R("kdb%d" % par)])
                yield

        def st_tail(t, ctx):
            par = ctx['par']
            if t >= 49:
                c = NSLOT - t
                sg = sbstg[nstg[0] % 2]
                sgr = "sbstg%d" % (nstg[0] % 2)
                nstg[0] += 1
                S.add('act', I('activation', out=sg[:], in_=S_b[:], func=AF.Copy), reads=[R("S_b")], writes=[R(sgr)])
                S.add('sp', I('dma_start', out=sbst[c * 128:(c + 1) * 128, :], in_=sg[:]), reads=[R(sgr)], writes=[R("sbst_d")], dma=sgr)
            kv_update(par, kdf[par], "kdf%d" % par, PS, PSr, S_f, "S_f", AFt, t * 4)
            yield
            kv_update(par, kdb[par], "kdb%d" % par, PO, POr, S_b, "S_b", ABt, t * 4)
            yield

        ctxs = {0: front_a(xs[0:128, :], css[0:128, :])}
        front_b(ctxs[0])
        prev_tail = None
        for t in range(NSLOT):
            if t + 1 < NSLOT:
                ctxs[t + 1] = front_a(xs[(t + 1) * 128:(t + 2) * 128, :], css[(t + 1) * 128:(t + 2) * 128, :])
            for _ in range(2):
                if deferred and DEFER_W:
                    deferred.pop(0)()
            interleave(prev_tail, st_head(t, ctxs[t]), k=2)
            if t + 1 < NSLOT:
                front_b(ctxs[t + 1])
            prev_tail = st_tail(t, ctxs[t])
        interleave(prev_tail, None)
        while deferred:
            deferred.pop(0)()
        sg = sbstg[nstg[0] % 2]
        sgr = "sbstg%d" % (nstg[0] % 2)
        S.add('act', I('activation', out=sg[:], in_=S_b[:], func=AF.Copy), reads=[R("S_b")], writes=[R(sgr)])
        S.add('sp', I('dma_start', out=sbst[0:128, :], in_=sg[:]), reads=[R(sgr)], writes=[R("sbst_d")], dma=sgr)
        S.add('act', I('activation', out=S_fbf[0][:], in_=S_f[:], func=AF.Copy), reads=[R("S_f")], writes=[R("S_fbf0")])
        if debug:
            final_dma.append(S.add('sp', I('dma_start', out=dbg['sf'], in_=S_f[:]), reads=[R("S_f")], dma='dbgsf'))
            final_dma.append(S.add('sp', I('dma_start', out=dbg['sb'], in_=S_b[:]), reads=[R("S_b")], dma='dbgsb'))

        if stop_after == 'state':
            S.add('sp', None, extra=final_dma)
            S.build()
            return nc
        mode[0] = 'main'

        def r_front(c):
            return front_a(xm[(2 + c) * 128:(3 + c) * 128, :], csm[(2 + c) * 128:(3 + c) * 128, :])

        def load_sb(c):
            sbc = sbstg[c % 2]
            sbr = "sbstg%d" % (c % 2)
            S.add('sp', I('dma_start', out=sbc[:], in_=sbst[c * 128:(c + 1) * 128, :]), reads=[R("sbst_d")], writes=[R(sbr)], dma=sbr)

        def r_head(c, ctx):
            xi, par = ctx['xi'], ctx['par']
            zq, zqr = proj(par, WW, "WWq", 0)
            rotary(zq, zqr, CS[xi], "CS%d" % xi, 4, q_rot[par][:, 0:512], "q_rot", 0)
            yield
            zk, zkr = proj(par, WW, "WW", 512)
            rotary(zk, zkr, CS[xi], "CS%d" % xi, 4, k_rot[par][:], "k_rot%d" % par, 1)
            tr_to(q_rot[par], "q_rot", 4, qT[par][:], "qT%d" % par)
            yield
            for hb in range(2):
                zv_, zvr = proj(par, WW, "WW", 1024 + hb * 512)
                S.add('act', I('activation', out=v_tok[par][:, hb * 512:(hb + 1) * 512], in_=zv_, func=AF.Copy),
                      reads=[zvr], writes=[R("v_tok%d" % par)])
                if hb == 0:
                    tr_to(k_rot[par], "k_rot%d" % par, 4, kT[par][:], kTR[par])
                    S.add('dve', I('tensor_tensor', out=h3(kdf[par][:]), in0=h3(k_rot[par][:]), in1=bc_d(KD8, 0), op=ALU.mult),
                          reads=[R("k_rot%d" % par), R("KD8")], writes=[R("kdf%d" % par)])
                yield
            for hb in range(2):
                zg, zgr = proj(par, WW, "WWq", 2048 + hb * 512)
                S.add('act', I('activation', out=g_r[par][:, hb * 512:(hb + 1) * 512], in_=zg, func=AF.Silu),
                      reads=[zgr], writes=[R("g_r%d" % par)])
                yield
            S.add('pool', I('tensor_tensor', out=g_r[par][:], in0=g_r[par][:], in1=nw_t[:], op=ALU.mult),
                  reads=[R("g_r%d" % par), R("nw_t")], writes=[R("g_r%d" % par)])
            for hb in range(2):
                zg, zgr = proj(par, WW, "WWq", 3072 + hb * 512)
                S.add('act', I('activation', out=sg_r[par][:, hb * 512:(hb + 1) * 512], in_=zg, func=AF.Tanh, scale=0.5),
                      reads=[zgr], writes=[R("sg_r%d" % par)])
                yield

        def r_tail_a(c, ctx):
            par = ctx['par']
            sbc = sbstg[c % 2]
            sbr = "sbstg%d" % (c % 2)
            for h in range(4):
                hs = slice(h * 128, (h + 1) * 128)
                S.add('pe', I('matmul', out=PS[:, hs], lhsT=kT[par][:, hs], rhs=qT[par][:, hs], start=True, stop=True),
                      reads=[R(kTR[par]), R("qT%d" % par)], writes=[PSr[0]])
            S.add('dve', I('tensor_tensor', out=sd[par][:], in0=PS[:, 0:512], in1=DT[:], op=ALU.mult), reads=[PSr[0], R("DT")], writes=[R("sd")])
            S.add('dve', I('tensor_tensor', out=qdf[par][:], in0=qT[par][:], in1=QDF[:], op=ALU.mult), reads=[R("qT%d" % par), R("QDF")], writes=[R("qdf")])
            S.add('pool', I('tensor_tensor', out=qdb[par][:], in0=qT[par][:], in1=QDB[:], op=ALU.mult), reads=[R("qT%d" % par), R("QDB")], writes=[R("qdb")])
            yield
            sfb = S_fbf[0]
            sfr = "S_fbf0"
            for h in range(4):
                hs = slice(h * 128, (h + 1) * 128)
                vs = slice(h * 256, (h + 1) * 256)
                S.add('pe', I('matmul', out=PO[:, vs], lhsT=qdf[par][:, hs], rhs=sfb[:, vs], start=True, stop=False),
                      reads=[R("qdf"), R(sfr)], writes=[POr[h // 2]])
                S.add('pe', I('matmul', out=PO[:, vs], lhsT=qdb[par][:, hs], rhs=sbc[:, vs], start=False, stop=False),
                      reads=[R("qdb"), R(sbr)], writes=[POr[h // 2]])
                S.add('pe', I('matmul', out=PO[:, vs], lhsT=sd[par][:, hs], rhs=v_tok[par][:, vs], start=False, stop=True),
                      reads=[R("sd"), R("v_tok%d" % par)], writes=[POr[h // 2]])
            if c < NOWN - 1:
                kv_update(par, kdf[par], "kdf%d" % par, PS, PSr, S_f, "S_f", CD8, 0)
                S.add('dve', I('tensor_copy', out=S_fbf[0][:], in_=S_f[:]), reads=[R("S_f")], writes=[R("S_fbf0")])
            for h in range(4):
                S.add('dve', I('bn_stats', out=gnst[:, h * 6:(h + 1) * 6], in_=PO[:, h * 256:(h + 1) * 256]), reads=[POr[h // 2]], writes=[R("gnst")])
            for h in range(4):
                S.add('dve', I('bn_aggr', out=gnmv[:, h * 2:(h + 1) * 2], in_=gnst[:, h * 6:(h + 1) * 6]), reads=[R("gnst")], writes=[R("gnmv")])
            mvv = gnmv[:].rearrange("p (h t) -> p h t", h=4, t=2)
            S.add('dve', I('tensor_scalar', out=gnve[:], in0=mvv[:, :, 1], scalar1=EPS, scalar2=None, op0=ALU.add), reads=[R("gnmv")], writes=[R("gnve")])
            S.add('pool', I('tensor_tensor', out=gnrs[:], in0=gnve[:], in1=mhalf[:, 0:4], op=ALU.pow), reads=[R("gnve"), R("mhalf")], writes=[R("gnrs")])
            S.add('dve', I('scalar_tensor_tensor', out=gnnm[:], in0=mvv[:, :, 0], scalar=-1.0, in1=gnrs[:], op0=ALU.mult, op1=ALU.mult),
                  reads=[R("gnmv"), R("gnrs")], writes=[R("gnnm")])
            for h in range(4):
                vs = slice(h * 256, (h + 1) * 256)
                S.add('dve', I('tensor_scalar', out=o_n[:, vs], in0=PO[:, vs], scalar1=gnrs[:, h:h + 1], scalar2=gnnm[:, h:h + 1], op0=ALU.mult, op1=ALU.add),
                      reads=[POr[h // 2], R("gnrs"), R("gnnm")], writes=[R("o_n")])
            S.add('dve', I('tensor_tensor', out=o_g[:], in0=o_n[:], in1=g_r[par][:], op=ALU.mult), reads=[R("o_n"), R("g_r%d" % par)], writes=[R("o_g")])
            yield

        def r_tail_b(c, ctx):
            par = ctx['par']
            tr_to(o_g, "o_g", 8, o_gT[:].rearrange("p a b -> p (a b)"), "o_gT")
            yield
            for nb_ in range(2):
                for kt in range(8):
                    S.add('pe', I('matmul', out=PO[:, nb_ * 512:(nb_ + 1) * 512], lhsT=o_gT[:, kt, :], rhs=W2[:, kt, nb_ * 512:(nb_ + 1) * 512],
                                  start=(kt == 0), stop=(kt == 7)),
                          reads=[R("o_gT"), R("W2")], writes=[POr[nb_]])
            mr = mixr[0]
            mrr = "mixr"
            for nb_ in range(2):
                S.add('dve', I('scalar_tensor_tensor', out=mr[:, nb_ * 512:(nb_ + 1) * 512], in0=sg_r[par][:, nb_ * 512:(nb_ + 1) * 512], scalar=1.0,
                               in1=PO[:, nb_ * 512:(nb_ + 1) * 512], op0=ALU.add, op1=ALU.mult),
                      reads=[R("sg_r%d" % par), POr[nb_]], writes=[R(mrr)])
            S.add('sp', I('dma_start', out=mixrd[c * 128:(c + 1) * 128, :], in_=mr[:]), reads=[R(mrr)], writes=[R("mixr_d")], dma=mrr)
            if debug:
                final_dma.append(S.add('sp', I('dma_start', out=dbg['mixr'][c * 128:(c + 1) * 128, :], in_=mr[:]), reads=[R(mrr)], dma='dbgm'))
            yield

        ctxs = {0: r_front(0)}
        front_b(ctxs[0])
        def chain(*gens):
            for g_ in gens:
                if g_ is not None:
                    yield from g_

        ctxs[1] = r_front(1)
        for c in range(NOWN + 2):
            if c < NOWN:
                load_sb(c)
            if c + 2 < NOWN:
                ctxs[c + 2] = r_front(c + 2)
            if c + 1 < NOWN:
                front_b(ctxs[c + 1])
            if c == NOWN:
                ld(nw_t[:], postw, "nw_t", key='pw')
                def reload(rn, sres, d0, n, key):
                    S.add('sp', I('dma_start', out=WW[:, :, d0:d0 + n], in_=wbf[:, d0:d0 + n].rearrange("(kt p) c -> p kt c", p=128)),
                          reads=[R(sres)], writes=[R("WW"), R("WWq"), R(rn)], dma=key)
                reload("WAkv", "wbf_kv", 1024, 512, 'w4a')
                reload("WAq", "wbf_q", 0, 1024, 'w4b')
                reload("WAg", "wbf_g", 1536, 1024, 'w4c')
                reload("WAs", "wbf_s", 2560, 1024, 'w4d')
            tb = r_tail_b(c - 2, ctxs[c - 2]) if 0 <= c - 2 < NOWN else None
            ta = r_tail_a(c - 1, ctxs[c - 1]) if 0 <= c - 1 < NOWN else None
            hd = r_head(c, ctxs[c]) if c < NOWN else None
            interleave(chain(tb, ta), hd, k=2, lead=1)

        if stop_after == 'R':
            S.add('sp', None, extra=final_dma)
            S.build()
            return nc
        S.add('sp', I('dma_start', out=W2[:], in_=wabf.rearrange("(kt p) c -> p kt c", p=128)), reads=[R("wabf")], writes=[R("W2")], dma='w6')
        while deferred:
            deferred.pop(0)()

        for p in range(2):
            for g in range(2):
                S.add('pool', I('memset', PTm[p][g][:], 0.0), writes=[R(PTmR[p][g])])
        for p in range(2):
            for g in range(2):
                for hh in range(4):
                    h = g * 4 + hh
                    S.add('act', I('activation', out=PTm[p][g][32:33, hh * 128:(hh + 1) * 128], in_=zrow[32:33, :], func=AF.Exp,
                                   bias=sink_t[32:33, h:h + 1], scale=1.0),
                          reads=[R("zrow"), R("sink_t")], writes=[R(PTmR[p][g])])

        def a_head(idx, ctx, nctx=None):
            xi, par = ctx['xi'], ctx['par']
            zkv, zkvr = proj(par, WW, "WAkv", 1024)
            if idx == 0:
                ktile, kres, vtile, vres = KTm, "KTm", Vm, "Vm"
            else:
                ktile, kres, vtile, vres = KT[idx % 4], "KT%d" % (idx % 4), VA[idx % 4], "VA%d" % (idx % 4)
            if idx == 0:
                S.add('act', I('activation', out=Vm[0:16, :, 0:128], in_=zkv[0:16, 256:512].rearrange("p (g d) -> p g d", g=2, d=128), func=AF.Copy),
                      reads=[zkvr], writes=[R("Vm")])
            else:
                S.add('act', I('activation', out=vtile[:, :, 0:128], in_=zkv[:, 256:512].rearrange("p (g d) -> p g d", g=2, d=128), func=AF.Copy),
                      reads=[zkvr], writes=[R(vres)])
            rotary(zkv[:, 0:256], zkvr, CS[xi], "CS%d" % xi, 2, k_rot[par][:, 0:256], "k_rot%d" % par, 0)
            yield
            if 2 <= idx <= 17:
                c = idx - 2
                pa_c = c % 2
                own_info[c] = (xi, pa_c)
                for hb in range(2):
                    zq, zqr = proj(par, WW, "WAq", hb * 512)
                    rotary(zq, zqr, CS[xi], "CS%d" % xi, 4, q_rot[par][:, hb * 512:(hb + 1) * 512], "q_rot", 1)
                    if hb == 0:
                        tr_to(k_rot[par], "k_rot%d" % par, 2, ktile[:].rearrange("p a b -> p (a b)"), kres)
                    yield
                for hb in range(2):
                    zg, zgr = proj(par, WW, "WAg", 1536 + hb * 512)
                    S.add('act', I('activation', out=th_t[:], in_=zg, func=AF.Tanh, scale=0.5), reads=[zgr], writes=[R("QDF")])
                    S.add('dve', I('scalar_tensor_tensor', out=g_r[pa_c][:, hb * 512:(hb + 1) * 512], in0=th_t[:], scalar=1.0, in1=zg, op0=ALU.add, op1=ALU.mult),
                          reads=[R("QDF"), zgr], writes=[R("g_r%d" % pa_c)])
                    if hb == 0:
                        tr_to(q_rot[par], "q_rot", 8, v_tok[pa_c][:], "v_tok%d" % pa_c)
                    yield
                for hb in range(2):
                    zg, zgr = proj(par, WW, "WAs", 2560 + hb * 512)
                    S.add('act', I('activation', out=sg_r[pa_c][:, hb * 512:(hb + 1) * 512], in_=zg, func=AF.Tanh, scale=0.5),
                          reads=[zgr], writes=[R("sg_r%d" % pa_c)])
                    yield
            else:
                tr_to(k_rot[par], "k_rot%d" % par, 2, ktile[:].rearrange("p a b -> p (a b)"), kres)
                yield
            if nctx is not None:
                front_b(nctx)
                yield

        def a_post(c, xi_c):
            q = c % 2
            S.add('act', I('activation', out=o_n[:], in_=PO[:, :], func=AF.Square, accum_out=ssq[q][:]), reads=POr, writes=[R("o_n"), R("ssq%d" % q)])
            S.add('dve', I('tensor_scalar', out=vv[q][:], in0=ssq[q][:], scalar1=1.0 / D_MODEL, scalar2=EPS, op0=ALU.mult, op1=ALU.add),
                  reads=[R("ssq%d" % q)], writes=[R("vv%d" % q)])
            S.add('pool', I('tensor_tensor', out=rstd[q][:], in0=vv[q][:], in1=mhalf[:, 0:1], op=ALU.pow), reads=[R("vv%d" % q), R("mhalf")], writes=[R("rstd%d" % q)])
            for nb_ in range(2):
                cs_ = slice(nb_ * 512, (nb_ + 1) * 512)
                S.add('dve', I('scalar_tensor_tensor', out=tfin[:, cs_], in0=PO[:, cs_], scalar=rstd[q][:], in1=nw_t[:, cs_], op0=ALU.mult, op1=ALU.mult),
                      reads=[POr[nb_], R("rstd%d" % q), R("nw_t")], writes=[R("S_b")])
            S.add('pool', I('tensor_tensor', out=tfin[:], in0=tfin[:], in1=XB[xi_c][:], op=ALU.add), reads=[R("S_b"), R("XB%d" % xi_c)], writes=[R("S_b")])
            final_dma.append(S.add('sp', I('dma_start', out=out[c * 128:(c + 1) * 128, :], in_=tfin[:]), reads=[R("S_b")], dma="resb"))

        def a_tail(c, xi_c, pa_c, post=None):
            if post is not None:
                a_post(*post)
            blocks = []
            for bi, idx in enumerate((c + 1, c + 2, c + 3)):
                if bi == 0:
                    mk = 2 if c == 0 else 0
                elif bi == 2:
                    mk = 3 if c == NOWN - 1 else 1
                else:
                    mk = None
                blocks.append((idx % 4, mk))
            pm = c % 2
            def tr_group(g):
                tr_to(o_g, "o_g", 4, o_gT[:, g * 4:(g + 1) * 4, :].rearrange("p a b -> p (a b)"), "o_gT", src_off=g * 512)

            for g in range(2):
                qg = v_tok[pa_c][:, g * 512:(g + 1) * 512]
                for bi, (sl, mk) in enumerate(blocks):
                    bank = PS[:, (bi % 2) * 512:(bi % 2 + 1) * 512]
                    br = PSr[bi % 2]
                    S.add('pe', I('matmul', out=bank, lhsT=KT[sl][:, g, :], rhs=qg, start=True, stop=(mk is None)),
                          reads=[R("KT%d" % sl), R("v_tok%d" % pa_c)], writes=[br])
                    if mk is not None:
                        S.add('pe', I('matmul', out=bank, lhsT=ident[:], rhs=mask_t[:, mk * 512:(mk + 1) * 512], start=False, stop=True),
                              reads=[R("ident_t"), R("mask_t")], writes=[br])
                    S.add('act', I('activation', out=PTb[g][bi][:], in_=bank, func=AF.Exp, scale=SCALE),
                          reads=[br], writes=[R(PTbR[g][bi])])
                bank = PS[0:16, 512:1024]
                S.add('pe', I('matmul', out=bank, lhsT=KTm[:, g, 0:16], rhs=qg, start=True, stop=True),
                      reads=[R("KTm"), R("v_tok%d" % pa_c)], writes=[PSr[1]])
                S.add('act', I('activation', out=PTm[pm][g][0:16, :], in_=bank, func=AF.Exp, scale=SCALE),
                      reads=[PSr[1]], writes=[R(PTmR[pm][g])])
                if g == 1:
                    tr_group(0)
                yield
                for hh in range(4):
                    dst = PO[:, hh * 129:(hh + 1) * 129] if hh < 3 else PO[:, 512:641]
                    dr = POr[0] if hh < 3 else POr[1]
                    for bi, (sl, mk) in enumerate(blocks):
                        S.add('pe', I('matmul', out=dst, lhsT=PTb[g][bi][:, hh * 128:(hh + 1) * 128], rhs=VA[sl][:, g, 0:129],
                                      start=(bi == 0), stop=False),
                              reads=[R(PTbR[g][bi]), R("VA%d" % sl)], writes=[dr])
                    S.add('pe', I('matmul', out=dst, lhsT=PTm[pm][g][0:33, hh * 128:(hh + 1) * 128], rhs=Vm[0:33, g, 0:129], start=False, stop=True),
                          reads=[R(PTmR[pm][g]), R("Vm")], writes=[dr])
                l3 = PO[:, 0:387].rearrange("p (h c) -> p h c", h=3, c=129)[:, :, 128]
                S.add('dve', I('reciprocal', out=rl[:, 0:3], in_=l3), reads=[POr[0]], writes=[R("rl")])
                S.add('dve', I('reciprocal', out=rl[:, 3:4], in_=PO[:, 640:641]), reads=[POr[1], R("rl")], writes=[R("rl")])
                S.add('dve', I('tensor_scalar', out=rl[:], in0=rl[:], scalar1=0.5, scalar2=None, op0=ALU.mult), reads=[R("rl")], writes=[R("rl")])
                for hh in range(4):
                    h = g * 4 + hh
                    src_ = PO[:, hh * 129:hh * 129 + 128] if hh < 3 else PO[:, 512:640]
                    dr = POr[0] if hh < 3 else POr[1]
                    S.add('dve', I('scalar_tensor_tensor', out=o_g[:, h * 128:(h + 1) * 128], in0=src_, scalar=rl[:, hh:hh + 1],
                                   in1=g_r[pa_c][:, h * 128:(h + 1) * 128], op0=ALU.mult, op1=ALU.mult),
                          reads=[dr, R("rl"), R("g_r%d" % pa_c)], writes=[R("o_g")])
                if g == 0:
                    S.add('sp', I('dma_start', out=mixr[0][:], in_=mixrd[c * 128:(c + 1) * 128, :]), reads=[R("mixr_d")], writes=[R("mixr")], dma="mixr")
                yield
            tr_group(1)
            for nb_ in range(2):
                for kt in range(8):
                    S.add('pe', I('matmul', out=PO[:, nb_ * 512:(nb_ + 1) * 512], lhsT=o_gT[:, kt, :], rhs=W2[:, kt, nb_ * 512:(nb_ + 1) * 512],
                                  start=(kt == 0), stop=(kt == 7)),
                          reads=[R("o_gT"), R("W2")], writes=[POr[nb_]])
            for nb_ in range(2):
                cs_ = slice(nb_ * 512, (nb_ + 1) * 512)
                S.add('dve', I('scalar_tensor_tensor', out=mixa[:, cs_], in0=sg_r[pa_c][:, cs_], scalar=1.0, in1=PO[:, cs_], op0=ALU.add, op1=ALU.mult),
                      reads=[R("sg_r%d" % pa_c), POr[nb_]], writes=[R("S_f")])
            S.add('pool', I('tensor_tensor', out=mixb[:], in0=mixa[:], in1=mixr[0][:], op=ALU.add), reads=[R("S_f"), R("mixr")], writes=[R("o_n")])
            yield
            tr_to(mixb, "o_n", 8, S_fbf[0][:], "S_fbf0", scale=0.5)
            yield
            for nb_ in range(2):
                for kt in range(8):
                    S.add('pe', I('matmul', out=PO[:, nb_ * 512:(nb_ + 1) * 512], lhsT=mixT[:, kt, :], rhs=W3[:, kt, nb_ * 512:(nb_ + 1) * 512],
                                  start=(kt == 0), stop=(kt == 7)),
                          reads=[R("S_fbf0"), R("W3")], writes=[POr[nb_]])
            yield

        own_info = {}
        ctxs = {0: front_a(xm[0:128, :], csm[0:128, :], ring=4)}
        front_b(ctxs[0])
        for idx in range(a_steps):
            if idx + 1 < 19:
                ctxs[idx + 1] = front_a(xm[(idx + 1) * 128:(idx + 2) * 128, :], csm[(idx + 1) * 128:(idx + 2) * 128, :], ring=4)
            tail = None
            if idx >= 3:
                c_ = idx - 3
                post = (c_ - 1, own_info[c_ - 1][0]) if c_ >= 1 else None
                tail = a_tail(c_, *own_info[c_], post=post)
            hd = a_head(idx, ctxs[idx], ctxs.get(idx + 1))
            if PIPE_A:
                interleave(tail, hd, k=1, lead=2)
            else:
                interleave(None, hd)
                interleave(tail, None)

        if a_steps == 19:
            a_post(NOWN - 1, own_info[NOWN - 1][0])
        S.add('sp', None, extra=final_dma)
        S.build()
        print("sched stats", S.stats, flush=True)
    return nc


def _host_prep(inputs):
    x = np.asarray(inputs["x"], np.float32)
    meta = np.asarray(inputs["meta_tokens"], np.float32)
    import jax
    import jax.numpy as jnp
    _cpu = jax.devices("cpu")[0]
    with jax.default_device(_cpu):
        inv_j = 10000.0 ** (-jnp.arange(64, dtype=jnp.float32) * 2.0 / 128)

    def chunk_data(b, n):
        if n == 0:
            z = np.zeros((128, 1024), np.float32)
            z[112:] = meta
            return z
        if n > 64:
            return np.zeros((128, 1024), np.float32)
        return x[b, (n - 1) * 128:n * 128]

    def cs_of(pos):
        with jax.default_device(_cpu):
            ang = jnp.asarray(np.asarray(pos, np.float32))[:, None] * inv_j[None, :]
            return np.concatenate([np.asarray(jnp.cos(ang), np.float32), np.asarray(jnp.sin(ang), np.float32)], axis=1)

    def chunk_pos(n):
        return np.maximum(n * 128 + np.arange(128) - 112, 0)

    jj = np.arange(128)[:, None].astype(np.float64)
    ii = np.arange(128)[None, :].astype(np.float64)
    sc = 128.0 ** -0.5
    rtab = np.concatenate([np.maximum(ii - jj, 0), (ii >= jj) * sc, np.maximum(jj - ii, 0), (jj > ii) * sc], axis=1).astype(np.float32)
    ctab = np.concatenate([np.broadcast_to(ii + 1, (128, 128)), np.broadcast_to(128 - ii, (128, 128))], axis=1).astype(np.float32)
    jtab = np.concatenate([127 - jj, jj], axis=1).astype(np.float32)
    bf = ml_dtypes.bfloat16
    m_prev = np.where(jj >= ii, 0.0, NEG).astype(np.float32)
    m_next = np.where(jj <= ii, 0.0, NEG).astype(np.float32)
    m_all = np.full((128, 128), NEG, np.float32)
    ident = np.eye(128, dtype=np.float32).astype(bf)

    def bc(v, n):
        return np.ascontiguousarray(np.broadcast_to(np.asarray(v, np.float32).reshape(1, n), (128, n)))

    common = dict(
        w_in=np.ascontiguousarray(inputs["w_in"][0], dtype=np.float32),
        w_rb=np.ascontiguousarray(inputs["w_ret_branch"][0], dtype=np.float32),
        w_ab=np.ascontiguousarray(inputs["w_attn_branch"][0], dtype=np.float32),
        w_o=np.ascontiguousarray(inputs["w_out"][0], dtype=np.float32),
        prew=bc(inputs["pre_norm_w"][0], 1024), postw=bc(inputs["post_norm_w"][0], 1024), nwb=bc(inputs["ret_norm_w"][0], 1024),
        dec8=bc(np.concatenate([np.asarray(inputs["ret_decay_fwd"][0]), np.asarray(inputs["ret_decay_bwd"][0])]), 8),
        sink8=bc(inputs["attn_sink"][0], 8),
        rtab=rtab, ctab=ctab, jtab=jtab, ident=ident,
    )
    in_maps = []
    for core in range(8):
        b, s = divmod(core, 4)
        fwd = list(range(0, 16 * s + 1))
        bwd = list(range(64, 16 * s + 16, -1))
        own = [16 * s + 1 + c for c in range(15, 0, -1)]
        slots = fwd + bwd + own
        assert len(slots) == NSLOT
        xs = np.concatenate([chunk_data(b, n) for n in slots], axis=0)
        css = np.concatenate([cs_of(chunk_pos(n)) for n in slots], axis=0)
        actf = np.array([1.0] * len(fwd) + [0.0] * (NSLOT - len(fwd)), np.float32)
        acttab = bc(np.concatenate([actf, 1.0 - actf]), 128)
        metac = np.zeros((128, 1024), np.float32)
        metac[:16] = meta
        mpos = np.zeros(128)
        mpos[:16] = np.arange(16)
        mchunks = [16 * s] + [16 * s + 1 + c for c in range(16)] + [16 * s + 17]
        xm = np.concatenate([metac] + [chunk_data(b, n) for n in mchunks], axis=0)
        csm = np.concatenate([cs_of(mpos)] + [cs_of(chunk_pos(n)) for n in mchunks], axis=0)
        mk = [m_prev, m_next, m_all if s == 0 else m_prev, m_all if s == 3 else m_next]
        masks = np.concatenate([np.tile(m, (1, 4)) for m in mk], axis=1).astype(bf)
        d = dict(common)
        d.update(xs=xs, css=css, xm=xm, csm=csm, acttab=acttab, masks=masks)
        in_maps.append(d)
    return in_maps


_NC_CACHE = {}


def kernel(**inputs):
    in_maps = _host_prep(inputs)
    if 'nc' not in _NC_CACHE:
        _NC_CACHE['nc'] = build_nc()
    nc = _NC_CACHE['nc']
    res = run_bass_kernel_spmd(nc, in_maps, core_ids=list(range(8)))
    out = np.empty((2, SEQ, D_MODEL), np.float32)
    for core in range(8):
        b, s = divmod(core, 4)
        out[b, s * 2048:(s + 1) * 2048] = np.asarray(res.results[core]["out"], np.float32)
    return out
```

```python
import contextlib
import numpy as np
import ml_dtypes
import concourse.bass as bass
import concourse.mybir as mybir
from concourse.bass_utils import run_bass_kernel_spmd

F32 = mybir.dt.float32
BF16 = mybir.dt.bfloat16
AF = mybir.ActivationFunctionType
ALU = mybir.AluOpType

SAME_ENGINE_SYNC = True
DEFER_W = True
PIPE_A = True
ENGS = ['pe', 'dve', 'act', 'pool', 'sp']

D_MODEL = 1024
SEQ = 8192
N_META = 16
NSLOT = 64
NOWN = 16
EPS = 1e-6
SCALE = 128 ** -0.5
LN_SCALE = float(-0.5 * np.log(128.0))
NEG = -30000.0


class Res:
    __slots__ = ('name', 'last_w', 'readers', 'excl')

    def __init__(self, name):
        self.name = name
        self.last_w = None
        self.readers = []
        self.excl = False


class Sched:
    def __init__(self, nc):
        self.nc = nc
        self.ops = []

    def add(self, eng, fn, reads=(), writes=(), dma=None, extra=()):
        self.ops.append((eng, fn, tuple(reads), tuple(writes), dma, tuple(extra)))
        return len(self.ops) - 1

    def build(self):
        nc = self.nc
        last = {}
        for i, op in enumerate(self.ops):
            if op[4] is not None:
                last[op[4]] = i
        self.add('sp', None, extra=sorted(last.values()))
        ops = self.ops
        n = len(ops)
        deps = [None] * n
        signaled = [False] * n
        for i, (eng, fn, reads, writes, dk, extra) in enumerate(ops):
            d = set(extra)
            for r in reads:
                if r.last_w is not None:
                    d.add(r.last_w)
                if r.excl:
                    d.update(j for j in r.readers if ops[j][0] != eng)
            for w in writes:
                if w.last_w is not None:
                    d.add(w.last_w)
                d.update(w.readers)
            d.discard(i)
            dd = set()
            latest = {}
            for j in d:
                ej = ops[j][0]
                dmaj = ops[j][4] is not None
                if dmaj and dk is not None and ops[j][4] == dk and dk.startswith('T:'):
                    continue
                if (not dmaj) and ej == eng and dk is None and (eng == 'pe' or not SAME_ENGINE_SYNC):
                    continue
                if dmaj:
                    dd.add(j)
                else:
                    latest[ej] = max(latest.get(ej, -1), j)
            dd.update(latest.values())
            deps[i] = dd
            for j in dd:
                signaled[j] = True
            for r in reads:
                r.readers.append(i)
            for w in writes:
                w.last_w = i
                w.readers = []
        cnt = {e: 0 for e in ENGS}
        val = [0] * n
        dmacnt = {}
        for i, op in enumerate(ops):
            if op[4] is not None:
                dmacnt[op[4]] = dmacnt.get(op[4], 0) + 1
                val[i] = 16 * dmacnt[op[4]]
            elif signaled[i]:
                cnt[op[0]] += 1
                val[i] = cnt[op[0]]
        for i, op in enumerate(ops):
            if op[4] is not None and op[4].startswith('T:'):
                val[i] = 16 * dmacnt[op[4]]
        self.stats = dict(n_ops=n, milestones=dict(cnt), dma_keys=len(dmacnt))
        with contextlib.ExitStack() as st:
            sem_eng = {e: st.enter_context(nc.semaphore('s_' + e)) for e in ENGS}
            dma_sem = {k: st.enter_context(nc.semaphore('d_' + k.replace(':', '_'))) for k in dmacnt}

            def emit(engname, e):
                waited = {}
                for i, op in enumerate(ops):
                    if op[0] != engname:
                        continue
                    need = {}
                    for j in deps[i]:
                        if ops[j][4] is not None:
                            key = ('d', ops[j][4])
                            sem = dma_sem[ops[j][4]]
                        else:
                            key = ('e', ops[j][0])
                            sem = sem_eng[ops[j][0]]
                        if val[j] > need.get(key, (None, 0))[1]:
                            need[key] = (sem, val[j])
                    for key, (sem, v) in need.items():
                        if waited.get(key, 0) >= v:
                            continue
                        e.wait_ge(sem, v)
                        waited[key] = v
                    if op[1] is None:
                        continue
                    inst = op[1](e)
                    if op[4] is not None:
                        inst.then_inc(dma_sem[op[4]], 16)
                    elif signaled[i]:
                        inst.then_inc(sem_eng[engname], 1)

            with nc.Block() as block:
                @block.tensor
                def _(e):
                    emit('pe', e)

                @block.vector
                def _(e):
                    emit('dve', e)

                @block.scalar
                def _(e):
                    emit('act', e)

                @block.gpsimd
                def _(e):
                    emit('pool', e)

                @block.sync
                def _(e):
                    emit('sp', e)


def I(name, *a, **kw):
    return lambda e: getattr(e, name)(*a, **kw)


def build_nc(debug=False, stop_after=None, a_steps=19):
    nc = bass.Bass("TRN2", target_bir_lowering=False)

    def din(name, shape, dt=F32):
        return nc.dram_tensor(name, shape, dt, kind="ExternalInput").ap()

    xs = din("xs", [NSLOT * 128, 1024])
    css = din("css", [NSLOT * 128, 128])
    xm = din("xm", [19 * 128, 1024])
    csm = din("csm", [19 * 128, 128])
    w_in = din("w_in", [1024, 7680])
    w_rb = din("w_rb", [1024, 1024])
    w_ab = din("w_ab", [1024, 1024])
    w_o = din("w_o", [1024, 1024])
    prew = din("prew", [128, 1024])
    postw = din("postw", [128, 1024])
    nwb = din("nwb", [128, 1024])
    dec8 = din("dec8", [128, 8])
    sink8 = din("sink8", [128, 8])
    rtab = din("rtab", [128, 512])
    ctab = din("ctab", [128, 256])
    jtab = din("jtab", [128, 2])
    acttab = din("acttab", [128, 128])
    masks = din("masks", [128, 4 * 512], BF16)
    identd = din("ident", [128, 128], BF16)
    out = nc.dram_tensor("out", [NOWN * 128, 1024], F32, kind="ExternalOutput").ap()
    sbst = nc.dram_tensor("sbst", [NOWN * 128, 1024], BF16, kind="Internal").ap()
    mixrd = nc.dram_tensor("mixrd", [NOWN * 128, 1024], BF16, kind="Internal").ap()
    wbf = nc.dram_tensor("wbf", [1024, 3584], BF16, kind="Internal").ap()
    wabf = nc.dram_tensor("wabf", [1024, 1024], BF16, kind="Internal").ap()
    dbg = {}
    if debug:
        dbg['sf'] = nc.dram_tensor("dbg_sf", [128, 1024], F32, kind="ExternalOutput").ap()
        dbg['sb'] = nc.dram_tensor("dbg_sb", [128, 1024], F32, kind="ExternalOutput").ap()
        dbg['mixr'] = nc.dram_tensor("dbg_mixr", [NOWN * 128, 1024], BF16, kind="ExternalOutput").ap()

    with contextlib.ExitStack() as st:
        RES = {}

        def R(name):
            if name not in RES:
                RES[name] = Res(name)
            return RES[name]

        def sb(name, shape, dt):
            t = st.enter_context(nc.sbuf_tensor(name, shape, dt))
            R(name)
            return t

        def ps(name, shape, dt):
            return st.enter_context(nc.psum_tensor(name, shape, dt))

        S = Sched(nc)

        WW = sb("WW", [128, 8, 4096], BF16)
        W2 = sb("W2", [128, 8, 1024], BF16)
        W3 = sb("W3", [128, 8, 1024], BF16)
        prew_t = sb("prew_t", [128, 1024], F32)
        nw_t = sb("nw_t", [128, 1024], F32)
        ident = sb("ident_t", [128, 128], BF16)
        mask_t = sb("mask_t", [128, 4 * 512], BF16)
        dec_t = sb("dec_t", [128, 8], F32)
        sink_t = sb("sink_t", [128, 8], F32)
        jtab_t = sb("jtab_t", [128, 2], F32)
        act_t = sb("act_t", [128, 128], F32)
        e8 = sb("e8", [128, 8], F32)
        lg8 = sb("lg8", [128, 8], F32)
        lg128 = sb("lg128", [128, 8], F32)
        DT = sb("DT", [128, 512], BF16)
        QDF = sb("QDF", [128, 512], F32)
        QDB = sb("QDB", [128, 512], F32)
        KD8 = sb("KD8", [128, 8], F32)
        CD8 = sb("CD8", [128, 8], F32)
        KDSf = sb("KDSf", [128, 256], F32)
        KDSb = sb("KDSb", [128, 256], F32)
        AFt = sb("AFt", [128, 256], F32)
        ABt = sb("ABt", [128, 256], F32)
        zrow = sb("zrow", [128, 128], F32)
        S_f = sb("S_f", [128, 1024], F32)
        S_b = sb("S_b", [128, 1024], F32)
        S_fbf1 = sb("S_fbf0", [128, 1024], BF16)
        S_fbf = [S_fbf1, S_fbf1]
        sbst2 = sb("sbst2", [128, 2048], BF16)
        sbstg = [sbst2[:, 0:1024], sbst2[:, 1024:2048]]
        R("sbstg0")
        R("sbstg1")
        XB = [sb("XB%d" % i, [128, 1024], F32) for i in range(3)]
        XB.append(sbst2.bitcast(F32))
        R("XB3")
        CS = [sb("CS%d" % i, [128, 128], F32) for i in range(4)]
        ssq = [sb("ssq%d" % i, [128, 1], F32) for i in range(2)]
        vv = [sb("vv%d" % i, [128, 1], F32) for i in range(2)]
        rstd = [sb("rstd%d" % i, [128, 1], F32) for i in range(2)]
        mhalf = sb("mhalf", [128, 8], F32)
        u0 = sb("u0", [128, 1024], F32)
        ub = [sb("ub%d" % i, [128, 1024], BF16) for i in range(2)]
        uT = [sb("uT%d" % i, [128, 8, 128], BF16) for i in range(2)]
        rotA = [sb("rotA%d" % i, [128, 512], F32) for i in range(2)]
        rotB = [sb("rotB%d" % i, [128, 512], F32) for i in range(2)]
        k_rot = [sb("k_rot%d" % i, [128, 512], BF16) for i in range(2)]
        q_rot1 = sb("q_rot", [128, 1024], BF16)
        q_rot = [q_rot1, q_rot1]
        kdf = [sb("kdf%d" % i, [128, 512], BF16) for i in range(2)]
        kdb = [sb("kdb%d" % i, [128, 512], BF16) for i in range(2)]
        v_tok = [sb("v_tok%d" % i, [128, 1024], BF16) for i in range(2)]
        qT = [sb("qT%d" % i, [128, 512], BF16) for i in range(2)]
        kT1 = sb("kT", [128, 512], BF16)
        qdf1 = sb("qdf", [128, 512], BF16)
        qdb1 = sb("qdb", [128, 512], BF16)
        sd1 = sb("sd", [128, 512], BF16)
        kT2 = sb("kT2", [128, 512], BF16)
        kT, qdf, qdb, sd = [kT1, kT2], [qdf1] * 2, [qdb1] * 2, [sd1] * 2
        kTR = ["kT", "kT2"]
        g_r = [sb("g_r%d" % i, [128, 1024], BF16) for i in range(2)]
        sg_r = [sb("sg_r%d" % i, [128, 1024], BF16) for i in range(2)]
        gnst = sb("gnst", [128, 24], F32)
        gnmv = sb("gnmv", [128, 8], F32)
        gnve = sb("gnve", [128, 4], F32)
        gnrs = sb("gnrs", [128, 4], F32)
        gnnm = sb("gnnm", [128, 4], F32)
        o_n = sb("o_n", [128, 1024], BF16)
        o_g = sb("o_g", [128, 1024], BF16)
        o_gT = sb("o_gT", [128, 8, 128], BF16)
        mixr1 = sb("mixr", [128, 1024], BF16)
        mixr = [mixr1, mixr1]
        KT = [sb("KT%d" % i, [128, 2, 128], BF16) for i in range(4)]
        VA = [sb("VA%d" % i, [128, 2, 130], BF16) for i in range(4)]
        KTm = sb("KTm", [128, 2, 128], BF16)
        Vm = sb("Vm", [128, 2, 130], BF16)
        PTb = [[kdf[0], kdf[1], kdb[0]], [kdb[1], qT[0], qT[1]]]
        PTbR = [["kdf0", "kdf1", "kdb0"], ["kdb1", "qT0", "qT1"]]
        PTm = [[kT1, qdf1], [qdb1, sd1]]
        PTmR = [["kT", "qdf"], ["qdb", "sd"]]
        th_t = QDF
        rl = sb("rl", [128, 4], F32)
        mixa = S_f
        mixb = o_n
        mixT = S_fbf[0][:].rearrange("p (a b) -> p a b", a=8, b=128)
        tfin = S_b

        rtab_t = rotA[0]
        RES["rtab_t"] = RES["rotA0"]
        ctab_t = u0[:, 256:512]
        tmpA = u0[:, 0:256]
        tmp1 = u0[:, 512:640]
        tmp2 = u0[:, 640:768]
        for nm_ in ("ctab_t", "tmpA", "tmp1", "tmp2"):
            RES[nm_] = RES["u0"]
        print("sbuf remaining", nc.sbuf_bytes_remaining() if callable(getattr(nc, 'sbuf_bytes_remaining', None)) else nc.sbuf_bytes_remaining, flush=True)
        PA = ps("PA", [128, 1024], F32)
        PTr = ps("PTr", [128, 2048], BF16)
        PS = ps("PS", [128, 1024], F32)
        PO = ps("PO", [128, 1024], F32)
        for nm in ["PA0", "PA1", "PT0", "PT1", "PS0", "PS1", "PO0", "PO1", "sbst_d", "mixr_d"]:
            R(nm)
        for nm in ["PA0", "PA1", "PT0", "PT1", "PS0", "PS1", "PO0", "PO1"]:
            R(nm).excl = True
        PTf = PTr.bitcast(F32)
        PA_all = {
            'state': [(PA[:, 0:512], R("PA0")), (PA[:, 512:1024], R("PA1"))],
            'main': [(PA[:, 0:512], R("PA0")), (PA[:, 512:1024], R("PA1"))],
        }
        PT_all = {
            'state': [(PTr[:, 0:1024], R("PT0")), (PTr[:, 1024:2048], R("PT1"))],
            'main': [(PTr[:, 0:1024], R("PT0")), (PTr[:, 1024:2048], R("PT1"))],
        }
        mode = ['state']
        cnt = dict(pa=0, pt=0, x=0, xa=0, par=0)

        def nxt(key, mod):
            v = cnt[key]
            cnt[key] = v + 1
            return v % mod

        def ld(dst, src, rname, q='sp', key='T:setup'):
            S.add(q, I('dma_start', out=dst, in_=src), writes=[R(rname)], dma=key)

        ld(dec_t[:], dec8, "dec_t")
        ld(sink_t[:], sink8, "sink_t")
        ld(rtab_t[:], rtab, "rtab_t")
        ld(ctab_t, ctab, "ctab_t")
        ld(jtab_t[:], jtab, "jtab_t")
        ld(act_t[:], acttab, "act_t")
        ld(ident[:], identd, "ident_t")
        ld(mask_t[:], masks, "mask_t")
        ld(prew_t[:], prew, "prew_t")
        ld(nw_t[:], nwb, "nw_t")

        def wload(dst_tile, rname, src, c0, n, d0, key):
            rn = rname if isinstance(rname, (list, tuple)) else [rname]
            for kt in range(8):
                S.add('pool', I('dma_start', out=dst_tile[:, kt, d0:d0 + n], in_=src[kt * 128:(kt + 1) * 128, c0:c0 + n]),
                      writes=[R(r_) for r_ in rn], dma=key)

        wload(WW, "WW", w_in, 512, 512, 512, 'T:w1')
        wload(WW, "WW", w_in, 1024, 1024, 1024, 'T:w1')

        S.add('pool', I('memset', mhalf[:], -0.5), writes=[R("mhalf")])
        S.add('pool', I('memset', zrow[:], 0.0), writes=[R("zrow")])
        S.add('pool', I('memset', S_f[:], 0.0), writes=[R("S_f")])
        S.add('pool', I('memset', S_b[:], 0.0), writes=[R("S_b")])
        for i in range(4):
            S.add('pool', I('memset', VA[i][:], 1.0), writes=[R("VA%d" % i)])
        S.add('pool', I('memset', Vm[:], 0.0), writes=[R("Vm")])
        S.add('pool', I('memset', Vm[0:16, :, 128:129], 1.0), writes=[R("Vm")])
        S.add('pool', I('memset', Vm[32:33, :, 128:129], 1.0), writes=[R("Vm")])
        S.add('act', I('activation', out=e8[:], in_=dec_t[:], func=AF.Exp, scale=-1.0), reads=[R("dec_t")], writes=[R("e8")])
        S.add('act', I('activation', out=e8[:], in_=e8[:], func=AF.Ln, bias=1.0), reads=[R("e8")], writes=[R("e8")])
        S.add('dve', I('tensor_scalar', out=lg8[:], in0=e8[:], scalar1=-1.0, scalar2=None, op0=ALU.mult), reads=[R("e8")], writes=[R("lg8")])
        S.add('dve', I('tensor_scalar', out=lg128[:], in0=e8[:], scalar1=-128.0, scalar2=None, op0=ALU.mult), reads=[R("e8")], writes=[R("lg128")])
        for h in range(4):
            hs = slice(h * 128, (h + 1) * 128)
            S.add('act', I('activation', out=tmp1, in_=rtab_t[:, 0:128], func=AF.Exp, scale=lg8[:, h:h + 1]),
                  reads=[R("rtab_t"), R("lg8")], writes=[R("tmp1")])
            S.add('act', I('activation', out=tmp2, in_=rtab_t[:, 256:384], func=AF.Exp, scale=lg8[:, 4 + h:5 + h]),
                  reads=[R("rtab_t"), R("lg8")], writes=[R("tmp2")])
            S.add('dve', I('tensor_tensor', out=tmp1, in0=tmp1, in1=rtab_t[:, 128:256], op=ALU.mult), reads=[R("tmp1"), R("rtab_t")], writes=[R("tmp1")])
            S.add('dve', I('tensor_tensor', out=tmp2, in0=tmp2, in1=rtab_t[:, 384:512], op=ALU.mult), reads=[R("tmp2"), R("rtab_t")], writes=[R("tmp2")])
            S.add('dve', I('tensor_tensor', out=DT[:, hs], in0=tmp1, in1=tmp2, op=ALU.add), reads=[R("tmp1"), R("tmp2")], writes=[R("DT")])
            S.add('act', I('activation', out=QDF[:, hs], in_=ctab_t[:, 0:128], func=AF.Exp, scale=lg8[:, h:h + 1]),
                  reads=[R("ctab_t"), R("lg8")], writes=[R("QDF")])
            S.add('act', I('activation', out=QDB[:, hs], in_=ctab_t[:, 128:256], func=AF.Exp, scale=lg8[:, 4 + h:5 + h]),
                  reads=[R("ctab_t"), R("lg8")], writes=[R("QDB")])
            S.add('act', I('activation', out=KD8[:, h:h + 1], in_=jtab_t[:, 0:1], func=AF.Exp, scale=lg8[:, h:h + 1], bias=LN_SCALE),
                  reads=[R("jtab_t"), R("lg8")], writes=[R("KD8")])
            S.add('act', I('activation', out=KD8[:, 4 + h:5 + h], in_=jtab_t[:, 1:2], func=AF.Exp, scale=lg8[:, 4 + h:5 + h], bias=LN_SCALE),
                  reads=[R("jtab_t"), R("lg8")], writes=[R("KD8")])
        S.add('act', I('activation', out=CD8[:], in_=lg128[:], func=AF.Exp), reads=[R("lg128")], writes=[R("CD8")])

        def bc_slot(tile, off):
            return bass.AP(tile, off, [[tile.shape[1], 128], [0, NSLOT], [1, 4]])

        def bc_head(tile, off):
            return bass.AP(tile, off, [[tile.shape[1], 128], [1, NSLOT], [0, 4]])

        def v3(tile):
            ap_ = tile if isinstance(tile, bass.AP) else tile[:]
            return ap_.rearrange("p (s h) -> p s h", s=NSLOT, h=4)

        S.add('dve', I('tensor_tensor', out=v3(KDSf), in0=bc_slot(KD8, 0), in1=bc_head(act_t, 0), op=ALU.mult),
              reads=[R("KD8"), R("act_t")], writes=[R("KDSf")])
        S.add('dve', I('tensor_tensor', out=v3(KDSb), in0=bc_slot(KD8, 4), in1=bc_head(act_t, 64), op=ALU.mult),
              reads=[R("KD8"), R("act_t")], writes=[R("KDSb")])
        S.add('dve', I('tensor_tensor', out=v3(tmpA), in0=bc_slot(lg128, 0), in1=bc_head(act_t, 0), op=ALU.mult),
              reads=[R("lg128"), R("act_t")], writes=[R("tmpA")])
        S.add('act', I('activation', out=AFt[:], in_=tmpA, func=AF.Exp), reads=[R("tmpA")], writes=[R("AFt")])
        S.add('dve', I('tensor_tensor', out=v3(tmpA), in0=bc_slot(lg128, 4), in1=bc_head(act_t, 64), op=ALU.mult),
              reads=[R("lg128"), R("act_t"), R("AFt")], writes=[R("tmpA")])
        S.add('act', I('activation', out=ABt[:], in_=tmpA, func=AF.Exp), reads=[R("tmpA")], writes=[R("ABt")])
        deferred = []

        def wload_def(dst_tile, rname, src, c0, n, d0, key):
            for kt in range(8):
                deferred.append(lambda kt=kt: S.add('pool', I('dma_start', out=dst_tile[:, kt, d0:d0 + n], in_=src[kt * 128:(kt + 1) * 128, c0:c0 + n]),
                                                   writes=[R(rname)], dma=key))

        wload_def(WW, "WWq", w_in, 0, 512, 0, 'T:w2')
        wload_def(WW, "WWq", w_in, 2048, 1024, 2048, 'T:w2')
        wload_def(WW, "WWq", w_in, 5632, 1024, 3072, 'T:w2')
        wload_def(W2, "W2", w_rb, 0, 1024, 0, 'T:w2')
        wload_def(W3, "W3", w_o, 0, 1024, 0, 'T:w3')

        def precast_def(dst, rname, src, c0, n, d0, key):
            for kt in range(8):
                deferred.append(lambda kt=kt: S.add('pool', I('dma_start', out=dst[kt * 128:(kt + 1) * 128, d0:d0 + n], in_=src[kt * 128:(kt + 1) * 128, c0:c0 + n]),
                                                   writes=[R(rname)], dma=key))

        precast_def(wbf, "wbf_kv", w_in, 4096, 512, 1024, 'T:pc1')
        precast_def(wbf, "wbf_q", w_in, 3072, 1024, 0, 'T:pc2')
        precast_def(wbf, "wbf_g", w_in, 4608, 1024, 1536, 'T:pc3')
        precast_def(wbf, "wbf_s", w_in, 6656, 1024, 2560, 'T:pc4')
        precast_def(wabf, "wabf", w_ab, 0, 1024, 0, 'T:pc5')
        if not DEFER_W:
            while deferred:
                deferred.pop(0)()

        def interleave(tail, filler, k=1, lead=0):
            t_done = tail is None
            f_done = filler is None

            def step(gen):
                try:
                    next(gen)
                    return False
                except StopIteration:
                    return True
            for _ in range(lead):
                if not f_done:
                    f_done = step(filler)
            while not (t_done and f_done):
                if not t_done:
                    t_done = step(tail)
                for _ in range(k):
                    if not f_done:
                        f_done = step(filler)

        def front_a(xsrc, cssrc, ring=3):
            xi = nxt('x' if ring == 3 else 'xa', ring)
            par = nxt('par', 2)
            xb, cs = XB[xi], CS[xi]
            S.add('sp', I('dma_start', out=xb[:], in_=xsrc), writes=[R("XB%d" % xi)] + ([R("sbstg0"), R("sbstg1")] if xi == 3 else []), dma='x%d' % xi)
            S.add('sp', I('dma_start', out=cs[:], in_=cssrc), writes=[R("CS%d" % xi)], dma='c%d' % xi)
            S.add('act', I('activation', out=ub[par][:], in_=xb[:], func=AF.Square, accum_out=ssq[par][:]),
                  reads=[R("XB%d" % xi)], writes=[R("ub%d" % par), R("ssq%d" % par)])
            S.add('dve', I('tensor_scalar', out=vv[par][:], in0=ssq[par][:], scalar1=1.0 / D_MODEL, scalar2=EPS, op0=ALU.mult, op1=ALU.add),
                  reads=[R("ssq%d" % par)], writes=[R("vv%d" % par)])
            S.add('pool', I('tensor_tensor', out=rstd[par][:], in0=vv[par][:], in1=mhalf[:, 0:1], op=ALU.pow),
                  reads=[R("vv%d" % par), R("mhalf")], writes=[R("rstd%d" % par)])
            S.add('pool', I('tensor_tensor', out=u0[:], in0=xb[:], in1=prew_t[:], op=ALU.mult),
                  reads=[R("XB%d" % xi), R("prew_t")], writes=[R("u0")])
            S.add('act', I('activation', out=ub[par][:], in_=u0[:], func=AF.Copy, scale=rstd[par][:]),
                  reads=[R("u0"), R("rstd%d" % par)], writes=[R("ub%d" % par)])
            return dict(xi=xi, par=par)

        def front_b(ctx):
            par = ctx['par']
            tr_to(ub[par], "ub%d" % par, 8, uT[par][:].rearrange("p a b -> p (a b)"), "uT%d" % par)

        def tr_to(src_tile, src_res, ntile, dst_ap, dst_res, scale=None, src_off=0):
            pts = PT_all[mode[0]]
            pt, ptr = pts[nxt('pt', len(pts))]
            for t in range(ntile):
                S.add('pe', I('transpose', out=pt[:, t * 128:(t + 1) * 128], in_=src_tile[:, src_off + t * 128:src_off + (t + 1) * 128], identity=ident[:]),
                      reads=[R(src_res), R("ident_t")], writes=[ptr])
            if scale is None:
                S.add('act', I('activation', out=dst_ap, in_=pt[:, 0:ntile * 128], func=AF.Copy), reads=[ptr], writes=[R(dst_res)])
            else:
                S.add('act', I('activation', out=dst_ap, in_=pt[:, 0:ntile * 128], func=AF.Copy, scale=scale), reads=[ptr], writes=[R(dst_res)])

        def proj(par, wt, wres, c0, n=512):
            pas = PA_all[mode[0]]
            bank, bres = pas[nxt('pa', len(pas))]
            for kt in range(8):
                S.add('pe', I('matmul', out=bank[:, 0:n], lhsT=uT[par][:, kt, :], rhs=wt[:, kt, c0:c0 + n], start=(kt == 0), stop=(kt == 7)),
                      reads=[R("uT%d" % par)] + [R(r_) for r_ in (wres if isinstance(wres, (list, tuple)) else [wres])], writes=[bres])
            return bank[:, 0:n], bres

        def rotary(zb, zres, cs, csres, nh, dst_ap, dst_res, ri):
            n = nh * 128
            zv = zb.rearrange("p (h t f) -> p h t f", h=nh, t=2, f=64)
            A = rotA[ri][:, 0:n].rearrange("p (h t f) -> p h t f", h=nh, t=2, f=64)
            B = rotB[ri][:, 0:n].rearrange("p (h t f) -> p h t f", h=nh, t=2, f=64)
            dv = dst_ap.rearrange("p (h t f) -> p h t f", h=nh, t=2, f=64)
            cosb = bass.AP(cs, 0, [[128, 128], [0, nh], [0, 2], [1, 64]])
            sinb = bass.AP(cs, 64, [[128, 128], [0, nh], [1, 64]])
            rA, rB = R("rotA%d" % ri), R("rotB%d" % ri)
            S.add('dve', I('tensor_tensor', out=A, in0=zv, in1=cosb, op=ALU.mult), reads=[zres, R(csres)], writes=[rA])
            S.add('dve', I('tensor_tensor', out=B[:, :, 0, :], in0=zv[:, :, 1, :], in1=sinb, op=ALU.mult), reads=[zres, R(csres)], writes=[rB])
            S.add('dve', I('tensor_tensor', out=B[:, :, 1, :], in0=zv[:, :, 0, :], in1=sinb, op=ALU.mult), reads=[zres, R(csres)], writes=[rB])
            S.add('pool', I('tensor_tensor', out=dv[:, :, 0, :], in0=A[:, :, 0, :], in1=B[:, :, 0, :], op=ALU.subtract), reads=[rA, rB], writes=[R(dst_res)])
            S.add('pool', I('tensor_tensor', out=dv[:, :, 1, :], in0=A[:, :, 1, :], in1=B[:, :, 1, :], op=ALU.add), reads=[rA, rB], writes=[R(dst_res)])

        def bc_d(tile, off):
            return bass.AP(tile, off, [[tile.shape[1], 128], [1, 4], [0, 128]])

        def h3(ap):
            return ap.rearrange("p (h d) -> p h d", h=4, d=128)

        def kv_update(par, kd_tile, kd_res, bank_tile, bres, Stile, Sres, coef_tile, coef_off):
            for h in range(4):
                S.add('pe', I('matmul', out=bank_tile[:, h * 256:(h + 1) * 256], lhsT=kd_tile[:, h * 128:(h + 1) * 128],
                              rhs=v_tok[par][:, h * 256:(h + 1) * 256], start=True, stop=True),
                      reads=[R(kd_res), R("v_tok%d" % par)], writes=[bres[h // 2]])
            for h in range(4):
                S.add('dve', I('scalar_tensor_tensor',
                               out=Stile[:, h * 256:(h + 1) * 256], in0=Stile[:, h * 256:(h + 1) * 256], scalar=coef_tile[:, coef_off + h:coef_off + h + 1],
                               in1=bank_tile[:, h * 256:(h + 1) * 256], op0=ALU.mult, op1=ALU.add),
                      reads=[R(Sres), bres[h // 2], R(coef_tile.name)], writes=[R(Sres)])

        PSr = [R("PS0"), R("PS1")]
        POr = [R("PO0"), R("PO1")]
        final_dma = []

        nstg = [0]

        def st_head(t, ctx):
            xi, par = ctx['xi'], ctx['par']
            zk, zkr = proj(par, WW, "WW", 512)
            rotary(zk, zkr, CS[xi], "CS%d" % xi, 4, k_rot[par][:], "k_rot%d" % par, par)
            yield
            for hb in range(2):
                zv_, zvr = proj(par, WW, "WW", 1024 + hb * 512)
                S.add('act', I('activation', out=v_tok[par][:, hb * 512:(hb + 1) * 512], in_=zv_, func=AF.Copy),
                      reads=[zvr], writes=[R("v_tok%d" % par)])
                if hb == 0:
                    S.add('dve', I('tensor_tensor', out=h3(kdf[par][:]), in0=h3(k_rot[par][:]), in1=bc_d(KDSf, t * 4), op=ALU.mult),
                          reads=[R("k_rot%d" % par), R("KDSf")], writes=[R("kdf%d" % par)])
                    S.add('pool', I('tensor_tensor', out=h3(kdb[par][:]), in0=h3(k_rot[par][:]), in1=bc_d(KDSb, t * 4), op=ALU.mult),
                          reads=[R("k_rot%d" % par), R("KDSb")], writes=[R("kdb%d" % par)])
                yield

        def st_tail(t, ctx):
            par = ctx['par']
            if t >= 49:
                c = NSLOT - t
                sg = sbstg[nstg[0] % 2]
                sgr = "sbstg%d" % (nstg[0] % 2)
                nstg[0] += 1
                S.add('act', I('activation', out=sg[:], in_=S_b[:], func=AF.Copy), reads=[R("S_b")], writes=[R(sgr)])
                S.add('sp', I('dma_start', out=sbst[c * 128:(c + 1) * 128, :], in_=sg[:]), reads=[R(sgr)], writes=[R("sbst_d")], dma=sgr)
            kv_update(par, kdf[par], "kdf%d" % par, PS, PSr, S_f, "S_f", AFt, t * 4)
            yield
            kv_update(par, kdb[par], "kdb%d" % par, PO, POr, S_b, "S_b", ABt, t * 4)
            yield

        ctxs = {0: front_a(xs[0:128, :], css[0:128, :])}
        front_b(ctxs[0])
        prev_tail = None
        for t in range(NSLOT):
            if t + 1 < NSLOT:
                ctxs[t + 1] = front_a(xs[(t + 1) * 128:(t + 2) * 128, :], css[(t + 1) * 128:(t + 2) * 128, :])
            for _ in range(2):
                if deferred and DEFER_W:
                    deferred.pop(0)()
            interleave(prev_tail, st_head(t, ctxs[t]), k=2)
            if t + 1 < NSLOT:
                front_b(ctxs[t + 1])
            prev_tail = st_tail(t, ctxs[t])
        interleave(prev_tail, None)
        while deferred:
            deferred.pop(0)()
        sg = sbstg[nstg[0] % 2]
        sgr = "sbstg%d" % (nstg[0] % 2)
        S.add('act', I('activation', out=sg[:], in_=S_b[:], func=AF.Copy), reads=[R("S_b")], writes=[R(sgr)])
        S.add('sp', I('dma_start', out=sbst[0:128, :], in_=sg[:]), reads=[R(sgr)], writes=[R("sbst_d")], dma=sgr)
        S.add('act', I('activation', out=S_fbf[0][:], in_=S_f[:], func=AF.Copy), reads=[R("S_f")], writes=[R("S_fbf0")])
        if debug:
            final_dma.append(S.add('sp', I('dma_start', out=dbg['sf'], in_=S_f[:]), reads=[R("S_f")], dma='dbgsf'))
            final_dma.append(S.add('sp', I('dma_start', out=dbg['sb'], in_=S_b[:]), reads=[R("S_b")], dma='dbgsb'))

        if stop_after == 'state':
            S.add('sp', None, extra=final_dma)
            S.build()
            return nc
        mode[0] = 'main'

        def r_front(c):
            return front_a(xm[(2 + c) * 128:(3 + c) * 128, :], csm[(2 + c) * 128:(3 + c) * 128, :])

        def load_sb(c):
            sbc = sbstg[c % 2]
            sbr = "sbstg%d" % (c % 2)
            S.add('sp', I('dma_start', out=sbc[:], in_=sbst[c * 128:(c + 1) * 128, :]), reads=[R("sbst_d")], writes=[R(sbr)], dma=sbr)

        def r_head(c, ctx):
            xi, par = ctx['xi'], ctx['par']
            zq, zqr = proj(par, WW, "WWq", 0)
            rotary(zq, zqr, CS[xi], "CS%d" % xi, 4, q_rot[par][:, 0:512], "q_rot", 0)
            yield
            zk, zkr = proj(par, WW, "WW", 512)
            rotary(zk, zkr, CS[xi], "CS%d" % xi, 4, k_rot[par][:], "k_rot%d" % par, 1)
            tr_to(q_rot[par], "q_rot", 4, qT[par][:], "qT%d" % par)
            yield
            for hb in range(2):
                zv_, zvr = proj(par, WW, "WW", 1024 + hb * 512)
                S.add('act', I('activation', out=v_tok[par][:, hb * 512:(hb + 1) * 512], in_=zv_, func=AF.Copy),
                      reads=[zvr], writes=[R("v_tok%d" % par)])
                if hb == 0:
                    tr_to(k_rot[par], "k_rot%d" % par, 4, kT[par][:], kTR[par])
                    S.add('dve', I('tensor_tensor', out=h3(kdf[par][:]), in0=h3(k_rot[par][:]), in1=bc_d(KD8, 0), op=ALU.mult),
                          reads=[R("k_rot%d" % par), R("KD8")], writes=[R("kdf%d" % par)])
                yield
            for hb in range(2):
                zg, zgr = proj(par, WW, "WWq", 2048 + hb * 512)
                S.add('act', I('activation', out=g_r[par][:, hb * 512:(hb + 1) * 512], in_=zg, func=AF.Silu),
                      reads=[zgr], writes=[R("g_r%d" % par)])
                yield
            S.add('pool', I('tensor_tensor', out=g_r[par][:], in0=g_r[par][:], in1=nw_t[:], op=ALU.mult),
                  reads=[R("g_r%d" % par), R("nw_t")], writes=[R("g_r%d" % par)])
            for hb in range(2):
                zg, zgr = proj(par, WW, "WWq", 3072 + hb * 512)
                S.add('act', I('activation', out=sg_r[par][:, hb * 512:(hb + 1) * 512], in_=zg, func=AF.Tanh, scale=0.5),
                      reads=[zgr], writes=[R("sg_r%d" % par)])
                yield

        def r_tail_a(c, ctx):
            par = ctx['par']
            sbc = sbstg[c % 2]
            sbr = "sbstg%d" % (c % 2)
            for h in range(4):
                hs = slice(h * 128, (h + 1) * 128)
                S.add('pe', I('matmul', out=PS[:, hs], lhsT=kT[par][:, hs], rhs=qT[par][:, hs], start=True, stop=True),
                      reads=[R(kTR[par]), R("qT%d" % par)], writes=[PSr[0]])
            S.add('dve', I('tensor_tensor', out=sd[par][:], in0=PS[:, 0:512], in1=DT[:], op=ALU.mult), reads=[PSr[0], R("DT")], writes=[R("sd")])
            S.add('dve', I('tensor_tensor', out=qdf[par][:], in0=qT[par][:], in1=QDF[:], op=ALU.mult), reads=[R("qT%d" % par), R("QDF")], writes=[R("qdf")])
            S.add('pool', I('tensor_tensor', out=qdb[par][:], in0=qT[par][:], in1=QDB[:], op=ALU.mult), reads=[R("qT%d" % par), R("QDB")], writes=[R("qdb")])
            yield
            sfb = S_fbf[0]
            sfr = "S_fbf0"
            for h in range(4):
                hs = slice(h * 128, (h + 1) * 128)
                vs = slice(h * 256, (h + 1) * 256)
                S.add('pe', I('matmul', out=PO[:, vs], lhsT=qdf[par][:, hs], rhs=sfb[:, vs], start=True, stop=False),
                      reads=[R("qdf"), R(sfr)], writes=[POr[h // 2]])
                S.add('pe', I('matmul', out=PO[:, vs], lhsT=qdb[par][:, hs], rhs=sbc[:, vs], start=False, stop=False),
                      reads=[R("qdb"), R(sbr)], writes=[POr[h // 2]])
                S.add('pe', I('matmul', out=PO[:, vs], lhsT=sd[par][:, hs], rhs=v_tok[par][:, vs], start=False, stop=True),
                      reads=[R("sd"), R("v_tok%d" % par)], writes=[POr[h // 2]])
            if c < NOWN - 1:
                kv_update(par, kdf[par], "kdf%d" % par, PS, PSr, S_f, "S_f", CD8, 0)
                S.add('dve', I('tensor_copy', out=S_fbf[0][:], in_=S_f[:]), reads=[R("S_f")], writes=[R("S_fbf0")])
            for h in range(4):
                S.add('dve', I('bn_stats', out=gnst[:, h * 6:(h + 1) * 6], in_=PO[:, h * 256:(h + 1) * 256]), reads=[POr[h // 2]], writes=[R("gnst")])
            for h in range(4):
                S.add('dve', I('bn_aggr', out=gnmv[:, h * 2:(h + 1) * 2], in_=gnst[:, h * 6:(h + 1) * 6]), reads=[R("gnst")], writes=[R("gnmv")])
            mvv = gnmv[:].rearrange("p (h t) -> p h t", h=4, t=2)
            S.add('dve', I('tensor_scalar', out=gnve[:], in0=mvv[:, :, 1], scalar1=EPS, scalar2=None, op0=ALU.add), reads=[R("gnmv")], writes=[R("gnve")])
            S.add('pool', I('tensor_tensor', out=gnrs[:], in0=gnve[:], in1=mhalf[:, 0:4], op=ALU.pow), reads=[R("gnve"), R("mhalf")], writes=[R("gnrs")])
            S.add('dve', I('scalar_tensor_tensor', out=gnnm[:], in0=mvv[:, :, 0], scalar=-1.0, in1=gnrs[:], op0=ALU.mult, op1=ALU.mult),
                  reads=[R("gnmv"), R("gnrs")], writes=[R("gnnm")])
            for h in range(4):
                vs = slice(h * 256, (h + 1) * 256)
                S.add('dve', I('tensor_scalar', out=o_n[:, vs], in0=PO[:, vs], scalar1=gnrs[:, h:h + 1], scalar2=gnnm[:, h:h + 1], op0=ALU.mult, op1=ALU.add),
                      reads=[POr[h // 2], R("gnrs"), R("gnnm")], writes=[R("o_n")])
            S.add('dve', I('tensor_tensor', out=o_g[:], in0=o_n[:], in1=g_r[par][:], op=ALU.mult), reads=[R("o_n"), R("g_r%d" % par)], writes=[R("o_g")])
            yield

        def r_tail_b(c, ctx):
            par = ctx['par']
            tr_to(o_g, "o_g", 8, o_gT[:].rearrange("p a b -> p (a b)"), "o_gT")
            yield
            for nb_ in range(2):
                for kt in range(8):
                    S.add('pe', I('matmul', out=PO[:, nb_ * 512:(nb_ + 1) * 512], lhsT=o_gT[:, kt, :], rhs=W2[:, kt, nb_ * 512:(nb_ + 1) * 512],
                                  start=(kt == 0), stop=(kt == 7)),
                          reads=[R("o_gT"), R("W2")], writes=[POr[nb_]])
            mr = mixr[0]
            mrr = "mixr"
            for nb_ in range(2):
                S.add('dve', I('scalar_tensor_tensor', out=mr[:, nb_ * 512:(nb_ + 1) * 512], in0=sg_r[par][:, nb_ * 512:(nb_ + 1) * 512], scalar=1.0,
                               in1=PO[:, nb_ * 512:(nb_ + 1) * 512], op0=ALU.add, op1=ALU.mult),
                      reads=[R("sg_r%d" % par), POr[nb_]], writes=[R(mrr)])
            S.add('sp', I('dma_start', out=mixrd[c * 128:(c + 1) * 128, :], in_=mr[:]), reads=[R(mrr)], writes=[R("mixr_d")], dma=mrr)
            if debug:
                final_dma.append(S.add('sp', I('dma_start', out=dbg['mixr'][c * 128:(c + 1) * 128, :], in_=mr[:]), reads=[R(mrr)], dma='dbgm'))
            yield

        ctxs = {0: r_front(0)}
        front_b(ctxs[0])
        def chain(*gens):
            for g_ in gens:
                if g_ is not None:
                    yield from g_

        def alt(ga, gb):
            live = [g_ for g_ in (ga, gb) if g_ is not None]
            while live:
                for g_ in list(live):
                    try:
                        next(g_)
                        yield
                    except StopIteration:
                        live.remove(g_)

        ctxs[1] = r_front(1)
        for c in range(NOWN + 2):
            if c < NOWN:
                load_sb(c)
            if c + 2 < NOWN:
                ctxs[c + 2] = r_front(c + 2)
            if c + 1 < NOWN:
                front_b(ctxs[c + 1])
            if c == NOWN:
                ld(nw_t[:], postw, "nw_t", key='pw')
                def reload(rn, sres, d0, n, key):
                    S.add('sp', I('dma_start', out=WW[:, :, d0:d0 + n], in_=wbf[:, d0:d0 + n].rearrange("(kt p) c -> p kt c", p=128)),
                          reads=[R(sres)], writes=[R("WW"), R("WWq"), R(rn)], dma=key)
                reload("WAkv", "wbf_kv", 1024, 512, 'w4a')
                reload("WAq", "wbf_q", 0, 1024, 'w4b')
                reload("WAg", "wbf_g", 1536, 1024, 'w4c')
                reload("WAs", "wbf_s", 2560, 1024, 'w4d')
            tb = r_tail_b(c - 2, ctxs[c - 2]) if 0 <= c - 2 < NOWN else None
            ta = r_tail_a(c - 1, ctxs[c - 1]) if 0 <= c - 1 < NOWN else None
            hd = r_head(c, ctxs[c]) if c < NOWN else None
            interleave(alt(tb, ta), hd, k=2, lead=1)

        if stop_after == 'R':
            S.add('sp', None, extra=final_dma)
            S.build()
            return nc
        S.add('sp', I('dma_start', out=W2[:], in_=wabf.rearrange("(kt p) c -> p kt c", p=128)), reads=[R("wabf")], writes=[R("W2")], dma='w6')
        while deferred:
            deferred.pop(0)()

        for p in range(2):
            for g in range(2):
                S.add('pool', I('memset', PTm[p][g][:], 0.0), writes=[R(PTmR[p][g])])
        for p in range(2):
            for g in range(2):
                for hh in range(4):
                    h = g * 4 + hh
                    S.add('act', I('activation', out=PTm[p][g][32:33, hh * 128:(hh + 1) * 128], in_=zrow[32:33, :], func=AF.Exp,
                                   bias=sink_t[32:33, h:h + 1], scale=1.0),
                          reads=[R("zrow"), R("sink_t")], writes=[R(PTmR[p][g])])

        def a_head(idx, ctx, nctx=None):
            xi, par = ctx['xi'], ctx['par']
            zkv, zkvr = proj(par, WW, "WAkv", 1024)
            if idx == 0:
                ktile, kres, vtile, vres = KTm, "KTm", Vm, "Vm"
            else:
                ktile, kres, vtile, vres = KT[idx % 4], "KT%d" % (idx % 4), VA[idx % 4], "VA%d" % (idx % 4)
            if idx == 0:
                S.add('act', I('activation', out=Vm[0:16, :, 0:128], in_=zkv[0:16, 256:512].rearrange("p (g d) -> p g d", g=2, d=128), func=AF.Copy),
                      reads=[zkvr], writes=[R("Vm")])
            else:
                S.add('act', I('activation', out=vtile[:, :, 0:128], in_=zkv[:, 256:512].rearrange("p (g d) -> p g d", g=2, d=128), func=AF.Copy),
                      reads=[zkvr], writes=[R(vres)])
            rotary(zkv[:, 0:256], zkvr, CS[xi], "CS%d" % xi, 2, k_rot[par][:, 0:256], "k_rot%d" % par, 0)
            yield
            if 2 <= idx <= 17:
                c = idx - 2
                pa_c = c % 2
                own_info[c] = (xi, pa_c)
                for hb in range(2):
                    zq, zqr = proj(par, WW, "WAq", hb * 512)
                    rotary(zq, zqr, CS[xi], "CS%d" % xi, 4, q_rot[par][:, hb * 512:(hb + 1) * 512], "q_rot", 1)
                    if hb == 0:
                        tr_to(k_rot[par], "k_rot%d" % par, 2, ktile[:].rearrange("p a b -> p (a b)"), kres)
                    yield
                for hb in range(2):
                    zg, zgr = proj(par, WW, "WAg", 1536 + hb * 512)
                    S.add('act', I('activation', out=th_t[:], in_=zg, func=AF.Tanh, scale=0.5), reads=[zgr], writes=[R("QDF")])
                    S.add('dve', I('scalar_tensor_tensor', out=g_r[pa_c][:, hb * 512:(hb + 1) * 512], in0=th_t[:], scalar=1.0, in1=zg, op0=ALU.add, op1=ALU.mult),
                          reads=[R("QDF"), zgr], writes=[R("g_r%d" % pa_c)])
                    if hb == 0:
                        tr_to(q_rot[par], "q_rot", 8, v_tok[pa_c][:], "v_tok%d" % pa_c)
                    yield
                for hb in range(2):
                    zg, zgr = proj(par, WW, "WAs", 2560 + hb * 512)
                    S.add('act', I('activation', out=sg_r[pa_c][:, hb * 512:(hb + 1) * 512], in_=zg, func=AF.Tanh, scale=0.5),
                          reads=[zgr], writes=[R("sg_r%d" % pa_c)])
                    yield
            else:
                tr_to(k_rot[par], "k_rot%d" % par, 2, ktile[:].rearrange("p a b -> p (a b)"), kres)
                yield
            if nctx is not None:
                front_b(nctx)
                yield

        def a_post(c, xi_c):
            q = c % 2
            S.add('act', I('activation', out=o_n[:], in_=PO[:, :], func=AF.Square, accum_out=ssq[q][:]), reads=POr, writes=[R("o_n"), R("ssq%d" % q)])
            S.add('dve', I('tensor_scalar', out=vv[q][:], in0=ssq[q][:], scalar1=1.0 / D_MODEL, scalar2=EPS, op0=ALU.mult, op1=ALU.add),
                  reads=[R("ssq%d" % q)], writes=[R("vv%d" % q)])
            S.add('pool', I('tensor_tensor', out=rstd[q][:], in0=vv[q][:], in1=mhalf[:, 0:1], op=ALU.pow), reads=[R("vv%d" % q), R("mhalf")], writes=[R("rstd%d" % q)])
            for nb_ in range(2):
                cs_ = slice(nb_ * 512, (nb_ + 1) * 512)
                S.add('dve', I('scalar_tensor_tensor', out=tfin[:, cs_], in0=PO[:, cs_], scalar=rstd[q][:], in1=nw_t[:, cs_], op0=ALU.mult, op1=ALU.mult),
                      reads=[POr[nb_], R("rstd%d" % q), R("nw_t")], writes=[R("S_b")])
            S.add('pool', I('tensor_tensor', out=tfin[:], in0=tfin[:], in1=XB[xi_c][:], op=ALU.add), reads=[R("S_b"), R("XB%d" % xi_c)], writes=[R("S_b")])
            final_dma.append(S.add('sp', I('dma_start', out=out[c * 128:(c + 1) * 128, :], in_=tfin[:]), reads=[R("S_b")], dma="resb"))

        def a_tail(c, xi_c, pa_c, post=None):
            if post is not None:
                a_post(*post)
            blocks = []
            for bi, idx in enumerate((c + 1, c + 2, c + 3)):
                if bi == 0:
                    mk = 2 if c == 0 else 0
                elif bi == 2:
                    mk = 3 if c == NOWN - 1 else 1
                else:
                    mk = None
                blocks.append((idx % 4, mk))
            pm = c % 2
            def tr_group(g):
                tr_to(o_g, "o_g", 4, o_gT[:, g * 4:(g + 1) * 4, :].rearrange("p a b -> p (a b)"), "o_gT", src_off=g * 512)

            for g in range(2):
                qg = v_tok[pa_c][:, g * 512:(g + 1) * 512]
                for bi, (sl, mk) in enumerate(blocks):
                    bank = PS[:, (bi % 2) * 512:(bi % 2 + 1) * 512]
                    br = PSr[bi % 2]
                    S.add('pe', I('matmul', out=bank, lhsT=KT[sl][:, g, :], rhs=qg, start=True, stop=(mk is None)),
                          reads=[R("KT%d" % sl), R("v_tok%d" % pa_c)], writes=[br])
                    if mk is not None:
                        S.add('pe', I('matmul', out=bank, lhsT=ident[:], rhs=mask_t[:, mk * 512:(mk + 1) * 512], start=False, stop=True),
                              reads=[R("ident_t"), R("mask_t")], writes=[br])
                    S.add('act', I('activation', out=PTb[g][bi][:], in_=bank, func=AF.Exp, scale=SCALE),
                          reads=[br], writes=[R(PTbR[g][bi])])
                bank = PS[0:16, 512:1024]
                S.add('pe', I('matmul', out=bank, lhsT=KTm[:, g, 0:16], rhs=qg, start=True, stop=True),
                      reads=[R("KTm"), R("v_tok%d" % pa_c)], writes=[PSr[1]])
                S.add('act', I('activation', out=PTm[pm][g][0:16, :], in_=bank, func=AF.Exp, scale=SCALE),
                      reads=[PSr[1]], writes=[R(PTmR[pm][g])])
                if g == 1:
                    tr_group(0)
                yield
                for hh in range(4):
                    dst = PO[:, hh * 129:(hh + 1) * 129] if hh < 3 else PO[:, 512:641]
                    dr = POr[0] if hh < 3 else POr[1]
                    for bi, (sl, mk) in enumerate(blocks):
                        S.add('pe', I('matmul', out=dst, lhsT=PTb[g][bi][:, hh * 128:(hh + 1) * 128], rhs=VA[sl][:, g, 0:129],
                                      start=(bi == 0), stop=False),
                              reads=[R(PTbR[g][bi]), R("VA%d" % sl)], writes=[dr])
                    S.add('pe', I('matmul', out=dst, lhsT=PTm[pm][g][0:33, hh * 128:(hh + 1) * 128], rhs=Vm[0:33, g, 0:129], start=False, stop=True),
                          reads=[R(PTmR[pm][g]), R("Vm")], writes=[dr])
                l3 = PO[:, 0:387].rearrange("p (h c) -> p h c", h=3, c=129)[:, :, 128]
                S.add('dve', I('reciprocal', out=rl[:, 0:3], in_=l3), reads=[POr[0]], writes=[R("rl")])
                S.add('dve', I('reciprocal', out=rl[:, 3:4], in_=PO[:, 640:641]), reads=[POr[1], R("rl")], writes=[R("rl")])
                S.add('dve', I('tensor_scalar', out=rl[:], in0=rl[:], scalar1=0.5, scalar2=None, op0=ALU.mult), reads=[R("rl")], writes=[R("rl")])
                for hh in range(4):
                    h = g * 4 + hh
                    src_ = PO[:, hh * 129:hh * 129 + 128] if hh < 3 else PO[:, 512:640]
                    dr = POr[0] if hh < 3 else POr[1]
                    S.add('dve', I('scalar_tensor_tensor', out=o_g[:, h * 128:(h + 1) * 128], in0=src_, scalar=rl[:, hh:hh + 1],
                                   in1=g_r[pa_c][:, h * 128:(h + 1) * 128], op0=ALU.mult, op1=ALU.mult),
                          reads=[dr, R("rl"), R("g_r%d" % pa_c)], writes=[R("o_g")])
                if g == 0:
                    S.add('sp', I('dma_start', out=mixr[0][:], in_=mixrd[c * 128:(c + 1) * 128, :]), reads=[R("mixr_d")], writes=[R("mixr")], dma="mixr")
                yield
            tr_group(1)
            for nb_ in range(2):
                for kt in range(8):
                    S.add('pe', I('matmul', out=PO[:, nb_ * 512:(nb_ + 1) * 512], lhsT=o_gT[:, kt, :], rhs=W2[:, kt, nb_ * 512:(nb_ + 1) * 512],
                                  start=(kt == 0), stop=(kt == 7)),
                          reads=[R("o_gT"), R("W2")], writes=[POr[nb_]])
            for nb_ in range(2):
                cs_ = slice(nb_ * 512, (nb_ + 1) * 512)
                S.add('dve', I('scalar_tensor_tensor', out=mixa[:, cs_], in0=sg_r[pa_c][:, cs_], scalar=1.0, in1=PO[:, cs_], op0=ALU.add, op1=ALU.mult),
                      reads=[R("sg_r%d" % pa_c), POr[nb_]], writes=[R("S_f")])
            S.add('pool', I('tensor_tensor', out=mixb[:], in0=mixa[:], in1=mixr[0][:], op=ALU.add), reads=[R("S_f"), R("mixr")], writes=[R("o_n")])
            yield
            tr_to(mixb, "o_n", 8, S_fbf[0][:], "S_fbf0", scale=0.5)
            yield
            for nb_ in range(2):
                for kt in range(8):
                    S.add('pe', I('matmul', out=PO[:, nb_ * 512:(nb_ + 1) * 512], lhsT=mixT[:, kt, :], rhs=W3[:, kt, nb_ * 512:(nb_ + 1) * 512],
                                  start=(kt == 0), stop=(kt == 7)),
                          reads=[R("S_fbf0"), R("W3")], writes=[POr[nb_]])
            yield

        own_info = {}
        ctxs = {0: front_a(xm[0:128, :], csm[0:128, :], ring=4)}
        front_b(ctxs[0])
        for idx in range(a_steps):
            if idx + 1 < 19:
                ctxs[idx + 1] = front_a(xm[(idx + 1) * 128:(idx + 2) * 128, :], csm[(idx + 1) * 128:(idx + 2) * 128, :], ring=4)
            tail = None
            if idx >= 3:
                c_ = idx - 3
                post = (c_ - 1, own_info[c_ - 1][0]) if c_ >= 1 else None
                tail = a_tail(c_, *own_info[c_], post=post)
            hd = a_head(idx, ctxs[idx], ctxs.get(idx + 1))
            if PIPE_A:
                interleave(tail, hd, k=1, lead=2)
            else:
                interleave(None, hd)
                interleave(tail, None)

        if a_steps == 19:
            a_post(NOWN - 1, own_info[NOWN - 1][0])
        S.add('sp', None, extra=final_dma)
        S.build()
        print("sched stats", S.stats, flush=True)
    return nc


def _host_prep(inputs):
    x = np.asarray(inputs["x"], np.float32)
    meta = np.asarray(inputs["meta_tokens"], np.float32)
    import jax
    import jax.numpy as jnp
    _cpu = jax.devices("cpu")[0]
    with jax.default_device(_cpu):
        inv_j = 10000.0 ** (-jnp.arange(64, dtype=jnp.float32) * 2.0 / 128)

    def chunk_data(b, n):
        if n == 0:
            z = np.zeros((128, 1024), np.float32)
            z[112:] = meta
            return z
        if n > 64:
            return np.zeros((128, 1024), np.float32)
        return x[b, (n - 1) * 128:n * 128]

    def cs_of(pos):
        with jax.default_device(_cpu):
            ang = jnp.asarray(np.asarray(pos, np.float32))[:, None] * inv_j[None, :]
            return np.concatenate([np.asarray(jnp.cos(ang), np.float32), np.asarray(jnp.sin(ang), np.float32)], axis=1)

    def chunk_pos(n):
        return np.maximum(n * 128 + np.arange(128) - 112, 0)

    jj = np.arange(128)[:, None].astype(np.float64)
    ii = np.arange(128)[None, :].astype(np.float64)
    sc = 128.0 ** -0.5
    rtab = np.concatenate([np.maximum(ii - jj, 0), (ii >= jj) * sc, np.maximum(jj - ii, 0), (jj > ii) * sc], axis=1).astype(np.float32)
    ctab = np.concatenate([np.broadcast_to(ii + 1, (128, 128)), np.broadcast_to(128 - ii, (128, 128))], axis=1).astype(np.float32)
    jtab = np.concatenate([127 - jj, jj], axis=1).astype(np.float32)
    bf = ml_dtypes.bfloat16
    m_prev = np.where(jj >= ii, 0.0, NEG).astype(np.float32)
    m_next = np.where(jj <= ii, 0.0, NEG).astype(np.float32)
    m_all = np.full((128, 128), NEG, np.float32)
    ident = np.eye(128, dtype=np.float32).astype(bf)

    def bc(v, n):
        return np.ascontiguousarray(np.broadcast_to(np.asarray(v, np.float32).reshape(1, n), (128, n)))

    common = dict(
        w_in=np.ascontiguousarray(inputs["w_in"][0], dtype=np.float32),
        w_rb=np.ascontiguousarray(inputs["w_ret_branch"][0], dtype=np.float32),
        w_ab=np.ascontiguousarray(inputs["w_attn_branch"][0], dtype=np.float32),
        w_o=np.ascontiguousarray(inputs["w_out"][0], dtype=np.float32),
        prew=bc(inputs["pre_norm_w"][0], 1024), postw=bc(inputs["post_norm_w"][0], 1024), nwb=bc(inputs["ret_norm_w"][0], 1024),
        dec8=bc(np.concatenate([np.asarray(inputs["ret_decay_fwd"][0]), np.asarray(inputs["ret_decay_bwd"][0])]), 8),
        sink8=bc(inputs["attn_sink"][0], 8),
        rtab=rtab, ctab=ctab, jtab=jtab, ident=ident,
    )
    in_maps = []
    for core in range(8):
        b, s = divmod(core, 4)
        fwd = list(range(0, 16 * s + 1))
        bwd = list(range(64, 16 * s + 16, -1))
        own = [16 * s + 1 + c for c in range(15, 0, -1)]
        slots = fwd + bwd + own
        assert len(slots) == NSLOT
        xs = np.concatenate([chunk_data(b, n) for n in slots], axis=0)
        css = np.concatenate([cs_of(chunk_pos(n)) for n in slots], axis=0)
        actf = np.array([1.0] * len(fwd) + [0.0] * (NSLOT - len(fwd)), np.float32)
        acttab = bc(np.concatenate([actf, 1.0 - actf]), 128)
        metac = np.zeros((128, 1024), np.float32)
        metac[:16] = meta
        mpos = np.zeros(128)
        mpos[:16] = np.arange(16)
        mchunks = [16 * s] + [16 * s + 1 + c for c in range(16)] + [16 * s + 17]
        xm = np.concatenate([metac] + [chunk_data(b, n) for n in mchunks], axis=0)
        csm = np.concatenate([cs_of(mpos)] + [cs_of(chunk_pos(n)) for n in mchunks], axis=0)
        mk = [m_prev, m_next, m_all if s == 0 else m_prev, m_all if s == 3 else m_next]
        masks = np.concatenate([np.tile(m, (1, 4)) for m in mk], axis=1).astype(bf)
        d = dict(common)
        d.update(xs=xs, css=css, xm=xm, csm=csm, acttab=acttab, masks=masks)
        in_maps.append(d)
    return in_maps


_NC_CACHE = {}


def kernel(**inputs):
    in_maps = _host_prep(inputs)
    if 'nc' not in _NC_CACHE:
        _NC_CACHE['nc'] = build_nc()
    nc = _NC_CACHE['nc']
    res = run_bass_kernel_spmd(nc, in_maps, core_ids=list(range(8)))
    out = np.empty((2, SEQ, D_MODEL), np.float32)
    for core in range(8):
        b, s = divmod(core, 4)
        out[b, s * 2048:(s + 1) * 2048] = np.asarray(res.results[core]["out"], np.float32)
    return out
```

```python
import contextlib
import numpy as np
import ml_dtypes
import concourse.bass as bass
import concourse.mybir as mybir
from concourse.bass_utils import run_bass_kernel_spmd

F32 = mybir.dt.float32
BF16 = mybir.dt.bfloat16
AF = mybir.ActivationFunctionType
ALU = mybir.AluOpType

SAME_ENGINE_SYNC = True
DEFER_W = True
PIPE_A = True
ENGS = ['pe', 'dve', 'act', 'pool', 'sp']

D_MODEL = 1024
SEQ = 8192
N_META = 16
NSLOT = 64
NOWN = 16
EPS = 1e-6
SCALE = 128 ** -0.5
LN_SCALE = float(-0.5 * np.log(128.0))
NEG = -30000.0


class Res:
    __slots__ = ('name', 'last_w', 'readers', 'excl')

    def __init__(self, name):
        self.name = name
        self.last_w = None
        self.readers = []
        self.excl = False


class Sched:
    def __init__(self, nc):
        self.nc = nc
        self.ops = []

    def add(self, eng, fn, reads=(), writes=(), dma=None, extra=()):
        self.ops.append((eng, fn, tuple(reads), tuple(writes), dma, tuple(extra)))
        return len(self.ops) - 1

    def build(self):
        nc = self.nc
        last = {}
        for i, op in enumerate(self.ops):
            if op[4] is not None:
                last[op[4]] = i
        self.add('sp', None, extra=sorted(last.values()))
        ops = self.ops
        n = len(ops)
        deps = [None] * n
        signaled = [False] * n
        for i, (eng, fn, reads, writes, dk, extra) in enumerate(ops):
            d = set(extra)
            for r in reads:
                if r.last_w is not None:
                    d.add(r.last_w)
                if r.excl:
                    d.update(j for j in r.readers if ops[j][0] != eng)
            for w in writes:
                if w.last_w is not None:
                    d.add(w.last_w)
                d.update(w.readers)
            d.discard(i)
            dd = set()
            latest = {}
            for j in d:
                ej = ops[j][0]
                dmaj = ops[j][4] is not None
                if dmaj and dk is not None and ops[j][4] == dk and dk.startswith('T:'):
                    continue
                if (not dmaj) and ej == eng and dk is None and (eng == 'pe' or not SAME_ENGINE_SYNC):
                    continue
                if dmaj:
                    dd.add(j)
                else:
                    latest[ej] = max(latest.get(ej, -1), j)
            dd.update(latest.values())
            deps[i] = dd
            for j in dd:
                signaled[j] = True
            for r in reads:
                r.readers.append(i)
            for w in writes:
                w.last_w = i
                w.readers = []
        cnt = {e: 0 for e in ENGS}
        val = [0] * n
        dmacnt = {}
        for i, op in enumerate(ops):
            if op[4] is not None:
                dmacnt[op[4]] = dmacnt.get(op[4], 0) + 1
                val[i] = 16 * dmacnt[op[4]]
            elif signaled[i]:
                cnt[op[0]] += 1
                val[i] = cnt[op[0]]
        for i, op in enumerate(ops):
            if op[4] is not None and op[4].startswith('T:'):
                val[i] = 16 * dmacnt[op[4]]
        self.stats = dict(n_ops=n, milestones=dict(cnt), dma_keys=len(dmacnt))
        with contextlib.ExitStack() as st:
            sem_eng = {e: st.enter_context(nc.semaphore('s_' + e)) for e in ENGS}
            dma_sem = {k: st.enter_context(nc.semaphore('d_' + k.replace(':', '_'))) for k in dmacnt}

            def emit(engname, e):
                waited = {}
                for i, op in enumerate(ops):
                    if op[0] != engname:
                        continue
                    need = {}
                    for j in deps[i]:
                        if ops[j][4] is not None:
                            key = ('d', ops[j][4])
                            sem = dma_sem[ops[j][4]]
                        else:
                            key = ('e', ops[j][0])
                            sem = sem_eng[ops[j][0]]
                        if val[j] > need.get(key, (None, 0))[1]:
                            need[key] = (sem, val[j])
                    for key, (sem, v) in need.items():
                        if waited.get(key, 0) >= v:
                            continue
                        e.wait_ge(sem, v)
                        waited[key] = v
                    if op[1] is None:
                        continue
                    inst = op[1](e)
                    if op[4] is not None:
                        inst.then_inc(dma_sem[op[4]], 16)
                    elif signaled[i]:
                        inst.then_inc(sem_eng[engname], 1)

            with nc.Block() as block:
                @block.tensor
                def _(e):
                    emit('pe', e)

                @block.vector
                def _(e):
                    emit('dve', e)

                @block.scalar
                def _(e):
                    emit('act', e)

                @block.gpsimd
                def _(e):
                    emit('pool', e)

                @block.sync
                def _(e):
                    emit('sp', e)


def I(name, *a, **kw):
    return lambda e: getattr(e, name)(*a, **kw)


def build_nc(debug=False, stop_after=None, a_steps=19):
    nc = bass.Bass("TRN2", target_bir_lowering=False)

    def din(name, shape, dt=F32):
        return nc.dram_tensor(name, shape, dt, kind="ExternalInput").ap()

    xs = din("xs", [NSLOT * 128, 1024])
    css = din("css", [NSLOT * 128, 128])
    xm = din("xm", [19 * 128, 1024])
    csm = din("csm", [19 * 128, 128])
    w_in = din("w_in", [1024, 7680])
    w_rb = din("w_rb", [1024, 1024])
    w_ab = din("w_ab", [1024, 1024])
    w_o = din("w_o", [1024, 1024])
    prew = din("prew", [128, 1024])
    postw = din("postw", [128, 1024])
    nwb = din("nwb", [128, 1024])
    dec8 = din("dec8", [128, 8])
    sink8 = din("sink8", [128, 8])
    rtab = din("rtab", [128, 512])
    ctab = din("ctab", [128, 256])
    jtab = din("jtab", [128, 2])
    acttab = din("acttab", [128, 128])
    masks = din("masks", [128, 4 * 512], BF16)
    identd = din("ident", [128, 128], BF16)
    out = nc.dram_tensor("out", [NOWN * 128, 1024], F32, kind="ExternalOutput").ap()
    sbst = nc.dram_tensor("sbst", [NOWN * 128, 1024], BF16, kind="Internal").ap()
    mixrd = nc.dram_tensor("mixrd", [NOWN * 128, 1024], BF16, kind="Internal").ap()
    wbf = nc.dram_tensor("wbf", [1024, 3584], BF16, kind="Internal").ap()
    wabf = nc.dram_tensor("wabf", [1024, 1024], BF16, kind="Internal").ap()
    dbg = {}
    if debug:
        dbg['sf'] = nc.dram_tensor("dbg_sf", [128, 1024], F32, kind="ExternalOutput").ap()
        dbg['sb'] = nc.dram_tensor("dbg_sb", [128, 1024], F32, kind="ExternalOutput").ap()
        dbg['mixr'] = nc.dram_tensor("dbg_mixr", [NOWN * 128, 1024], BF16, kind="ExternalOutput").ap()

    with contextlib.ExitStack() as st:
        RES = {}

        def R(name):
            if name not in RES:
                RES[name] = Res(name)
            return RES[name]

        def sb(name, shape, dt):
            t = st.enter_context(nc.sbuf_tensor(name, shape, dt))
            R(name)
            return t

        def ps(name, shape, dt):
            return st.enter_context(nc.psum_tensor(name, shape, dt))

        S = Sched(nc)

        WW = sb("WW", [128, 8, 4096], BF16)
        W2 = sb("W2", [128, 8, 1024], BF16)
        W3 = sb("W3", [128, 8, 1024], BF16)
        prew_t = sb("prew_t", [128, 1024], F32)
        nw_t = sb("nw_t", [128, 1024], F32)
        ident = sb("ident_t", [128, 128], BF16)
        mask_t = sb("mask_t", [128, 4 * 512], BF16)
        dec_t = sb("dec_t", [128, 8], F32)
        sink_t = sb("sink_t", [128, 8], F32)
        jtab_t = sb("jtab_t", [128, 2], F32)
        act_t = sb("act_t", [128, 128], F32)
        e8 = sb("e8", [128, 8], F32)
        lg8 = sb("lg8", [128, 8], F32)
        lg128 = sb("lg128", [128, 8], F32)
        DT = sb("DT", [128, 512], BF16)
        QDF = sb("QDF", [128, 512], F32)
        QDB = sb("QDB", [128, 512], F32)
        KD8 = sb("KD8", [128, 8], F32)
        CD8 = sb("CD8", [128, 8], F32)
        KDSf = sb("KDSf", [128, 256], F32)
        KDSb = sb("KDSb", [128, 256], F32)
        AFt = sb("AFt", [128, 256], F32)
        ABt = sb("ABt", [128, 256], F32)
        zrow = sb("zrow", [128, 128], F32)
        S_f = sb("S_f", [128, 1024], F32)
        S_b = sb("S_b", [128, 1024], F32)
        S_fbf1 = sb("S_fbf0", [128, 1024], BF16)
        S_fbf = [S_fbf1, S_fbf1]
        sbst2 = sb("sbst2", [128, 2048], BF16)
        sbstg = [sbst2[:, 0:1024], sbst2[:, 1024:2048]]
        R("sbstg0")
        R("sbstg1")
        XB = [sb("XB%d" % i, [128, 1024], F32) for i in range(3)]
        XB.append(sbst2.bitcast(F32))
        R("XB3")
        CS = [sb("CS%d" % i, [128, 128], F32) for i in range(4)]
        ssq = [sb("ssq%d" % i, [128, 1], F32) for i in range(2)]
        vv = [sb("vv%d" % i, [128, 1], F32) for i in range(2)]
        rstd = [sb("rstd%d" % i, [128, 1], F32) for i in range(2)]
        mhalf = sb("mhalf", [128, 8], F32)
        u0 = sb("u0", [128, 1024], F32)
        ub = [sb("ub%d" % i, [128, 1024], BF16) for i in range(2)]
        uT = [sb("uT%d" % i, [128, 8, 128], BF16) for i in range(2)]
        rotA = [sb("rotA%d" % i, [128, 512], F32) for i in range(2)]
        rotB = [sb("rotB%d" % i, [128, 512], F32) for i in range(2)]
        k_rot = [sb("k_rot%d" % i, [128, 512], BF16) for i in range(2)]
        q_rot1 = sb("q_rot", [128, 1024], BF16)
        q_rot = [q_rot1, q_rot1]
        kdf = [sb("kdf%d" % i, [128, 512], BF16) for i in range(2)]
        kdb = [sb("kdb%d" % i, [128, 512], BF16) for i in range(2)]
        v_tok = [sb("v_tok%d" % i, [128, 1024], BF16) for i in range(2)]
        qT = [sb("qT%d" % i, [128, 512], BF16) for i in range(2)]
        kT1 = sb("kT", [128, 512], BF16)
        qdf1 = sb("qdf", [128, 512], BF16)
        qdb1 = sb("qdb", [128, 512], BF16)
        sd1 = sb("sd", [128, 512], BF16)
        kT2 = sb("kT2", [128, 512], BF16)
        kT, qdf, qdb, sd = [kT1, kT2], [qdf1] * 2, [qdb1] * 2, [sd1] * 2
        kTR = ["kT", "kT2"]
        g_r = [sb("g_r%d" % i, [128, 1024], BF16) for i in range(2)]
        sg_r = [sb("sg_r%d" % i, [128, 1024], BF16) for i in range(2)]
        gnst = sb("gnst", [128, 24], F32)
        gnmv = sb("gnmv", [128, 8], F32)
        gnve = sb("gnve", [128, 4], F32)
        gnrs = sb("gnrs", [128, 4], F32)
        gnnm = sb("gnnm", [128, 4], F32)
        o_n = sb("o_n", [128, 1024], BF16)
        o_g = sb("o_g", [128, 1024], BF16)
        o_gT = sb("o_gT", [128, 8, 128], BF16)
        mixr1 = sb("mixr", [128, 1024], BF16)
        mixr = [mixr1, mixr1]
        KT = [sb("KT%d" % i, [128, 2, 128], BF16) for i in range(4)]
        VA = [sb("VA%d" % i, [128, 2, 130], BF16) for i in range(4)]
        KTm = sb("KTm", [128, 2, 128], BF16)
        Vm = sb("Vm", [128, 2, 130], BF16)
        PTb = [[kdf[0], kdf[1], kdb[0]], [kdb[1], qT[0], qT[1]]]
        PTbR = [["kdf0", "kdf1", "kdb0"], ["kdb1", "qT0", "qT1"]]
        PTm = [[kT1, qdf1], [qdb1, sd1]]
        PTmR = [["kT", "qdf"], ["qdb", "sd"]]
        th_t = QDF
        rl = sb("rl", [128, 4], F32)
        mixa = S_f.bitcast(BF16)
        mixb = o_n
        mixT = S_fbf[0][:].rearrange("p (a b) -> p a b", a=8, b=128)
        tfin = S_b

        rtab_t = rotA[0]
        RES["rtab_t"] = RES["rotA0"]
        ctab_t = u0[:, 256:512]
        tmpA = u0[:, 0:256]
        tmp1 = u0[:, 512:640]
        tmp2 = u0[:, 640:768]
        for nm_ in ("ctab_t", "tmpA", "tmp1", "tmp2"):
            RES[nm_] = RES["u0"]
        print("sbuf remaining", nc.sbuf_bytes_remaining() if callable(getattr(nc, 'sbuf_bytes_remaining', None)) else nc.sbuf_bytes_remaining, flush=True)
        PA = ps("PA", [128, 1024], F32)
        PTr = ps("PTr", [128, 2048], BF16)
        PS = ps("PS", [128, 1024], F32)
        PO = ps("PO", [128, 1024], F32)
        for nm in ["PA0", "PA1", "PT0", "PT1", "PS0", "PS1", "PO0", "PO1", "sbst_d", "mixr_d"]:
            R(nm)
        for nm in ["PA0", "PA1", "PT0", "PT1", "PS0", "PS1", "PO0", "PO1"]:
            R(nm).excl = True
        PTf = PTr.bitcast(F32)
        PA_all = {
            'state': [(PA[:, 0:512], R("PA0")), (PA[:, 512:1024], R("PA1"))],
            'main': [(PA[:, 0:512], R("PA0")), (PA[:, 512:1024], R("PA1"))],
        }
        PT_all = {
            'state': [(PTr[:, 0:1024], R("PT0")), (PTr[:, 1024:2048], R("PT1"))],
            'main': [(PTr[:, 0:1024], R("PT0")), (PTr[:, 1024:2048], R("PT1"))],
        }
        mode = ['state']
        cnt = dict(pa=0, pt=0, x=0, xa=0, par=0)

        def nxt(key, mod):
            v = cnt[key]
            cnt[key] = v + 1
            return v % mod

        def ld(dst, src, rname, q='sp', key='T:setup'):
            S.add(q, I('dma_start', out=dst, in_=src), writes=[R(rname)], dma=key)

        ld(dec_t[:], dec8, "dec_t")
        ld(sink_t[:], sink8, "sink_t")
        ld(rtab_t[:], rtab, "rtab_t")
        ld(ctab_t, ctab, "ctab_t")
        ld(jtab_t[:], jtab, "jtab_t")
        ld(act_t[:], acttab, "act_t")
        ld(ident[:], identd, "ident_t")
        ld(mask_t[:], masks, "mask_t")
        ld(prew_t[:], prew, "prew_t")
        ld(nw_t[:], nwb, "nw_t")

        def wload(dst_tile, rname, src, c0, n, d0, key):
            rn = rname if isinstance(rname, (list, tuple)) else [rname]
            for kt in range(8):
                S.add('pool', I('dma_start', out=dst_tile[:, kt, d0:d0 + n], in_=src[kt * 128:(kt + 1) * 128, c0:c0 + n]),
                      writes=[R(r_) for r_ in rn], dma=key)

        wload(WW, "WW", w_in, 512, 512, 512, 'T:w1')
        wload(WW, "WW", w_in, 1024, 1024, 1024, 'T:w1')

        S.add('pool', I('memset', mhalf[:], -0.5), writes=[R("mhalf")])
        S.add('pool', I('memset', zrow[:], 0.0), writes=[R("zrow")])
        S.add('pool', I('memset', S_f[:], 0.0), writes=[R("S_f")])
        S.add('pool', I('memset', S_b[:], 0.0), writes=[R("S_b")])
        for i in range(4):
            S.add('pool', I('memset', VA[i][:], 1.0), writes=[R("VA%d" % i)])
        S.add('pool', I('memset', Vm[:], 0.0), writes=[R("Vm")])
        S.add('pool', I('memset', Vm[0:16, :, 128:129], 1.0), writes=[R("Vm")])
        S.add('pool', I('memset', Vm[32:33, :, 128:129], 1.0), writes=[R("Vm")])
        S.add('act', I('activation', out=e8[:], in_=dec_t[:], func=AF.Exp, scale=-1.0), reads=[R("dec_t")], writes=[R("e8")])
        S.add('act', I('activation', out=e8[:], in_=e8[:], func=AF.Ln, bias=1.0), reads=[R("e8")], writes=[R("e8")])
        S.add('dve', I('tensor_scalar', out=lg8[:], in0=e8[:], scalar1=-1.0, scalar2=None, op0=ALU.mult), reads=[R("e8")], writes=[R("lg8")])
        S.add('dve', I('tensor_scalar', out=lg128[:], in0=e8[:], scalar1=-128.0, scalar2=None, op0=ALU.mult), reads=[R("e8")], writes=[R("lg128")])
        for h in range(4):
            hs = slice(h * 128, (h + 1) * 128)
            S.add('act', I('activation', out=tmp1, in_=rtab_t[:, 0:128], func=AF.Exp, scale=lg8[:, h:h + 1]),
                  reads=[R("rtab_t"), R("lg8")], writes=[R("tmp1")])
            S.add('act', I('activation', out=tmp2, in_=rtab_t[:, 256:384], func=AF.Exp, scale=lg8[:, 4 + h:5 + h]),
                  reads=[R("rtab_t"), R("lg8")], writes=[R("tmp2")])
            S.add('dve', I('tensor_tensor', out=tmp1, in0=tmp1, in1=rtab_t[:, 128:256], op=ALU.mult), reads=[R("tmp1"), R("rtab_t")], writes=[R("tmp1")])
            S.add('dve', I('tensor_tensor', out=tmp2, in0=tmp2, in1=rtab_t[:, 384:512], op=ALU.mult), reads=[R("tmp2"), R("rtab_t")], writes=[R("tmp2")])
            S.add('dve', I('tensor_tensor', out=DT[:, hs], in0=tmp1, in1=tmp2, op=ALU.add), reads=[R("tmp1"), R("tmp2")], writes=[R("DT")])
            S.add('act', I('activation', out=QDF[:, hs], in_=ctab_t[:, 0:128], func=AF.Exp, scale=lg8[:, h:h + 1]),
                  reads=[R("ctab_t"), R("lg8")], writes=[R("QDF")])
            S.add('act', I('activation', out=QDB[:, hs], in_=ctab_t[:, 128:256], func=AF.Exp, scale=lg8[:, 4 + h:5 + h]),
                  reads=[R("ctab_t"), R("lg8")], writes=[R("QDB")])
            S.add('act', I('activation', out=KD8[:, h:h + 1], in_=jtab_t[:, 0:1], func=AF.Exp, scale=lg8[:, h:h + 1], bias=LN_SCALE),
                  reads=[R("jtab_t"), R("lg8")], writes=[R("KD8")])
            S.add('act', I('activation', out=KD8[:, 4 + h:5 + h], in_=jtab_t[:, 1:2], func=AF.Exp, scale=lg8[:, 4 + h:5 + h], bias=LN_SCALE),
                  reads=[R("jtab_t"), R("lg8")], writes=[R("KD8")])
        S.add('act', I('activation', out=CD8[:], in_=lg128[:], func=AF.Exp), reads=[R("lg128")], writes=[R("CD8")])

        def bc_slot(tile, off):
            return bass.AP(tile, off, [[tile.shape[1], 128], [0, NSLOT], [1, 4]])

        def bc_head(tile, off):
            return bass.AP(tile, off, [[tile.shape[1], 128], [1, NSLOT], [0, 4]])

        def v3(tile):
            ap_ = tile if isinstance(tile, bass.AP) else tile[:]
            return ap_.rearrange("p (s h) -> p s h", s=NSLOT, h=4)

        S.add('dve', I('tensor_tensor', out=v3(KDSf), in0=bc_slot(KD8, 0), in1=bc_head(act_t, 0), op=ALU.mult),
              reads=[R("KD8"), R("act_t")], writes=[R("KDSf")])
        S.add('dve', I('tensor_tensor', out=v3(KDSb), in0=bc_slot(KD8, 4), in1=bc_head(act_t, 64), op=ALU.mult),
              reads=[R("KD8"), R("act_t")], writes=[R("KDSb")])
        S.add('dve', I('tensor_tensor', out=v3(tmpA), in0=bc_slot(lg128, 0), in1=bc_head(act_t, 0), op=ALU.mult),
              reads=[R("lg128"), R("act_t")], writes=[R("tmpA")])
        S.add('act', I('activation', out=AFt[:], in_=tmpA, func=AF.Exp), reads=[R("tmpA")], writes=[R("AFt")])
        S.add('dve', I('tensor_tensor', out=v3(tmpA), in0=bc_slot(lg128, 4), in1=bc_head(act_t, 64), op=ALU.mult),
              reads=[R("lg128"), R("act_t"), R("AFt")], writes=[R("tmpA")])
        S.add('act', I('activation', out=ABt[:], in_=tmpA, func=AF.Exp), reads=[R("tmpA")], writes=[R("ABt")])
        deferred = []

        def wload_def(dst_tile, rname, src, c0, n, d0, key):
            for kt in range(8):
                deferred.append(lambda kt=kt: S.add('pool', I('dma_start', out=dst_tile[:, kt, d0:d0 + n], in_=src[kt * 128:(kt + 1) * 128, c0:c0 + n]),
                                                   writes=[R(rname)], dma=key))

        wload_def(WW, "WWq", w_in, 0, 512, 0, 'T:w2')
        wload_def(WW, "WWq", w_in, 2048, 1024, 2048, 'T:w2')
        wload_def(WW, "WWq", w_in, 5632, 1024, 3072, 'T:w2')
        wload_def(W2, "W2", w_rb, 0, 1024, 0, 'T:w2')
        wload_def(W3, "W3", w_o, 0, 1024, 0, 'T:w3')

        def precast_def(dst, rname, src, c0, n, d0, key):
            for kt in range(8):
                deferred.append(lambda kt=kt: S.add('pool', I('dma_start', out=dst[kt * 128:(kt + 1) * 128, d0:d0 + n], in_=src[kt * 128:(kt + 1) * 128, c0:c0 + n]),
                                                   writes=[R(rname)], dma=key))

        precast_def(wbf, "wbf_kv", w_in, 4096, 512, 1024, 'T:pc1')
        precast_def(wbf, "wbf_q", w_in, 3072, 1024, 0, 'T:pc2')
        precast_def(wbf, "wbf_g", w_in, 4608, 1024, 1536, 'T:pc3')
        precast_def(wbf, "wbf_s", w_in, 6656, 1024, 2560, 'T:pc4')
        precast_def(wabf, "wabf", w_ab, 0, 1024, 0, 'T:pc5')
        if not DEFER_W:
            while deferred:
                deferred.pop(0)()

        def interleave(tail, filler, k=1, lead=0):
            t_done = tail is None
            f_done = filler is None

            def step(gen):
                try:
                    next(gen)
                    return False
                except StopIteration:
                    return True
            for _ in range(lead):
                if not f_done:
                    f_done = step(filler)
            while not (t_done and f_done):
                if not t_done:
                    t_done = step(tail)
                for _ in range(k):
                    if not f_done:
                        f_done = step(filler)

        def front_a(xsrc, cssrc, ring=3):
            xi = nxt('x' if ring == 3 else 'xa', ring)
            par = nxt('par', 2)
            xb, cs = XB[xi], CS[xi]
            S.add('sp', I('dma_start', out=xb[:], in_=xsrc), writes=[R("XB%d" % xi)] + ([R("sbstg0"), R("sbstg1")] if xi == 3 else []), dma='x%d' % xi)
            S.add('sp', I('dma_start', out=cs[:], in_=cssrc), writes=[R("CS%d" % xi)], dma='c%d' % xi)
            S.add('act', I('activation', out=ub[par][:], in_=xb[:], func=AF.Square, accum_out=ssq[par][:]),
                  reads=[R("XB%d" % xi)], writes=[R("ub%d" % par), R("ssq%d" % par)])
            S.add('dve', I('tensor_scalar', out=vv[par][:], in0=ssq[par][:], scalar1=1.0 / D_MODEL, scalar2=EPS, op0=ALU.mult, op1=ALU.add),
                  reads=[R("ssq%d" % par)], writes=[R("vv%d" % par)])
            S.add('pool', I('tensor_tensor', out=rstd[par][:], in0=vv[par][:], in1=mhalf[:, 0:1], op=ALU.pow),
                  reads=[R("vv%d" % par), R("mhalf")], writes=[R("rstd%d" % par)])
            S.add('pool', I('tensor_tensor', out=u0[:], in0=xb[:], in1=prew_t[:], op=ALU.mult),
                  reads=[R("XB%d" % xi), R("prew_t")], writes=[R("u0")])
            S.add('act', I('activation', out=ub[par][:], in_=u0[:], func=AF.Copy, scale=rstd[par][:]),
                  reads=[R("u0"), R("rstd%d" % par)], writes=[R("ub%d" % par)])
            return dict(xi=xi, par=par)

        def front_b(ctx):
            par = ctx['par']
            tr_to(ub[par], "ub%d" % par, 8, uT[par][:].rearrange("p a b -> p (a b)"), "uT%d" % par)

        def tr_to(src_tile, src_res, ntile, dst_ap, dst_res, scale=None, src_off=0):
            pts = PT_all[mode[0]]
            pt, ptr = pts[nxt('pt', len(pts))]
            for t in range(ntile):
                S.add('pe', I('transpose', out=pt[:, t * 128:(t + 1) * 128], in_=src_tile[:, src_off + t * 128:src_off + (t + 1) * 128], identity=ident[:]),
                      reads=[R(src_res), R("ident_t")], writes=[ptr])
            if scale is None:
                S.add('act', I('activation', out=dst_ap, in_=pt[:, 0:ntile * 128], func=AF.Copy), reads=[ptr], writes=[R(dst_res)])
            else:
                S.add('act', I('activation', out=dst_ap, in_=pt[:, 0:ntile * 128], func=AF.Copy, scale=scale), reads=[ptr], writes=[R(dst_res)])

        def proj(par, wt, wres, c0, n=512):
            pas = PA_all[mode[0]]
            bank, bres = pas[nxt('pa', len(pas))]
            for kt in range(8):
                S.add('pe', I('matmul', out=bank[:, 0:n], lhsT=uT[par][:, kt, :], rhs=wt[:, kt, c0:c0 + n], start=(kt == 0), stop=(kt == 7)),
                      reads=[R("uT%d" % par)] + [R(r_) for r_ in (wres if isinstance(wres, (list, tuple)) else [wres])], writes=[bres])
            return bank[:, 0:n], bres

        def rotary(zb, zres, cs, csres, nh, dst_ap, dst_res, ri):
            n = nh * 128
            zv = zb.rearrange("p (h t f) -> p h t f", h=nh, t=2, f=64)
            A = rotA[ri][:, 0:n].rearrange("p (h t f) -> p h t f", h=nh, t=2, f=64)
            B = rotB[ri][:, 0:n].rearrange("p (h t f) -> p h t f", h=nh, t=2, f=64)
            dv = dst_ap.rearrange("p (h t f) -> p h t f", h=nh, t=2, f=64)
            cosb = bass.AP(cs, 0, [[128, 128], [0, nh], [0, 2], [1, 64]])
            sinb = bass.AP(cs, 64, [[128, 128], [0, nh], [1, 64]])
            rA, rB = R("rotA%d" % ri), R("rotB%d" % ri)
            S.add('dve', I('tensor_tensor', out=A, in0=zv, in1=cosb, op=ALU.mult), reads=[zres, R(csres)], writes=[rA])
            S.add('dve', I('tensor_tensor', out=B[:, :, 0, :], in0=zv[:, :, 1, :], in1=sinb, op=ALU.mult), reads=[zres, R(csres)], writes=[rB])
            S.add('dve', I('tensor_tensor', out=B[:, :, 1, :], in0=zv[:, :, 0, :], in1=sinb, op=ALU.mult), reads=[zres, R(csres)], writes=[rB])
            S.add('pool', I('tensor_tensor', out=dv[:, :, 0, :], in0=A[:, :, 0, :], in1=B[:, :, 0, :], op=ALU.subtract), reads=[rA, rB], writes=[R(dst_res)])
            S.add('pool', I('tensor_tensor', out=dv[:, :, 1, :], in0=A[:, :, 1, :], in1=B[:, :, 1, :], op=ALU.add), reads=[rA, rB], writes=[R(dst_res)])

        def bc_d(tile, off):
            return bass.AP(tile, off, [[tile.shape[1], 128], [1, 4], [0, 128]])

        def h3(ap):
            return ap.rearrange("p (h d) -> p h d", h=4, d=128)

        def kv_update(par, kd_tile, kd_res, bank_tile, bres, Stile, Sres, coef_tile, coef_off):
            for h in range(4):
                S.add('pe', I('matmul', out=bank_tile[:, h * 256:(h + 1) * 256], lhsT=kd_tile[:, h * 128:(h + 1) * 128],
                              rhs=v_tok[par][:, h * 256:(h + 1) * 256], start=True, stop=True),
                      reads=[R(kd_res), R("v_tok%d" % par)], writes=[bres[h // 2]])
            for h in range(4):
                S.add('dve', I('scalar_tensor_tensor',
                               out=Stile[:, h * 256:(h + 1) * 256], in0=Stile[:, h * 256:(h + 1) * 256], scalar=coef_tile[:, coef_off + h:coef_off + h + 1],
                               in1=bank_tile[:, h * 256:(h + 1) * 256], op0=ALU.mult, op1=ALU.add),
                      reads=[R(Sres), bres[h // 2], R(coef_tile.name)], writes=[R(Sres)])

        PSr = [R("PS0"), R("PS1")]
        POr = [R("PO0"), R("PO1")]
        final_dma = []

        nstg = [0]

        def st_head(t, ctx):
            xi, par = ctx['xi'], ctx['par']
            zk, zkr = proj(par, WW, "WW", 512)
            rotary(zk, zkr, CS[xi], "CS%d" % xi, 4, k_rot[par][:], "k_rot%d" % par, par)
            yield
            for hb in range(2):
                zv_, zvr = proj(par, WW, "WW", 1024 + hb * 512)
                S.add('act', I('activation', out=v_tok[par][:, hb * 512:(hb + 1) * 512], in_=zv_, func=AF.Copy),
                      reads=[zvr], writes=[R("v_tok%d" % par)])
                if hb == 0:
                    S.add('dve', I('tensor_tensor', out=h3(kdf[par][:]), in0=h3(k_rot[par][:]), in1=bc_d(KDSf, t * 4), op=ALU.mult),
                          reads=[R("k_rot%d" % par), R("KDSf")], writes=[R("kdf%d" % par)])
                    S.add('pool', I('tensor_tensor', out=h3(kdb[par][:]), in0=h3(k_rot[par][:]), in1=bc_d(KDSb, t * 4), op=ALU.mult),
                          reads=[R("k_rot%d" % par), R("KDSb")], writes=[R("kdb%d" % par)])
                yield

        def st_tail(t, ctx):
            par = ctx['par']
            if t >= 49:
                c = NSLOT - t
                sg = sbstg[nstg[0] % 2]
                sgr = "sbstg%d" % (nstg[0] % 2)
                nstg[0] += 1
                S.add('act', I('activation', out=sg[:], in_=S_b[:], func=AF.Copy), reads=[R("S_b")], writes=[R(sgr)])
                S.add('sp', I('dma_start', out=sbst[c * 128:(c + 1) * 128, :], in_=sg[:]), reads=[R(sgr)], writes=[R("sbst_d")], dma=sgr)
            kv_update(par, kdf[par], "kdf%d" % par, PS, PSr, S_f, "S_f", AFt, t * 4)
            yield
            kv_update(par, kdb[par], "kdb%d" % par, PO, POr, S_b, "S_b", ABt, t * 4)
            yield

        ctxs = {0: front_a(xs[0:128, :], css[0:128, :])}
        front_b(ctxs[0])
        prev_tail = None
        for t in range(NSLOT):
            if t + 1 < NSLOT:
                ctxs[t + 1] = front_a(xs[(t + 1) * 128:(t + 2) * 128, :], css[(t + 1) * 128:(t + 2) * 128, :])
            for _ in range(2):
                if deferred and DEFER_W:
                    deferred.pop(0)()
            interleave(prev_tail, st_head(t, ctxs[t]), k=2)
            if t + 1 < NSLOT:
                front_b(ctxs[t + 1])
            prev_tail = st_tail(t, ctxs[t])
        interleave(prev_tail, None)
        while deferred:
            deferred.pop(0)()
        sg = sbstg[nstg[0] % 2]
        sgr = "sbstg%d" % (nstg[0] % 2)
        S.add('act', I('activation', out=sg[:], in_=S_b[:], func=AF.Copy), reads=[R("S_b")], writes=[R(sgr)])
        S.add('sp', I('dma_start', out=sbst[0:128, :], in_=sg[:]), reads=[R(sgr)], writes=[R("sbst_d")], dma=sgr)
        S.add('act', I('activation', out=S_fbf[0][:], in_=S_f[:], func=AF.Copy), reads=[R("S_f")], writes=[R("S_fbf0")])
        if debug:
            final_dma.append(S.add('sp', I('dma_start', out=dbg['sf'], in_=S_f[:]), reads=[R("S_f")], dma='dbgsf'))
            final_dma.append(S.add('sp', I('dma_start', out=dbg['sb'], in_=S_b[:]), reads=[R("S_b")], dma='dbgsb'))

        if stop_after == 'state':
            S.add('sp', None, extra=final_dma)
            S.build()
            return nc
        mode[0] = 'main'

        def r_front(c):
            return front_a(xm[(2 + c) * 128:(3 + c) * 128, :], csm[(2 + c) * 128:(3 + c) * 128, :])

        def load_sb(c):
            sbc = sbstg[c % 2]
            sbr = "sbstg%d" % (c % 2)
            S.add('sp', I('dma_start', out=sbc[:], in_=sbst[c * 128:(c + 1) * 128, :]), reads=[R("sbst_d")], writes=[R(sbr)], dma=sbr)

        def r_head(c, ctx):
            xi, par = ctx['xi'], ctx['par']
            zq, zqr = proj(par, WW, "WWq", 0)
            rotary(zq, zqr, CS[xi], "CS%d" % xi, 4, q_rot[par][:, 0:512], "q_rot", 0)
            yield
            zk, zkr = proj(par, WW, "WW", 512)
            rotary(zk, zkr, CS[xi], "CS%d" % xi, 4, k_rot[par][:], "k_rot%d" % par, 1)
            tr_to(q_rot[par], "q_rot", 4, qT[par][:], "qT%d" % par)
            yield
            for hb in range(2):
                zv_, zvr = proj(par, WW, "WW", 1024 + hb * 512)
                S.add('act', I('activation', out=v_tok[par][:, hb * 512:(hb + 1) * 512], in_=zv_, func=AF.Copy),
                      reads=[zvr], writes=[R("v_tok%d" % par)])
                if hb == 0:
                    tr_to(k_rot[par], "k_rot%d" % par, 4, kT[par][:], kTR[par])
                    S.add('dve', I('tensor_tensor', out=h3(kdf[par][:]), in0=h3(k_rot[par][:]), in1=bc_d(KD8, 0), op=ALU.mult),
                          reads=[R("k_rot%d" % par), R("KD8")], writes=[R("kdf%d" % par)])
                yield
            for hb in range(2):
                zg, zgr = proj(par, WW, "WWq", 2048 + hb * 512)
                S.add('act', I('activation', out=g_r[par][:, hb * 512:(hb + 1) * 512], in_=zg, func=AF.Silu),
                      reads=[zgr], writes=[R("g_r%d" % par)])
                yield
            S.add('pool', I('tensor_tensor', out=g_r[par][:], in0=g_r[par][:], in1=nw_t[:], op=ALU.mult),
                  reads=[R("g_r%d" % par), R("nw_t")], writes=[R("g_r%d" % par)])
            for hb in range(2):
                zg, zgr = proj(par, WW, "WWq", 3072 + hb * 512)
                S.add('act', I('activation', out=sg_r[par][:, hb * 512:(hb + 1) * 512], in_=zg, func=AF.Tanh, scale=0.5),
                      reads=[zgr], writes=[R("sg_r%d" % par)])
                yield

        def r_tail_a(c, ctx):
            par = ctx['par']
            sbc = sbstg[c % 2]
            sbr = "sbstg%d" % (c % 2)
            for h in range(4):
                hs = slice(h * 128, (h + 1) * 128)
                S.add('pe', I('matmul', out=PS[:, hs], lhsT=kT[par][:, hs], rhs=qT[par][:, hs], start=True, stop=True),
                      reads=[R(kTR[par]), R("qT%d" % par)], writes=[PSr[0]])
            S.add('dve', I('tensor_tensor', out=sd[par][:], in0=PS[:, 0:512], in1=DT[:], op=ALU.mult), reads=[PSr[0], R("DT")], writes=[R("sd")])
            S.add('dve', I('tensor_tensor', out=qdf[par][:], in0=qT[par][:], in1=QDF[:], op=ALU.mult), reads=[R("qT%d" % par), R("QDF")], writes=[R("qdf")])
            S.add('pool', I('tensor_tensor', out=qdb[par][:], in0=qT[par][:], in1=QDB[:], op=ALU.mult), reads=[R("qT%d" % par), R("QDB")], writes=[R("qdb")])
            yield
            sfb = S_fbf[0]
            sfr = "S_fbf0"
            for h in range(4):
                hs = slice(h * 128, (h + 1) * 128)
                vs = slice(h * 256, (h + 1) * 256)
                S.add('pe', I('matmul', out=PO[:, vs], lhsT=qdf[par][:, hs], rhs=sfb[:, vs], start=True, stop=False),
                      reads=[R("qdf"), R(sfr)], writes=[POr[h // 2]])
                S.add('pe', I('matmul', out=PO[:, vs], lhsT=qdb[par][:, hs], rhs=sbc[:, vs], start=False, stop=False),
                      reads=[R("qdb"), R(sbr)], writes=[POr[h // 2]])
                S.add('pe', I('matmul', out=PO[:, vs], lhsT=sd[par][:, hs], rhs=v_tok[par][:, vs], start=False, stop=True),
                      reads=[R("sd"), R("v_tok%d" % par)], writes=[POr[h // 2]])
            if c < NOWN - 1:
                kv_update(par, kdf[par], "kdf%d" % par, PS, PSr, S_f, "S_f", CD8, 0)
                S.add('dve', I('tensor_copy', out=S_fbf[0][:], in_=S_f[:]), reads=[R("S_f")], writes=[R("S_fbf0")])
            for h in range(4):
                S.add('dve', I('bn_stats', out=gnst[:, h * 6:(h + 1) * 6], in_=PO[:, h * 256:(h + 1) * 256]), reads=[POr[h // 2]], writes=[R("gnst")])
            for h in range(4):
                S.add('dve', I('bn_aggr', out=gnmv[:, h * 2:(h + 1) * 2], in_=gnst[:, h * 6:(h + 1) * 6]), reads=[R("gnst")], writes=[R("gnmv")])
            mvv = gnmv[:].rearrange("p (h t) -> p h t", h=4, t=2)
            S.add('dve', I('tensor_scalar', out=gnve[:], in0=mvv[:, :, 1], scalar1=EPS, scalar2=None, op0=ALU.add), reads=[R("gnmv")], writes=[R("gnve")])
            S.add('pool', I('tensor_tensor', out=gnrs[:], in0=gnve[:], in1=mhalf[:, 0:4], op=ALU.pow), reads=[R("gnve"), R("mhalf")], writes=[R("gnrs")])
            S.add('dve', I('scalar_tensor_tensor', out=gnnm[:], in0=mvv[:, :, 0], scalar=-1.0, in1=gnrs[:], op0=ALU.mult, op1=ALU.mult),
                  reads=[R("gnmv"), R("gnrs")], writes=[R("gnnm")])
            for h in range(4):
                vs = slice(h * 256, (h + 1) * 256)
                S.add('dve', I('tensor_scalar', out=o_n[:, vs], in0=PO[:, vs], scalar1=gnrs[:, h:h + 1], scalar2=gnnm[:, h:h + 1], op0=ALU.mult, op1=ALU.add),
                      reads=[POr[h // 2], R("gnrs"), R("gnnm")], writes=[R("o_n")])
            S.add('dve', I('tensor_tensor', out=o_g[:], in0=o_n[:], in1=g_r[par][:], op=ALU.mult), reads=[R("o_n"), R("g_r%d" % par)], writes=[R("o_g")])
            yield

        def r_tail_b(c, ctx):
            par = ctx['par']
            tr_to(o_g, "o_g", 8, o_gT[:].rearrange("p a b -> p (a b)"), "o_gT")
            yield
            for nb_ in range(2):
                for kt in range(8):
                    S.add('pe', I('matmul', out=PO[:, nb_ * 512:(nb_ + 1) * 512], lhsT=o_gT[:, kt, :], rhs=W2[:, kt, nb_ * 512:(nb_ + 1) * 512],
                                  start=(kt == 0), stop=(kt == 7)),
                          reads=[R("o_gT"), R("W2")], writes=[POr[nb_]])
            mr = mixr[0]
            mrr = "mixr"
            for nb_ in range(2):
                S.add('dve', I('scalar_tensor_tensor', out=mr[:, nb_ * 512:(nb_ + 1) * 512], in0=sg_r[par][:, nb_ * 512:(nb_ + 1) * 512], scalar=1.0,
                               in1=PO[:, nb_ * 512:(nb_ + 1) * 512], op0=ALU.add, op1=ALU.mult),
                      reads=[R("sg_r%d" % par), POr[nb_]], writes=[R(mrr)])
            S.add('sp', I('dma_start', out=mixrd[c * 128:(c + 1) * 128, :], in_=mr[:]), reads=[R(mrr)], writes=[R("mixr_d")], dma=mrr)
            if debug:
                final_dma.append(S.add('sp', I('dma_start', out=dbg['mixr'][c * 128:(c + 1) * 128, :], in_=mr[:]), reads=[R(mrr)], dma='dbgm'))
            yield

        ctxs = {0: r_front(0)}
        front_b(ctxs[0])
        def chain(*gens):
            for g_ in gens:
                if g_ is not None:
                    yield from g_

        def alt(ga, gb):
            live = [g_ for g_ in (ga, gb) if g_ is not None]
            while live:
                for g_ in list(live):
                    try:
                        next(g_)
                        yield
                    except StopIteration:
                        live.remove(g_)

        ctxs[1] = r_front(1)
        for c in range(NOWN + 2):
            if c < NOWN:
                load_sb(c)
            if c + 2 < NOWN:
                ctxs[c + 2] = r_front(c + 2)
            if c + 1 < NOWN:
                front_b(ctxs[c + 1])
            if c == NOWN:
                ld(nw_t[:], postw, "nw_t", key='pw')
                def reload(rn, sres, d0, n, key):
                    S.add('sp', I('dma_start', out=WW[:, :, d0:d0 + n], in_=wbf[:, d0:d0 + n].rearrange("(kt p) c -> p kt c", p=128)),
                          reads=[R(sres)], writes=[R("WW"), R("WWq"), R(rn)], dma=key)
                reload("WAkv", "wbf_kv", 1024, 512, 'w4a')
                reload("WAq", "wbf_q", 0, 1024, 'w4b')
                reload("WAg", "wbf_g", 1536, 1024, 'w4c')
                reload("WAs", "wbf_s", 2560, 1024, 'w4d')
            tb = r_tail_b(c - 2, ctxs[c - 2]) if 0 <= c - 2 < NOWN else None
            ta = r_tail_a(c - 1, ctxs[c - 1]) if 0 <= c - 1 < NOWN else None
            hd = r_head(c, ctxs[c]) if c < NOWN else None
            interleave(alt(tb, ta), hd, k=2, lead=1)

        if stop_after == 'R':
            S.add('sp', None, extra=final_dma)
            S.build()
            return nc
        S.add('sp', I('dma_start', out=W2[:], in_=wabf.rearrange("(kt p) c -> p kt c", p=128)), reads=[R("wabf")], writes=[R("W2")], dma='w6')
        while deferred:
            deferred.pop(0)()

        for p in range(2):
            for g in range(2):
                S.add('pool', I('memset', PTm[p][g][:], 0.0), writes=[R(PTmR[p][g])])
        for p in range(2):
            for g in range(2):
                for hh in range(4):
                    h = g * 4 + hh
                    S.add('act', I('activation', out=PTm[p][g][32:33, hh * 128:(hh + 1) * 128], in_=zrow[32:33, :], func=AF.Exp,
                                   bias=sink_t[32:33, h:h + 1], scale=1.0),
                          reads=[R("zrow"), R("sink_t")], writes=[R(PTmR[p][g])])

        def a_head(idx, ctx, nctx=None):
            xi, par = ctx['xi'], ctx['par']
            zkv, zkvr = proj(par, WW, "WAkv", 1024)
            if idx == 0:
                ktile, kres, vtile, vres = KTm, "KTm", Vm, "Vm"
            else:
                ktile, kres, vtile, vres = KT[idx % 4], "KT%d" % (idx % 4), VA[idx % 4], "VA%d" % (idx % 4)
            if idx == 0:
                S.add('act', I('activation', out=Vm[0:16, :, 0:128], in_=zkv[0:16, 256:512].rearrange("p (g d) -> p g d", g=2, d=128), func=AF.Copy),
                      reads=[zkvr], writes=[R("Vm")])
            else:
                S.add('act', I('activation', out=vtile[:, :, 0:128], in_=zkv[:, 256:512].rearrange("p (g d) -> p g d", g=2, d=128), func=AF.Copy),
                      reads=[zkvr], writes=[R(vres)])
            rotary(zkv[:, 0:256], zkvr, CS[xi], "CS%d" % xi, 2, k_rot[par][:, 0:256], "k_rot%d" % par, 0)
            yield
            if 2 <= idx <= 17:
                c = idx - 2
                pa_c = c % 2
                own_info[c] = (xi, pa_c)
                for hb in range(2):
                    zq, zqr = proj(par, WW, "WAq", hb * 512)
                    rotary(zq, zqr, CS[xi], "CS%d" % xi, 4, q_rot[par][:, hb * 512:(hb + 1) * 512], "q_rot", 1)
                    if hb == 0:
                        tr_to(k_rot[par], "k_rot%d" % par, 2, ktile[:].rearrange("p a b -> p (a b)"), kres)
                    yield
                for hb in range(2):
                    zg, zgr = proj(par, WW, "WAg", 1536 + hb * 512)
                    S.add('act', I('activation', out=th_t[:], in_=zg, func=AF.Tanh, scale=0.5), reads=[zgr], writes=[R("QDF")])
                    S.add('dve', I('scalar_tensor_tensor', out=g_r[pa_c][:, hb * 512:(hb + 1) * 512], in0=th_t[:], scalar=1.0, in1=zg, op0=ALU.add, op1=ALU.mult),
                          reads=[R("QDF"), zgr], writes=[R("g_r%d" % pa_c)])
                    if hb == 0:
                        tr_to(q_rot[par], "q_rot", 8, v_tok[pa_c][:], "v_tok%d" % pa_c)
                    yield
                for hb in range(2):
                    zg, zgr = proj(par, WW, "WAs", 2560 + hb * 512)
                    S.add('act', I('activation', out=sg_r[pa_c][:, hb * 512:(hb + 1) * 512], in_=zg, func=AF.Tanh, scale=0.5),
                          reads=[zgr], writes=[R("sg_r%d" % pa_c)])
                    yield
            else:
                tr_to(k_rot[par], "k_rot%d" % par, 2, ktile[:].rearrange("p a b -> p (a b)"), kres)
                yield
            if nctx is not None:
                front_b(nctx)
                yield

        def a_post(c, xi_c):
            q = c % 2
            S.add('act', I('activation', out=o_n[:], in_=PO[:, :], func=AF.Square, accum_out=ssq[q][:]), reads=POr, writes=[R("o_n"), R("ssq%d" % q)])
            S.add('dve', I('tensor_scalar', out=vv[q][:], in0=ssq[q][:], scalar1=1.0 / D_MODEL, scalar2=EPS, op0=ALU.mult, op1=ALU.add),
                  reads=[R("ssq%d" % q)], writes=[R("vv%d" % q)])
            S.add('pool', I('tensor_tensor', out=rstd[q][:], in0=vv[q][:], in1=mhalf[:, 0:1], op=ALU.pow), reads=[R("vv%d" % q), R("mhalf")], writes=[R("rstd%d" % q)])
            for nb_ in range(2):
                cs_ = slice(nb_ * 512, (nb_ + 1) * 512)
                S.add('dve', I('scalar_tensor_tensor', out=tfin[:, cs_], in0=PO[:, cs_], scalar=rstd[q][:], in1=nw_t[:, cs_], op0=ALU.mult, op1=ALU.mult),
                      reads=[POr[nb_], R("rstd%d" % q), R("nw_t")], writes=[R("S_b")])
            S.add('pool', I('tensor_tensor', out=tfin[:], in0=tfin[:], in1=XB[xi_c][:], op=ALU.add), reads=[R("S_b"), R("XB%d" % xi_c)], writes=[R("S_b")])
            final_dma.append(S.add('sp', I('dma_start', out=out[c * 128:(c + 1) * 128, :], in_=tfin[:]), reads=[R("S_b")], dma="resb"))

        def a_tail(c, xi_c, pa_c, post=None):
            if post is not None:
                a_post(*post)
            blocks = []
            for bi, idx in enumerate((c + 1, c + 2, c + 3)):
                if bi == 0:
                    mk = 2 if c == 0 else 0
                elif bi == 2:
                    mk = 3 if c == NOWN - 1 else 1
                else:
                    mk = None
                blocks.append((idx % 4, mk))
            pm = c % 2
            def tr_group(g):
                tr_to(o_g, "o_g", 4, o_gT[:, g * 4:(g + 1) * 4, :].rearrange("p a b -> p (a b)"), "o_gT", src_off=g * 512)

            for g in range(2):
                qg = v_tok[pa_c][:, g * 512:(g + 1) * 512]
                for bi, (sl, mk) in enumerate(blocks):
                    bank = PS[:, (bi % 2) * 512:(bi % 2 + 1) * 512]
                    br = PSr[bi % 2]
                    S.add('pe', I('matmul', out=bank, lhsT=KT[sl][:, g, :], rhs=qg, start=True, stop=(mk is None)),
                          reads=[R("KT%d" % sl), R("v_tok%d" % pa_c)], writes=[br])
                    if mk is not None:
                        S.add('pe', I('matmul', out=bank, lhsT=ident[:], rhs=mask_t[:, mk * 512:(mk + 1) * 512], start=False, stop=True),
                              reads=[R("ident_t"), R("mask_t")], writes=[br])
                    S.add('act', I('activation', out=PTb[g][bi][:], in_=bank, func=AF.Exp, scale=SCALE),
                          reads=[br], writes=[R(PTbR[g][bi])])
                bank = PS[0:16, 512:1024]
                S.add('pe', I('matmul', out=bank, lhsT=KTm[:, g, 0:16], rhs=qg, start=True, stop=True),
                      reads=[R("KTm"), R("v_tok%d" % pa_c)], writes=[PSr[1]])
                S.add('act', I('activation', out=PTm[pm][g][0:16, :], in_=bank, func=AF.Exp, scale=SCALE),
                      reads=[PSr[1]], writes=[R(PTmR[pm][g])])
                if g == 1:
                    tr_group(0)
                yield
                for hh in range(4):
                    dst = PO[:, hh * 129:(hh + 1) * 129] if hh < 3 else PO[:, 512:641]
                    dr = POr[0] if hh < 3 else POr[1]
                    for bi, (sl, mk) in enumerate(blocks):
                        S.add('pe', I('matmul', out=dst, lhsT=PTb[g][bi][:, hh * 128:(hh + 1) * 128], rhs=VA[sl][:, g, 0:129],
                                      start=(bi == 0), stop=False),
                              reads=[R(PTbR[g][bi]), R("VA%d" % sl)], writes=[dr])
                    S.add('pe', I('matmul', out=dst, lhsT=PTm[pm][g][0:33, hh * 128:(hh + 1) * 128], rhs=Vm[0:33, g, 0:129], start=False, stop=True),
                          reads=[R(PTmR[pm][g]), R("Vm")], writes=[dr])
                l3 = PO[:, 0:387].rearrange("p (h c) -> p h c", h=3, c=129)[:, :, 128]
                S.add('dve', I('reciprocal', out=rl[:, 0:3], in_=l3), reads=[POr[0]], writes=[R("rl")])
                S.add('dve', I('reciprocal', out=rl[:, 3:4], in_=PO[:, 640:641]), reads=[POr[1], R("rl")], writes=[R("rl")])
                S.add('dve', I('tensor_scalar', out=rl[:], in0=rl[:], scalar1=0.5, scalar2=None, op0=ALU.mult), reads=[R("rl")], writes=[R("rl")])
                for hh in range(4):
                    h = g * 4 + hh
                    src_ = PO[:, hh * 129:hh * 129 + 128] if hh < 3 else PO[:, 512:640]
                    dr = POr[0] if hh < 3 else POr[1]
                    S.add('dve', I('scalar_tensor_tensor', out=o_g[:, h * 128:(h + 1) * 128], in0=src_, scalar=rl[:, hh:hh + 1],
                                   in1=g_r[pa_c][:, h * 128:(h + 1) * 128], op0=ALU.mult, op1=ALU.mult),
                          reads=[dr, R("rl"), R("g_r%d" % pa_c)], writes=[R("o_g")])
                if g == 0:
                    S.add('sp', I('dma_start', out=mixr[0][:], in_=mixrd[c * 128:(c + 1) * 128, :]), reads=[R("mixr_d")], writes=[R("mixr")], dma="mixr")
                yield
            tr_group(1)
            for nb_ in range(2):
                for kt in range(8):
                    S.add('pe', I('matmul', out=PO[:, nb_ * 512:(nb_ + 1) * 512], lhsT=o_gT[:, kt, :], rhs=W2[:, kt, nb_ * 512:(nb_ + 1) * 512],
                                  start=(kt == 0), stop=(kt == 7)),
                          reads=[R("o_gT"), R("W2")], writes=[POr[nb_]])
            for nb_ in range(2):
                cs_ = slice(nb_ * 512, (nb_ + 1) * 512)
                S.add('dve', I('scalar_tensor_tensor', out=mixa[:, cs_], in0=sg_r[pa_c][:, cs_], scalar=1.0, in1=PO[:, cs_], op0=ALU.add, op1=ALU.mult),
                      reads=[R("sg_r%d" % pa_c), POr[nb_]], writes=[R("S_f")])
            S.add('dve', I('tensor_tensor', out=mixb[:], in0=mixa[:, 0:1024], in1=mixr[0][:], op=ALU.add), reads=[R("S_f"), R("mixr")], writes=[R("o_n")])
            yield
            tr_to(mixb, "o_n", 8, S_fbf[0][:], "S_fbf0", scale=0.5)
            yield
            for nb_ in range(2):
                for kt in range(8):
                    S.add('pe', I('matmul', out=PO[:, nb_ * 512:(nb_ + 1) * 512], lhsT=mixT[:, kt, :], rhs=W3[:, kt, nb_ * 512:(nb_ + 1) * 512],
                                  start=(kt == 0), stop=(kt == 7)),
                          reads=[R("S_fbf0"), R("W3")], writes=[POr[nb_]])
            yield

        own_info = {}
        ctxs = {0: front_a(xm[0:128, :], csm[0:128, :], ring=4)}
        front_b(ctxs[0])
        for idx in range(a_steps):
            if idx + 1 < 19:
                ctxs[idx + 1] = front_a(xm[(idx + 1) * 128:(idx + 2) * 128, :], csm[(idx + 1) * 128:(idx + 2) * 128, :], ring=4)
            tail = None
            if idx >= 3:
                c_ = idx - 3
                post = (c_ - 1, own_info[c_ - 1][0]) if c_ >= 1 else None
                tail = a_tail(c_, *own_info[c_], post=post)
            hd = a_head(idx, ctxs[idx], ctxs.get(idx + 1))
            if PIPE_A:
                interleave(tail, hd, k=1, lead=2)
            else:
                interleave(None, hd)
                interleave(tail, None)

        if a_steps == 19:
            a_post(NOWN - 1, own_info[NOWN - 1][0])
        S.add('sp', None, extra=final_dma)
        S.build()
        print("sched stats", S.stats, flush=True)
    return nc


def _host_prep(inputs):
    x = np.asarray(inputs["x"], np.float32)
    meta = np.asarray(inputs["meta_tokens"], np.float32)
    import jax
    import jax.numpy as jnp
    _cpu = jax.devices("cpu")[0]
    with jax.default_device(_cpu):
        inv_j = 10000.0 ** (-jnp.arange(64, dtype=jnp.float32) * 2.0 / 128)

    def chunk_data(b, n):
        if n == 0:
            z = np.zeros((128, 1024), np.float32)
            z[112:] = meta
            return z
        if n > 64:
            return np.zeros((128, 1024), np.float32)
        return x[b, (n - 1) * 128:n * 128]

    def cs_of(pos):
        with jax.default_device(_cpu):
            ang = jnp.asarray(np.asarray(pos, np.float32))[:, None] * inv_j[None, :]
            return np.concatenate([np.asarray(jnp.cos(ang), np.float32), np.asarray(jnp.sin(ang), np.float32)], axis=1)

    def chunk_pos(n):
        return np.maximum(n * 128 + np.arange(128) - 112, 0)

    jj = np.arange(128)[:, None].astype(np.float64)
    ii = np.arange(128)[None, :].astype(np.float64)
    sc = 128.0 ** -0.5
    rtab = np.concatenate([np.maximum(ii - jj, 0), (ii >= jj) * sc, np.maximum(jj - ii, 0), (jj > ii) * sc], axis=1).astype(np.float32)
    ctab = np.concatenate([np.broadcast_to(ii + 1, (128, 128)), np.broadcast_to(128 - ii, (128, 128))], axis=1).astype(np.float32)
    jtab = np.concatenate([127 - jj, jj], axis=1).astype(np.float32)
    bf = ml_dtypes.bfloat16
    m_prev = np.where(jj >= ii, 0.0, NEG).astype(np.float32)
    m_next = np.where(jj <= ii, 0.0, NEG).astype(np.float32)
    m_all = np.full((128, 128), NEG, np.float32)
    ident = np.eye(128, dtype=np.float32).astype(bf)

    def bc(v, n):
        return np.ascontiguousarray(np.broadcast_to(np.asarray(v, np.float32).reshape(1, n), (128, n)))

    common = dict(
        w_in=np.ascontiguousarray(inputs["w_in"][0], dtype=np.float32),
        w_rb=np.ascontiguousarray(inputs["w_ret_branch"][0], dtype=np.float32),
        w_ab=np.ascontiguousarray(inputs["w_attn_branch"][0], dtype=np.float32),
        w_o=np.ascontiguousarray(inputs["w_out"][0], dtype=np.float32),
        prew=bc(inputs["pre_norm_w"][0], 1024), postw=bc(inputs["post_norm_w"][0], 1024), nwb=bc(inputs["ret_norm_w"][0], 1024),
        dec8=bc(np.concatenate([np.asarray(inputs["ret_decay_fwd"][0]), np.asarray(inputs["ret_decay_bwd"][0])]), 8),
        sink8=bc(inputs["attn_sink"][0], 8),
        rtab=rtab, ctab=ctab, jtab=jtab, ident=ident,
    )
    in_maps = []
    for core in range(8):
        b, s = divmod(core, 4)
        fwd = list(range(0, 16 * s + 1))
        bwd = list(range(64, 16 * s + 16, -1))
        own = [16 * s + 1 + c for c in range(15, 0, -1)]
        slots = fwd + bwd + own
        assert len(slots) == NSLOT
        xs = np.concatenate([chunk_data(b, n) for n in slots], axis=0)
        css = np.concatenate([cs_of(chunk_pos(n)) for n in slots], axis=0)
        actf = np.array([1.0] * len(fwd) + [0.0] * (NSLOT - len(fwd)), np.float32)
        acttab = bc(np.concatenate([actf, 1.0 - actf]), 128)
        metac = np.zeros((128, 1024), np.float32)
        metac[:16] = meta
        mpos = np.zeros(128)
        mpos[:16] = np.arange(16)
        mchunks = [16 * s] + [16 * s + 1 + c for c in range(16)] + [16 * s + 17]
        xm = np.concatenate([metac] + [chunk_data(b, n) for n in mchunks], axis=0)
        csm = np.concatenate([cs_of(mpos)] + [cs_of(chunk_pos(n)) for n in mchunks], axis=0)
        mk = [m_prev, m_next, m_all if s == 0 else m_prev, m_all if s == 3 else m_next]
        masks = np.concatenate([np.tile(m, (1, 4)) for m in mk], axis=1).astype(bf)
        d = dict(common)
        d.update(xs=xs, css=css, xm=xm, csm=csm, acttab=acttab, masks=masks)
        in_maps.append(d)
    return in_maps


_NC_CACHE = {}


def kernel(**inputs):
    in_maps = _host_prep(inputs)
    if 'nc' not in _NC_CACHE:
        _NC_CACHE['nc'] = build_nc()
    nc = _NC_CACHE['nc']
    res = run_bass_kernel_spmd(nc, in_maps, core_ids=list(range(8)))
    out = np.empty((2, SEQ, D_MODEL), np.float32)
    for core in range(8):
        b, s = divmod(core, 4)
        out[b, s * 2048:(s + 1) * 2048] = np.asarray(res.results[core]["out"], np.float32)
    return out
```

```python
import contextlib
import numpy as np
import ml_dtypes
import concourse.bass as bass
import concourse.mybir as mybir
from concourse.bass_utils import run_bass_kernel_spmd

F32 = mybir.dt.float32
BF16 = mybir.dt.bfloat16
AF = mybir.ActivationFunctionType
ALU = mybir.AluOpType

SAME_ENGINE_SYNC = True
DEFER_W = True
PIPE_A = True
ENGS = ['pe', 'dve', 'act', 'pool', 'sp']

D_MODEL = 1024
SEQ = 8192
N_META = 16
NSLOT = 64
NOWN = 16
EPS = 1e-6
SCALE = 128 ** -0.5
LN_SCALE = float(-0.5 * np.log(128.0))
NEG = -30000.0


class Res:
    __slots__ = ('name', 'last_w', 'readers', 'excl')

    def __init__(self, name):
        self.name = name
        self.last_w = None
        self.readers = []
        self.excl = False


class Sched:
    def __init__(self, nc):
        self.nc = nc
        self.ops = []

    def add(self, eng, fn, reads=(), writes=(), dma=None, extra=()):
        self.ops.append((eng, fn, tuple(reads), tuple(writes), dma, tuple(extra)))
        return len(self.ops) - 1

    def build(self):
        nc = self.nc
        last = {}
        for i, op in enumerate(self.ops):
            if op[4] is not None:
                last[op[4]] = i
        self.add('sp', None, extra=sorted(last.values()))
        ops = self.ops
        n = len(ops)
        deps = [None] * n
        signaled = [False] * n
        for i, (eng, fn, reads, writes, dk, extra) in enumerate(ops):
            d = set(extra)
            for r in reads:
                if r.last_w is not None:
                    d.add(r.last_w)
                if r.excl:
                    d.update(j for j in r.readers if ops[j][0] != eng)
            for w in writes:
                if w.last_w is not None:
                    d.add(w.last_w)
                d.update(w.readers)
            d.discard(i)
            dd = set()
            latest = {}
            for j in d:
                ej = ops[j][0]
                dmaj = ops[j][4] is not None
                if dmaj and dk is not None and ops[j][4] == dk and dk.startswith('T:'):
                    continue
                if (not dmaj) and ej == eng and dk is None and (eng == 'pe' or not SAME_ENGINE_SYNC):
                    continue
                if dmaj:
                    dd.add(j)
                else:
                    latest[ej] = max(latest.get(ej, -1), j)
            dd.update(latest.values())
            deps[i] = dd
            for j in dd:
                signaled[j] = True
            for r in reads:
                r.readers.append(i)
            for w in writes:
                w.last_w = i
                w.readers = []
        cnt = {e: 0 for e in ENGS}
        val = [0] * n
        dmacnt = {}
        for i, op in enumerate(ops):
            if op[4] is not None:
                dmacnt[op[4]] = dmacnt.get(op[4], 0) + 1
                val[i] = 16 * dmacnt[op[4]]
            elif signaled[i]:
                cnt[op[0]] += 1
                val[i] = cnt[op[0]]
        for i, op in enumerate(ops):
            if op[4] is not None and op[4].startswith('T:'):
                val[i] = 16 * dmacnt[op[4]]
        self.stats = dict(n_ops=n, milestones=dict(cnt), dma_keys=len(dmacnt))
        with contextlib.ExitStack() as st:
            sem_eng = {e: st.enter_context(nc.semaphore('s_' + e)) for e in ENGS}
            dma_sem = {k: st.enter_context(nc.semaphore('d_' + k.replace(':', '_'))) for k in dmacnt}

            def emit(engname, e):
                waited = {}
                for i, op in enumerate(ops):
                    if op[0] != engname:
                        continue
                    need = {}
                    for j in deps[i]:
                        if ops[j][4] is not None:
                            key = ('d', ops[j][4])
                            sem = dma_sem[ops[j][4]]
                        else:
                            key = ('e', ops[j][0])
                            sem = sem_eng[ops[j][0]]
                        if val[j] > need.get(key, (None, 0))[1]:
                            need[key] = (sem, val[j])
                    for key, (sem, v) in need.items():
                        if waited.get(key, 0) >= v:
                            continue
                        e.wait_ge(sem, v)
                        waited[key] = v
                    if op[1] is None:
                        continue
                    inst = op[1](e)
                    if op[4] is not None:
                        inst.then_inc(dma_sem[op[4]], 16)
                    elif signaled[i]:
                        inst.then_inc(sem_eng[engname], 1)

            with nc.Block() as block:
                @block.tensor
                def _(e):
                    emit('pe', e)

                @block.vector
                def _(e):
                    emit('dve', e)

                @block.scalar
                def _(e):
                    emit('act', e)

                @block.gpsimd
                def _(e):
                    emit('pool', e)

                @block.sync
                def _(e):
                    emit('sp', e)


def I(name, *a, **kw):
    return lambda e: getattr(e, name)(*a, **kw)


def build_nc(debug=False, stop_after=None, a_steps=19):
    nc = bass.Bass("TRN2", target_bir_lowering=False)

    def din(name, shape, dt=F32):
        return nc.dram_tensor(name, shape, dt, kind="ExternalInput").ap()

    xs = din("xs", [NSLOT * 128, 1024])
    css = din("css", [NSLOT * 128, 128])
    xm = din("xm", [19 * 128, 1024])
    csm = din("csm", [19 * 128, 128])
    w_in = din("w_in", [1024, 7680])
    w_rb = din("w_rb", [1024, 1024])
    w_ab = din("w_ab", [1024, 1024])
    w_o = din("w_o", [1024, 1024])
    prew = din("prew", [128, 1024])
    postw = din("postw", [128, 1024])
    nwb = din("nwb", [128, 1024])
    dec8 = din("dec8", [128, 8])
    sink8 = din("sink8", [128, 8])
    rtab = din("rtab", [128, 512])
    ctab = din("ctab", [128, 256])
    jtab = din("jtab", [128, 2])
    acttab = din("acttab", [128, 128])
    masks = din("masks", [128, 4 * 512], BF16)
    identd = din("ident", [128, 128], BF16)
    out = nc.dram_tensor("out", [NOWN * 128, 1024], F32, kind="ExternalOutput").ap()
    sbst = nc.dram_tensor("sbst", [NOWN * 128, 1024], BF16, kind="Internal").ap()
    mixrd = nc.dram_tensor("mixrd", [NOWN * 128, 1024], BF16, kind="Internal").ap()
    wbf = nc.dram_tensor("wbf", [1024, 3584], BF16, kind="Internal").ap()
    wabf = nc.dram_tensor("wabf", [1024, 1024], BF16, kind="Internal").ap()
    dbg = {}
    if debug:
        dbg['sf'] = nc.dram_tensor("dbg_sf", [128, 1024], F32, kind="ExternalOutput").ap()
        dbg['sb'] = nc.dram_tensor("dbg_sb", [128, 1024], F32, kind="ExternalOutput").ap()
        dbg['mixr'] = nc.dram_tensor("dbg_mixr", [NOWN * 128, 1024], BF16, kind="ExternalOutput").ap()

    with contextlib.ExitStack() as st:
        RES = {}

        def R(name):
            if name not in RES:
                RES[name] = Res(name)
            return RES[name]

        def sb(name, shape, dt):
            t = st.enter_context(nc.sbuf_tensor(name, shape, dt))
            R(name)
            return t

        def ps(name, shape, dt):
            return st.enter_context(nc.psum_tensor(name, shape, dt))

        S = Sched(nc)

        WW = sb("WW", [128, 8, 4096], BF16)
        W2 = sb("W2", [128, 8, 1024], BF16)
        W3 = sb("W3", [128, 8, 1024], BF16)
        prew_t = sb("prew_t", [128, 1024], F32)
        nw_t = sb("nw_t", [128, 1024], F32)
        ident = sb("ident_t", [128, 128], BF16)
        mask_t = sb("mask_t", [128, 4 * 512], BF16)
        dec_t = sb("dec_t", [128, 8], F32)
        sink_t = sb("sink_t", [128, 8], F32)
        jtab_t = sb("jtab_t", [128, 2], F32)
        act_t = sb("act_t", [128, 128], F32)
        e8 = sb("e8", [128, 8], F32)
        lg8 = sb("lg8", [128, 8], F32)
        lg128 = sb("lg128", [128, 8], F32)
        DT = sb("DT", [128, 512], BF16)
        QDF = sb("QDF", [128, 512], F32)
        QDB = sb("QDB", [128, 512], F32)
        KD8 = sb("KD8", [128, 8], F32)
        CD8 = sb("CD8", [128, 8], F32)
        KDSf = sb("KDSf", [128, 256], F32)
        KDSb = sb("KDSb", [128, 256], F32)
        AFt = sb("AFt", [128, 256], F32)
        ABt = sb("ABt", [128, 256], F32)
        zrow = sb("zrow", [128, 128], F32)
        S_f = sb("S_f", [128, 1024], F32)
        S_b = sb("S_b", [128, 1024], F32)
        S_fbf1 = sb("S_fbf0", [128, 1024], BF16)
        S_fbf = [S_fbf1, S_fbf1]
        sbst2 = sb("sbst2", [128, 2048], BF16)
        sbstg = [sbst2[:, 0:1024], sbst2[:, 1024:2048]]
        R("sbstg0")
        R("sbstg1")
        XB = [sb("XB%d" % i, [128, 1024], F32) for i in range(3)]
        XB.append(sbst2.bitcast(F32))
        R("XB3")
        CS = [sb("CS%d" % i, [128, 128], F32) for i in range(4)]
        ssq = [sb("ssq%d" % i, [128, 1], F32) for i in range(2)]
        vv = [sb("vv%d" % i, [128, 1], F32) for i in range(2)]
        rstd = [sb("rstd%d" % i, [128, 1], F32) for i in range(2)]
        mhalf = sb("mhalf", [128, 8], F32)
        u0 = sb("u0", [128, 1024], F32)
        ub = [sb("ub%d" % i, [128, 1024], BF16) for i in range(2)]
        uT = [sb("uT%d" % i, [128, 8, 128], BF16) for i in range(2)]
        rotA = [sb("rotA%d" % i, [128, 512], F32) for i in range(2)]
        rotB = [sb("rotB%d" % i, [128, 512], F32) for i in range(2)]
        k_rot = [sb("k_rot%d" % i, [128, 512], BF16) for i in range(2)]
        q_rot1 = sb("q_rot", [128, 1024], BF16)
        q_rot = [q_rot1, q_rot1]
        kdf = [sb("kdf%d" % i, [128, 512], BF16) for i in range(2)]
        kdb = [sb("kdb%d" % i, [128, 512], BF16) for i in range(2)]
        v_tok = [sb("v_tok%d" % i, [128, 1024], BF16) for i in range(2)]
        qT = [sb("qT%d" % i, [128, 512], BF16) for i in range(2)]
        kT1 = sb("kT", [128, 512], BF16)
        qdf1 = sb("qdf", [128, 512], BF16)
        qdb1 = sb("qdb", [128, 512], BF16)
        sd1 = sb("sd", [128, 512], BF16)
        kT2 = sb("kT2", [128, 512], BF16)
        kT, qdf, qdb, sd = [kT1, kT2], [qdf1] * 2, [qdb1] * 2, [sd1] * 2
        kTR = ["kT", "kT2"]
        g_r = [sb("g_r%d" % i, [128, 1024], BF16) for i in range(2)]
        sg_r = [sb("sg_r%d" % i, [128, 1024], BF16) for i in range(2)]
        gnst = sb("gnst", [128, 24], F32)
        gnmv = sb("gnmv", [128, 8], F32)
        gnve = sb("gnve", [128, 4], F32)
        gnrs = sb("gnrs", [128, 4], F32)
        gnnm = sb("gnnm", [128, 4], F32)
        o_n = sb("o_n", [128, 1024], BF16)
        o_g = sb("o_g", [128, 1024], BF16)
        o_gT = sb("o_gT", [128, 8, 128], BF16)
        mixr1 = sb("mixr", [128, 1024], BF16)
        mixr = [mixr1, mixr1]
        KT = [sb("KT%d" % i, [128, 2, 128], BF16) for i in range(4)]
        VA = [sb("VA%d" % i, [128, 2, 130], BF16) for i in range(4)]
        KTm = sb("KTm", [128, 2, 128], BF16)
        Vm = sb("Vm", [128, 2, 130], BF16)
        PTb = [[kdf[0], kdf[1], kdb[0]], [kdb[1], qT[0], qT[1]]]
        PTbR = [["kdf0", "kdf1", "kdb0"], ["kdb1", "qT0", "qT1"]]
        PTm = [[kT1, qdf1], [qdb1, sd1]]
        PTmR = [["kT", "qdf"], ["qdb", "sd"]]
        th_t = QDF
        rl = sb("rl", [128, 4], F32)
        mixa = S_f.bitcast(BF16)
        mixb = o_n
        mixT = S_fbf[0][:].rearrange("p (a b) -> p a b", a=8, b=128)
        tfin = S_b

        rtab_t = rotA[0]
        RES["rtab_t"] = RES["rotA0"]
        ctab_t = u0[:, 256:512]
        tmpA = u0[:, 0:256]
        tmp1 = u0[:, 512:640]
        tmp2 = u0[:, 640:768]
        for nm_ in ("ctab_t", "tmpA", "tmp1", "tmp2"):
            RES[nm_] = RES["u0"]
        print("sbuf remaining", nc.sbuf_bytes_remaining() if callable(getattr(nc, 'sbuf_bytes_remaining', None)) else nc.sbuf_bytes_remaining, flush=True)
        PA = ps("PA", [128, 1024], F32)
        PTr = ps("PTr", [128, 2048], BF16)
        PS = ps("PS", [128, 1024], F32)
        PO = ps("PO", [128, 1024], F32)
        for nm in ["PA0", "PA1", "PT0", "PT1", "PS0", "PS1", "PO0", "PO1", "sbst_d", "mixr_d"]:
            R(nm)
        for nm in ["PA0", "PA1", "PT0", "PT1", "PS0", "PS1", "PO0", "PO1"]:
            R(nm).excl = True
        PTf = PTr.bitcast(F32)
        PA_all = {
            'state': [(PA[:, 0:512], R("PA0")), (PA[:, 512:1024], R("PA1"))],
            'main': [(PA[:, 0:512], R("PA0")), (PA[:, 512:1024], R("PA1"))],
        }
        PT_all = {
            'state': [(PTr[:, 0:1024], R("PT0")), (PTr[:, 1024:2048], R("PT1"))],
            'main': [(PTr[:, 0:1024], R("PT0")), (PTr[:, 1024:2048], R("PT1"))],
        }
        mode = ['state']
        cnt = dict(pa=0, pt=0, x=0, xa=0, par=0)

        def nxt(key, mod):
            v = cnt[key]
            cnt[key] = v + 1
            return v % mod

        def ld(dst, src, rname, q='sp', key='T:setup'):
            S.add(q, I('dma_start', out=dst, in_=src), writes=[R(rname)], dma=key)

        ld(dec_t[:], dec8, "dec_t")
        ld(sink_t[:], sink8, "sink_t")
        ld(rtab_t[:], rtab, "rtab_t")
        ld(ctab_t, ctab, "ctab_t")
        ld(jtab_t[:], jtab, "jtab_t")
        ld(act_t[:], acttab, "act_t")
        ld(ident[:], identd, "ident_t")
        ld(mask_t[:], masks, "mask_t", key='T:setup2')
        ld(prew_t[:], prew, "prew_t")
        ld(nw_t[:], nwb, "nw_t", key='T:setup2')

        def wload(dst_tile, rname, src, c0, n, d0, key):
            rn = rname if isinstance(rname, (list, tuple)) else [rname]
            for kt in range(8):
                S.add('pool', I('dma_start', out=dst_tile[:, kt, d0:d0 + n], in_=src[kt * 128:(kt + 1) * 128, c0:c0 + n]),
                      writes=[R(r_) for r_ in rn], dma=key)

        wload(WW, "WW", w_in, 512, 512, 512, 'T:w1')
        wload(WW, "WWv", w_in, 1024, 1024, 1024, 'T:w1v')

        S.add('pool', I('memset', mhalf[:], -0.5), writes=[R("mhalf")])
        S.add('pool', I('memset', zrow[:], 0.0), writes=[R("zrow")])
        S.add('pool', I('memset', S_f[:], 0.0), writes=[R("S_f")])
        S.add('pool', I('memset', S_b[:], 0.0), writes=[R("S_b")])
        for i in range(4):
            S.add('pool', I('memset', VA[i][:], 1.0), writes=[R("VA%d" % i)])
        S.add('pool', I('memset', Vm[:], 0.0), writes=[R("Vm")])
        S.add('pool', I('memset', Vm[0:16, :, 128:129], 1.0), writes=[R("Vm")])
        S.add('pool', I('memset', Vm[32:33, :, 128:129], 1.0), writes=[R("Vm")])
        S.add('act', I('activation', out=e8[:], in_=dec_t[:], func=AF.Exp, scale=-1.0), reads=[R("dec_t")], writes=[R("e8")])
        S.add('act', I('activation', out=e8[:], in_=e8[:], func=AF.Ln, bias=1.0), reads=[R("e8")], writes=[R("e8")])
        S.add('dve', I('tensor_scalar', out=lg8[:], in0=e8[:], scalar1=-1.0, scalar2=None, op0=ALU.mult), reads=[R("e8")], writes=[R("lg8")])
        S.add('dve', I('tensor_scalar', out=lg128[:], in0=e8[:], scalar1=-128.0, scalar2=None, op0=ALU.mult), reads=[R("e8")], writes=[R("lg128")])
        for h in range(4):
            hs = slice(h * 128, (h + 1) * 128)
            S.add('act', I('activation', out=tmp1, in_=rtab_t[:, 0:128], func=AF.Exp, scale=lg8[:, h:h + 1]),
                  reads=[R("rtab_t"), R("lg8")], writes=[R("tmp1")])
            S.add('act', I('activation', out=tmp2, in_=rtab_t[:, 256:384], func=AF.Exp, scale=lg8[:, 4 + h:5 + h]),
                  reads=[R("rtab_t"), R("lg8")], writes=[R("tmp2")])
            S.add('dve', I('tensor_tensor', out=tmp1, in0=tmp1, in1=rtab_t[:, 128:256], op=ALU.mult), reads=[R("tmp1"), R("rtab_t")], writes=[R("tmp1")])
            S.add('dve', I('tensor_tensor', out=tmp2, in0=tmp2, in1=rtab_t[:, 384:512], op=ALU.mult), reads=[R("tmp2"), R("rtab_t")], writes=[R("tmp2")])
            S.add('dve', I('tensor_tensor', out=DT[:, hs], in0=tmp1, in1=tmp2, op=ALU.add), reads=[R("tmp1"), R("tmp2")], writes=[R("DT")])
            S.add('act', I('activation', out=QDF[:, hs], in_=ctab_t[:, 0:128], func=AF.Exp, scale=lg8[:, h:h + 1]),
                  reads=[R("ctab_t"), R("lg8")], writes=[R("QDF")])
            S.add('act', I('activation', out=QDB[:, hs], in_=ctab_t[:, 128:256], func=AF.Exp, scale=lg8[:, 4 + h:5 + h]),
                  reads=[R("ctab_t"), R("lg8")], writes=[R("QDB")])
            S.add('act', I('activation', out=KD8[:, h:h + 1], in_=jtab_t[:, 0:1], func=AF.Exp, scale=lg8[:, h:h + 1], bias=LN_SCALE),
                  reads=[R("jtab_t"), R("lg8")], writes=[R("KD8")])
            S.add('act', I('activation', out=KD8[:, 4 + h:5 + h], in_=jtab_t[:, 1:2], func=AF.Exp, scale=lg8[:, 4 + h:5 + h], bias=LN_SCALE),
                  reads=[R("jtab_t"), R("lg8")], writes=[R("KD8")])
        S.add('act', I('activation', out=CD8[:], in_=lg128[:], func=AF.Exp), reads=[R("lg128")], writes=[R("CD8")])

        def bc_slot(tile, off):
            return bass.AP(tile, off, [[tile.shape[1], 128], [0, NSLOT], [1, 4]])

        def bc_head(tile, off):
            return bass.AP(tile, off, [[tile.shape[1], 128], [1, NSLOT], [0, 4]])

        def v3(tile):
            ap_ = tile if isinstance(tile, bass.AP) else tile[:]
            return ap_.rearrange("p (s h) -> p s h", s=NSLOT, h=4)

        S.add('dve', I('tensor_tensor', out=v3(KDSf), in0=bc_slot(KD8, 0), in1=bc_head(act_t, 0), op=ALU.mult),
              reads=[R("KD8"), R("act_t")], writes=[R("KDSf")])
        S.add('dve', I('tensor_tensor', out=v3(KDSb), in0=bc_slot(KD8, 4), in1=bc_head(act_t, 64), op=ALU.mult),
              reads=[R("KD8"), R("act_t")], writes=[R("KDSb")])
        S.add('dve', I('tensor_tensor', out=v3(tmpA), in0=bc_slot(lg128, 0), in1=bc_head(act_t, 0), op=ALU.mult),
              reads=[R("lg128"), R("act_t")], writes=[R("tmpA")])
        S.add('act', I('activation', out=AFt[:], in_=tmpA, func=AF.Exp), reads=[R("tmpA")], writes=[R("AFt")])
        S.add('dve', I('tensor_tensor', out=v3(tmpA), in0=bc_slot(lg128, 4), in1=bc_head(act_t, 64), op=ALU.mult),
              reads=[R("lg128"), R("act_t"), R("AFt")], writes=[R("tmpA")])
        S.add('act', I('activation', out=ABt[:], in_=tmpA, func=AF.Exp), reads=[R("tmpA")], writes=[R("ABt")])
        deferred = []

        def wload_def(dst_tile, rname, src, c0, n, d0, key):
            for kt in range(8):
                deferred.append(lambda kt=kt: S.add('pool', I('dma_start', out=dst_tile[:, kt, d0:d0 + n], in_=src[kt * 128:(kt + 1) * 128, c0:c0 + n]),
                                                   writes=[R(rname)], dma=key))

        wload_def(WW, "WWq", w_in, 0, 512, 0, 'T:w2')
        wload_def(WW, "WWq", w_in, 2048, 1024, 2048, 'T:w2')
        wload_def(WW, "WWq", w_in, 5632, 1024, 3072, 'T:w2')
        wload_def(W2, "W2", w_rb, 0, 1024, 0, 'T:w2')
        wload_def(W3, "W3", w_o, 0, 1024, 0, 'T:w3')

        def precast_def(dst, rname, src, c0, n, d0, key):
            for kt in range(8):
                deferred.append(lambda kt=kt: S.add('pool', I('dma_start', out=dst[kt * 128:(kt + 1) * 128, d0:d0 + n], in_=src[kt * 128:(kt + 1) * 128, c0:c0 + n]),
                                                   writes=[R(rname)], dma=key))

        precast_def(wbf, "wbf_kv", w_in, 4096, 512, 1024, 'T:pc1')
        precast_def(wbf, "wbf_q", w_in, 3072, 1024, 0, 'T:pc2')
        precast_def(wbf, "wbf_g", w_in, 4608, 1024, 1536, 'T:pc3')
        precast_def(wbf, "wbf_s", w_in, 6656, 1024, 2560, 'T:pc4')
        precast_def(wabf, "wabf", w_ab, 0, 1024, 0, 'T:pc5')
        if not DEFER_W:
            while deferred:
                deferred.pop(0)()

        def interleave(tail, filler, k=1, lead=0):
            t_done = tail is None
            f_done = filler is None

            def step(gen):
                try:
                    next(gen)
                    return False
                except StopIteration:
                    return True
            for _ in range(lead):
                if not f_done:
                    f_done = step(filler)
            while not (t_done and f_done):
                if not t_done:
                    t_done = step(tail)
                for _ in range(k):
                    if not f_done:
                        f_done = step(filler)

        def front_a(xsrc, cssrc, ring=3):
            xi = nxt('x' if ring == 3 else 'xa', ring)
            par = nxt('par', 2)
            xb, cs = XB[xi], CS[xi]
            S.add('sp', I('dma_start', out=xb[:], in_=xsrc), writes=[R("XB%d" % xi)] + ([R("sbstg0"), R("sbstg1")] if xi == 3 else []), dma='x%d' % xi)
            S.add('sp', I('dma_start', out=cs[:], in_=cssrc), writes=[R("CS%d" % xi)], dma='c%d' % xi)
            S.add('act', I('activation', out=ub[par][:], in_=xb[:], func=AF.Square, accum_out=ssq[par][:]),
                  reads=[R("XB%d" % xi)], writes=[R("ub%d" % par), R("ssq%d" % par)])
            S.add('dve', I('tensor_scalar', out=vv[par][:], in0=ssq[par][:], scalar1=1.0 / D_MODEL, scalar2=EPS, op0=ALU.mult, op1=ALU.add),
                  reads=[R("ssq%d" % par)], writes=[R("vv%d" % par)])
            S.add('pool', I('tensor_tensor', out=rstd[par][:], in0=vv[par][:], in1=mhalf[:, 0:1], op=ALU.pow),
                  reads=[R("vv%d" % par), R("mhalf")], writes=[R("rstd%d" % par)])
            S.add('pool', I('tensor_tensor', out=u0[:], in0=xb[:], in1=prew_t[:], op=ALU.mult),
                  reads=[R("XB%d" % xi), R("prew_t")], writes=[R("u0")])
            S.add('act', I('activation', out=ub[par][:], in_=u0[:], func=AF.Copy, scale=rstd[par][:]),
                  reads=[R("u0"), R("rstd%d" % par)], writes=[R("ub%d" % par)])
            return dict(xi=xi, par=par)

        def front_b(ctx):
            par = ctx['par']
            tr_to(ub[par], "ub%d" % par, 8, uT[par][:].rearrange("p a b -> p (a b)"), "uT%d" % par)

        def tr_to(src_tile, src_res, ntile, dst_ap, dst_res, scale=None, src_off=0):
            pts = PT_all[mode[0]]
            pt, ptr = pts[nxt('pt', len(pts))]
            for t in range(ntile):
                S.add('pe', I('transpose', out=pt[:, t * 128:(t + 1) * 128], in_=src_tile[:, src_off + t * 128:src_off + (t + 1) * 128], identity=ident[:]),
                      reads=[R(src_res), R("ident_t")], writes=[ptr])
            if scale is None:
                S.add('act', I('activation', out=dst_ap, in_=pt[:, 0:ntile * 128], func=AF.Copy), reads=[ptr], writes=[R(dst_res)])
            else:
                S.add('act', I('activation', out=dst_ap, in_=pt[:, 0:ntile * 128], func=AF.Copy, scale=scale), reads=[ptr], writes=[R(dst_res)])

        def proj(par, wt, wres, c0, n=512):
            pas = PA_all[mode[0]]
            bank, bres = pas[nxt('pa', len(pas))]
            for kt in range(8):
                S.add('pe', I('matmul', out=bank[:, 0:n], lhsT=uT[par][:, kt, :], rhs=wt[:, kt, c0:c0 + n], start=(kt == 0), stop=(kt == 7)),
                      reads=[R("uT%d" % par)] + [R(r_) for r_ in (wres if isinstance(wres, (list, tuple)) else [wres])], writes=[bres])
            return bank[:, 0:n], bres

        def rotary(zb, zres, cs, csres, nh, dst_ap, dst_res, ri):
            n = nh * 128
            zv = zb.rearrange("p (h t f) -> p h t f", h=nh, t=2, f=64)
            A = rotA[ri][:, 0:n].rearrange("p (h t f) -> p h t f", h=nh, t=2, f=64)
            B = rotB[ri][:, 0:n].rearrange("p (h t f) -> p h t f", h=nh, t=2, f=64)
            dv = dst_ap.rearrange("p (h t f) -> p h t f", h=nh, t=2, f=64)
            cosb = bass.AP(cs, 0, [[128, 128], [0, nh], [0, 2], [1, 64]])
            sinb = bass.AP(cs, 64, [[128, 128], [0, nh], [1, 64]])
            rA, rB = R("rotA%d" % ri), R("rotB%d" % ri)
            S.add('dve', I('tensor_tensor', out=A, in0=zv, in1=cosb, op=ALU.mult), reads=[zres, R(csres)], writes=[rA])
            S.add('dve', I('tensor_tensor', out=B[:, :, 0, :], in0=zv[:, :, 1, :], in1=sinb, op=ALU.mult), reads=[zres, R(csres)], writes=[rB])
            S.add('dve', I('tensor_tensor', out=B[:, :, 1, :], in0=zv[:, :, 0, :], in1=sinb, op=ALU.mult), reads=[zres, R(csres)], writes=[rB])
            S.add('pool', I('tensor_tensor', out=dv[:, :, 0, :], in0=A[:, :, 0, :], in1=B[:, :, 0, :], op=ALU.subtract), reads=[rA, rB], writes=[R(dst_res)])
            S.add('pool', I('tensor_tensor', out=dv[:, :, 1, :], in0=A[:, :, 1, :], in1=B[:, :, 1, :], op=ALU.add), reads=[rA, rB], writes=[R(dst_res)])

        def bc_d(tile, off):
            return bass.AP(tile, off, [[tile.shape[1], 128], [1, 4], [0, 128]])

        def h3(ap):
            return ap.rearrange("p (h d) -> p h d", h=4, d=128)

        def kv_update(par, kd_tile, kd_res, bank_tile, bres, Stile, Sres, coef_tile, coef_off):
            for h in range(4):
                S.add('pe', I('matmul', out=bank_tile[:, h * 256:(h + 1) * 256], lhsT=kd_tile[:, h * 128:(h + 1) * 128],
                              rhs=v_tok[par][:, h * 256:(h + 1) * 256], start=True, stop=True),
                      reads=[R(kd_res), R("v_tok%d" % par)], writes=[bres[h // 2]])
            for h in range(4):
                S.add('dve', I('scalar_tensor_tensor',
                               out=Stile[:, h * 256:(h + 1) * 256], in0=Stile[:, h * 256:(h + 1) * 256], scalar=coef_tile[:, coef_off + h:coef_off + h + 1],
                               in1=bank_tile[:, h * 256:(h + 1) * 256], op0=ALU.mult, op1=ALU.add),
                      reads=[R(Sres), bres[h // 2], R(coef_tile.name)], writes=[R(Sres)])

        PSr = [R("PS0"), R("PS1")]
        POr = [R("PO0"), R("PO1")]
        final_dma = []

        nstg = [0]

        def st_head(t, ctx):
            xi, par = ctx['xi'], ctx['par']
            zk, zkr = proj(par, WW, "WW", 512)
            rotary(zk, zkr, CS[xi], "CS%d" % xi, 4, k_rot[par][:], "k_rot%d" % par, par)
            yield
            for hb in range(2):
                zv_, zvr = proj(par, WW, "WWv", 1024 + hb * 512)
                S.add('act', I('activation', out=v_tok[par][:, hb * 512:(hb + 1) * 512], in_=zv_, func=AF.Copy),
                      reads=[zvr], writes=[R("v_tok%d" % par)])
                if hb == 0:
                    S.add('dve', I('tensor_tensor', out=h3(kdf[par][:]), in0=h3(k_rot[par][:]), in1=bc_d(KDSf, t * 4), op=ALU.mult),
                          reads=[R("k_rot%d" % par), R("KDSf")], writes=[R("kdf%d" % par)])
                    S.add('pool', I('tensor_tensor', out=h3(kdb[par][:]), in0=h3(k_rot[par][:]), in1=bc_d(KDSb, t * 4), op=ALU.mult),
                          reads=[R("k_rot%d" % par), R("KDSb")], writes=[R("kdb%d" % par)])
                yield

        def st_tail(t, ctx):
            par = ctx['par']
            if t >= 49:
                c = NSLOT - t
                sg = sbstg[nstg[0] % 2]
                sgr = "sbstg%d" % (nstg[0] % 2)
                nstg[0] += 1
                S.add('act', I('activation', out=sg[:], in_=S_b[:], func=AF.Copy), reads=[R("S_b")], writes=[R(sgr)])
                S.add('sp', I('dma_start', out=sbst[c * 128:(c + 1) * 128, :], in_=sg[:]), reads=[R(sgr)], writes=[R("sbst_d")], dma=sgr)
            kv_update(par, kdf[par], "kdf%d" % par, PS, PSr, S_f, "S_f", AFt, t * 4)
            yield
            kv_update(par, kdb[par], "kdb%d" % par, PO, POr, S_b, "S_b", ABt, t * 4)
            yield

        ctxs = {0: front_a(xs[0:128, :], css[0:128, :])}
        front_b(ctxs[0])
        prev_tail = None
        for t in range(NSLOT):
            if t + 1 < NSLOT:
                ctxs[t + 1] = front_a(xs[(t + 1) * 128:(t + 2) * 128, :], css[(t + 1) * 128:(t + 2) * 128, :])
            for _ in range(2):
                if deferred and DEFER_W:
                    deferred.pop(0)()
            interleave(prev_tail, st_head(t, ctxs[t]), k=2)
            if t + 1 < NSLOT:
                front_b(ctxs[t + 1])
            prev_tail = st_tail(t, ctxs[t])
        interleave(prev_tail, None)
        while deferred:
            deferred.pop(0)()
        sg = sbstg[nstg[0] % 2]
        sgr = "sbstg%d" % (nstg[0] % 2)
        S.add('act', I('activation', out=sg[:], in_=S_b[:], func=AF.Copy), reads=[R("S_b")], writes=[R(sgr)])
        S.add('sp', I('dma_start', out=sbst[0:128, :], in_=sg[:]), reads=[R(sgr)], writes=[R("sbst_d")], dma=sgr)
        S.add('act', I('activation', out=S_fbf[0][:], in_=S_f[:], func=AF.Copy), reads=[R("S_f")], writes=[R("S_fbf0")])
        if debug:
            final_dma.append(S.add('sp', I('dma_start', out=dbg['sf'], in_=S_f[:]), reads=[R("S_f")], dma='dbgsf'))
            final_dma.append(S.add('sp', I('dma_start', out=dbg['sb'], in_=S_b[:]), reads=[R("S_b")], dma='dbgsb'))

        if stop_after == 'state':
            S.add('sp', None, extra=final_dma)
            S.build()
            return nc
        mode[0] = 'main'

        def r_front(c):
            return front_a(xm[(2 + c) * 128:(3 + c) * 128, :], csm[(2 + c) * 128:(3 + c) * 128, :])

        def load_sb(c):
            sbc = sbstg[c % 2]
            sbr = "sbstg%d" % (c % 2)
            S.add('sp', I('dma_start', out=sbc[:], in_=sbst[c * 128:(c + 1) * 128, :]), reads=[R("sbst_d")], writes=[R(sbr)], dma=sbr)

        def r_head(c, ctx):
            xi, par = ctx['xi'], ctx['par']
            zq, zqr = proj(par, WW, "WWq", 0)
            rotary(zq, zqr, CS[xi], "CS%d" % xi, 4, q_rot[par][:, 0:512], "q_rot", 0)
            yield
            zk, zkr = proj(par, WW, "WW", 512)
            rotary(zk, zkr, CS[xi], "CS%d" % xi, 4, k_rot[par][:], "k_rot%d" % par, 1)
            tr_to(q_rot[par], "q_rot", 4, qT[par][:], "qT%d" % par)
            yield
            for hb in range(2):
                zv_, zvr = proj(par, WW, "WWv", 1024 + hb * 512)
                S.add('act', I('activation', out=v_tok[par][:, hb * 512:(hb + 1) * 512], in_=zv_, func=AF.Copy),
                      reads=[zvr], writes=[R("v_tok%d" % par)])
                if hb == 0:
                    tr_to(k_rot[par], "k_rot%d" % par, 4, kT[par][:], kTR[par])
                    S.add('dve', I('tensor_tensor', out=h3(kdf[par][:]), in0=h3(k_rot[par][:]), in1=bc_d(KD8, 0), op=ALU.mult),
                          reads=[R("k_rot%d" % par), R("KD8")], writes=[R("kdf%d" % par)])
                yield
            for hb in range(2):
                zg, zgr = proj(par, WW, "WWq", 2048 + hb * 512)
                S.add('act', I('activation', out=g_r[par][:, hb * 512:(hb + 1) * 512], in_=zg, func=AF.Silu),
                      reads=[zgr], writes=[R("g_r%d" % par)])
                yield
            S.add('pool', I('tensor_tensor', out=g_r[par][:], in0=g_r[par][:], in1=nw_t[:], op=ALU.mult),
                  reads=[R("g_r%d" % par), R("nw_t")], writes=[R("g_r%d" % par)])
            for hb in range(2):
                zg, zgr = proj(par, WW, "WWq", 3072 + hb * 512)
                S.add('act', I('activation', out=sg_r[par][:, hb * 512:(hb + 1) * 512], in_=zg, func=AF.Tanh, scale=0.5),
                      reads=[zgr], writes=[R("sg_r%d" % par)])
                yield

        def r_tail_a(c, ctx):
            par = ctx['par']
            sbc = sbstg[c % 2]
            sbr = "sbstg%d" % (c % 2)
            for h in range(4):
                hs = slice(h * 128, (h + 1) * 128)
                S.add('pe', I('matmul', out=PS[:, hs], lhsT=kT[par][:, hs], rhs=qT[par][:, hs], start=True, stop=True),
                      reads=[R(kTR[par]), R("qT%d" % par)], writes=[PSr[0]])
            S.add('dve', I('tensor_tensor', out=sd[par][:], in0=PS[:, 0:512], in1=DT[:], op=ALU.mult), reads=[PSr[0], R("DT")], writes=[R("sd")])
            S.add('dve', I('tensor_tensor', out=qdf[par][:], in0=qT[par][:], in1=QDF[:], op=ALU.mult), reads=[R("qT%d" % par), R("QDF")], writes=[R("qdf")])
            S.add('pool', I('tensor_tensor', out=qdb[par][:], in0=qT[par][:], in1=QDB[:], op=ALU.mult), reads=[R("qT%d" % par), R("QDB")], writes=[R("qdb")])
            yield
            sfb = S_fbf[0]
            sfr = "S_fbf0"
            for h in range(4):
                hs = slice(h * 128, (h + 1) * 128)
                vs = slice(h * 256, (h + 1) * 256)
                S.add('pe', I('matmul', out=PO[:, vs], lhsT=qdf[par][:, hs], rhs=sfb[:, vs], start=True, stop=False),
                      reads=[R("qdf"), R(sfr)], writes=[POr[h // 2]])
                S.add('pe', I('matmul', out=PO[:, vs], lhsT=qdb[par][:, hs], rhs=sbc[:, vs], start=False, stop=False),
                      reads=[R("qdb"), R(sbr)], writes=[POr[h // 2]])
                S.add('pe', I('matmul', out=PO[:, vs], lhsT=sd[par][:, hs], rhs=v_tok[par][:, vs], start=False, stop=True),
                      reads=[R("sd"), R("v_tok%d" % par)], writes=[POr[h // 2]])
            if c < NOWN - 1:
                kv_update(par, kdf[par], "kdf%d" % par, PS, PSr, S_f, "S_f", CD8, 0)
                S.add('dve', I('tensor_copy', out=S_fbf[0][:], in_=S_f[:]), reads=[R("S_f")], writes=[R("S_fbf0")])
            for h in range(4):
                S.add('dve', I('bn_stats', out=gnst[:, h * 6:(h + 1) * 6], in_=PO[:, h * 256:(h + 1) * 256]), reads=[POr[h // 2]], writes=[R("gnst")])
            for h in range(4):
                S.add('dve', I('bn_aggr', out=gnmv[:, h * 2:(h + 1) * 2], in_=gnst[:, h * 6:(h + 1) * 6]), reads=[R("gnst")], writes=[R("gnmv")])
            mvv = gnmv[:].rearrange("p (h t) -> p h t", h=4, t=2)
            S.add('dve', I('tensor_scalar', out=gnve[:], in0=mvv[:, :, 1], scalar1=EPS, scalar2=None, op0=ALU.add), reads=[R("gnmv")], writes=[R("gnve")])
            S.add('pool', I('tensor_tensor', out=gnrs[:], in0=gnve[:], in1=mhalf[:, 0:4], op=ALU.pow), reads=[R("gnve"), R("mhalf")], writes=[R("gnrs")])
            S.add('dve', I('scalar_tensor_tensor', out=gnnm[:], in0=mvv[:, :, 0], scalar=-1.0, in1=gnrs[:], op0=ALU.mult, op1=ALU.mult),
                  reads=[R("gnmv"), R("gnrs")], writes=[R("gnnm")])
            for h in range(4):
                vs = slice(h * 256, (h + 1) * 256)
                S.add('dve', I('tensor_scalar', out=o_n[:, vs], in0=PO[:, vs], scalar1=gnrs[:, h:h + 1], scalar2=gnnm[:, h:h + 1], op0=ALU.mult, op1=ALU.add),
                      reads=[POr[h // 2], R("gnrs"), R("gnnm")], writes=[R("o_n")])
            S.add('dve', I('tensor_tensor', out=o_g[:], in0=o_n[:], in1=g_r[par][:], op=ALU.mult), reads=[R("o_n"), R("g_r%d" % par)], writes=[R("o_g")])
            yield

        def r_tail_b(c, ctx):
            par = ctx['par']
            tr_to(o_g, "o_g", 8, o_gT[:].rearrange("p a b -> p (a b)"), "o_gT")
            yield
            for nb_ in range(2):
                for kt in range(8):
                    S.add('pe', I('matmul', out=PO[:, nb_ * 512:(nb_ + 1) * 512], lhsT=o_gT[:, kt, :], rhs=W2[:, kt, nb_ * 512:(nb_ + 1) * 512],
                                  start=(kt == 0), stop=(kt == 7)),
                          reads=[R("o_gT"), R("W2")], writes=[POr[nb_]])
            mr = mixr[0]
            mrr = "mixr"
            for nb_ in range(2):
                S.add('dve', I('scalar_tensor_tensor', out=mr[:, nb_ * 512:(nb_ + 1) * 512], in0=sg_r[par][:, nb_ * 512:(nb_ + 1) * 512], scalar=1.0,
                               in1=PO[:, nb_ * 512:(nb_ + 1) * 512], op0=ALU.add, op1=ALU.mult),
                      reads=[R("sg_r%d" % par), POr[nb_]], writes=[R(mrr)])
            S.add('sp', I('dma_start', out=mixrd[c * 128:(c + 1) * 128, :], in_=mr[:]), reads=[R(mrr)], writes=[R("mixr_d")], dma=mrr)
            if debug:
                final_dma.append(S.add('sp', I('dma_start', out=dbg['mixr'][c * 128:(c + 1) * 128, :], in_=mr[:]), reads=[R(mrr)], dma='dbgm'))
            yield

        ctxs = {0: r_front(0)}
        front_b(ctxs[0])
        def chain(*gens):
            for g_ in gens:
                if g_ is not None:
                    yield from g_

        def alt(ga, gb):
            live = [g_ for g_ in (ga, gb) if g_ is not None]
            while live:
                for g_ in list(live):
                    try:
                        next(g_)
                        yield
                    except StopIteration:
                        live.remove(g_)

        ctxs[1] = r_front(1)
        for c in range(NOWN + 2):
            if c < NOWN:
                load_sb(c)
            if c + 2 < NOWN:
                ctxs[c + 2] = r_front(c + 2)
            if c + 1 < NOWN:
                front_b(ctxs[c + 1])
            if c == NOWN:
                ld(nw_t[:], postw, "nw_t", key='pw')
                def reload(rn, sres, d0, n, key):
                    S.add('sp', I('dma_start', out=WW[:, :, d0:d0 + n], in_=wbf[:, d0:d0 + n].rearrange("(kt p) c -> p kt c", p=128)),
                          reads=[R(sres)], writes=[R("WW"), R("WWv"), R("WWq"), R(rn)], dma=key)
                reload("WAkv", "wbf_kv", 1024, 512, 'w4a')
                reload("WAq", "wbf_q", 0, 1024, 'w4b')
                reload("WAg", "wbf_g", 1536, 1024, 'w4c')
                reload("WAs", "wbf_s", 2560, 1024, 'w4d')
            tb = r_tail_b(c - 2, ctxs[c - 2]) if 0 <= c - 2 < NOWN else None
            ta = r_tail_a(c - 1, ctxs[c - 1]) if 0 <= c - 1 < NOWN else None
            hd = r_head(c, ctxs[c]) if c < NOWN else None
            interleave(alt(tb, ta), hd, k=2, lead=1)

        if stop_after == 'R':
            S.add('sp', None, extra=final_dma)
            S.build()
            return nc
        S.add('sp', I('dma_start', out=W2[:], in_=wabf.rearrange("(kt p) c -> p kt c", p=128)), reads=[R("wabf")], writes=[R("W2")], dma='w6')
        while deferred:
            deferred.pop(0)()

        for p in range(2):
            for g in range(2):
                S.add('pool', I('memset', PTm[p][g][:], 0.0), writes=[R(PTmR[p][g])])
        for p in range(2):
            for g in range(2):
                for hh in range(4):
                    h = g * 4 + hh
                    S.add('act', I('activation', out=PTm[p][g][32:33, hh * 128:(hh + 1) * 128], in_=zrow[32:33, :], func=AF.Exp,
                                   bias=sink_t[32:33, h:h + 1], scale=1.0),
                          reads=[R("zrow"), R("sink_t")], writes=[R(PTmR[p][g])])

        def a_head(idx, ctx, nctx=None):
            xi, par = ctx['xi'], ctx['par']
            zkv, zkvr = proj(par, WW, "WAkv", 1024)
            if idx == 0:
                ktile, kres, vtile, vres = KTm, "KTm", Vm, "Vm"
            else:
                ktile, kres, vtile, vres = KT[idx % 4], "KT%d" % (idx % 4), VA[idx % 4], "VA%d" % (idx % 4)
            if idx == 0:
                S.add('act', I('activation', out=Vm[0:16, :, 0:128], in_=zkv[0:16, 256:512].rearrange("p (g d) -> p g d", g=2, d=128), func=AF.Copy),
                      reads=[zkvr], writes=[R("Vm")])
            else:
                S.add('act', I('activation', out=vtile[:, :, 0:128], in_=zkv[:, 256:512].rearrange("p (g d) -> p g d", g=2, d=128), func=AF.Copy),
                      reads=[zkvr], writes=[R(vres)])
            rotary(zkv[:, 0:256], zkvr, CS[xi], "CS%d" % xi, 2, k_rot[par][:, 0:256], "k_rot%d" % par, 0)
            yield
            if 2 <= idx <= 17:
                c = idx - 2
                pa_c = c % 2
                own_info[c] = (xi, pa_c)
                for hb in range(2):
                    zq, zqr = proj(par, WW, "WAq", hb * 512)
                    rotary(zq, zqr, CS[xi], "CS%d" % xi, 4, q_rot[par][:, hb * 512:(hb + 1) * 512], "q_rot", 1)
                    if hb == 0:
                        tr_to(k_rot[par], "k_rot%d" % par, 2, ktile[:].rearrange("p a b -> p (a b)"), kres)
                    yield
                for hb in range(2):
                    zg, zgr = proj(par, WW, "WAg", 1536 + hb * 512)
                    S.add('act', I('activation', out=th_t[:], in_=zg, func=AF.Tanh, scale=0.5), reads=[zgr], writes=[R("QDF")])
                    S.add('dve', I('scalar_tensor_tensor', out=g_r[pa_c][:, hb * 512:(hb + 1) * 512], in0=th_t[:], scalar=1.0, in1=zg, op0=ALU.add, op1=ALU.mult),
                          reads=[R("QDF"), zgr], writes=[R("g_r%d" % pa_c)])
                    if hb == 0:
                        tr_to(q_rot[par], "q_rot", 8, v_tok[pa_c][:], "v_tok%d" % pa_c)
                    yield
                for hb in range(2):
                    zg, zgr = proj(par, WW, "WAs", 2560 + hb * 512)
                    S.add('act', I('activation', out=sg_r[pa_c][:, hb * 512:(hb + 1) * 512], in_=zg, func=AF.Tanh, scale=0.5),
                          reads=[zgr], writes=[R("sg_r%d" % pa_c)])
                    yield
            else:
                tr_to(k_rot[par], "k_rot%d" % par, 2, ktile[:].rearrange("p a b -> p (a b)"), kres)
                yield
            if nctx is not None:
                front_b(nctx)
                yield

        def a_post(c, xi_c):
            q = c % 2
            S.add('act', I('activation', out=o_n[:], in_=PO[:, :], func=AF.Square, accum_out=ssq[q][:]), reads=POr, writes=[R("o_n"), R("ssq%d" % q)])
            S.add('dve', I('tensor_scalar', out=vv[q][:], in0=ssq[q][:], scalar1=1.0 / D_MODEL, scalar2=EPS, op0=ALU.mult, op1=ALU.add),
                  reads=[R("ssq%d" % q)], writes=[R("vv%d" % q)])
            S.add('pool', I('tensor_tensor', out=rstd[q][:], in0=vv[q][:], in1=mhalf[:, 0:1], op=ALU.pow), reads=[R("vv%d" % q), R("mhalf")], writes=[R("rstd%d" % q)])
            for nb_ in range(2):
                cs_ = slice(nb_ * 512, (nb_ + 1) * 512)
                S.add('dve', I('scalar_tensor_tensor', out=tfin[:, cs_], in0=PO[:, cs_], scalar=rstd[q][:], in1=nw_t[:, cs_], op0=ALU.mult, op1=ALU.mult),
                      reads=[POr[nb_], R("rstd%d" % q), R("nw_t")], writes=[R("S_b")])
            S.add('pool', I('tensor_tensor', out=tfin[:], in0=tfin[:], in1=XB[xi_c][:], op=ALU.add), reads=[R("S_b"), R("XB%d" % xi_c)], writes=[R("S_b")])
            final_dma.append(S.add('sp', I('dma_start', out=out[c * 128:(c + 1) * 128, :], in_=tfin[:]), reads=[R("S_b")], dma="resb"))

        def a_tail(c, xi_c, pa_c, post=None):
            if post is not None:
                a_post(*post)
            blocks = []
            for bi, idx in enumerate((c + 1, c + 2, c + 3)):
                if bi == 0:
                    mk = 2 if c == 0 else 0
                elif bi == 2:
                    mk = 3 if c == NOWN - 1 else 1
                else:
                    mk = None
                blocks.append((idx % 4, mk))
            pm = c % 2
            def tr_group(g):
                tr_to(o_g, "o_g", 4, o_gT[:, g * 4:(g + 1) * 4, :].rearrange("p a b -> p (a b)"), "o_gT", src_off=g * 512)

            for g in range(2):
                qg = v_tok[pa_c][:, g * 512:(g + 1) * 512]
                for bi, (sl, mk) in enumerate(blocks):
                    bank = PS[:, (bi % 2) * 512:(bi % 2 + 1) * 512]
                    br = PSr[bi % 2]
                    S.add('pe', I('matmul', out=bank, lhsT=KT[sl][:, g, :], rhs=qg, start=True, stop=(mk is None)),
                          reads=[R("KT%d" % sl), R("v_tok%d" % pa_c)], writes=[br])
                    if mk is not None:
                        S.add('pe', I('matmul', out=bank, lhsT=ident[:], rhs=mask_t[:, mk * 512:(mk + 1) * 512], start=False, stop=True),
                              reads=[R("ident_t"), R("mask_t")], writes=[br])
                    S.add('act', I('activation', out=PTb[g][bi][:], in_=bank, func=AF.Exp, scale=SCALE),
                          reads=[br], writes=[R(PTbR[g][bi])])
                bank = PS[0:16, 512:1024]
                S.add('pe', I('matmul', out=bank, lhsT=KTm[:, g, 0:16], rhs=qg, start=True, stop=True),
                      reads=[R("KTm"), R("v_tok%d" % pa_c)], writes=[PSr[1]])
                S.add('act', I('activation', out=PTm[pm][g][0:16, :], in_=bank, func=AF.Exp, scale=SCALE),
                      reads=[PSr[1]], writes=[R(PTmR[pm][g])])
                if g == 1:
                    tr_group(0)
                yield
                for hh in range(4):
                    dst = PO[:, hh * 129:(hh + 1) * 129] if hh < 3 else PO[:, 512:641]
                    dr = POr[0] if hh < 3 else POr[1]
                    for bi, (sl, mk) in enumerate(blocks):
                        S.add('pe', I('matmul', out=dst, lhsT=PTb[g][bi][:, hh * 128:(hh + 1) * 128], rhs=VA[sl][:, g, 0:129],
                                      start=(bi == 0), stop=False),
                              reads=[R(PTbR[g][bi]), R("VA%d" % sl)], writes=[dr])
                    S.add('pe', I('matmul', out=dst, lhsT=PTm[pm][g][0:33, hh * 128:(hh + 1) * 128], rhs=Vm[0:33, g, 0:129], start=False, stop=True),
                          reads=[R(PTmR[pm][g]), R("Vm")], writes=[dr])
                l3 = PO[:, 0:387].rearrange("p (h c) -> p h c", h=3, c=129)[:, :, 128]
                S.add('dve', I('reciprocal', out=rl[:, 0:3], in_=l3), reads=[POr[0]], writes=[R("rl")])
                S.add('dve', I('reciprocal', out=rl[:, 3:4], in_=PO[:, 640:641]), reads=[POr[1], R("rl")], writes=[R("rl")])
                S.add('dve', I('tensor_scalar', out=rl[:], in0=rl[:], scalar1=0.5, scalar2=None, op0=ALU.mult), reads=[R("rl")], writes=[R("rl")])
                for hh in range(4):
                    h = g * 4 + hh
                    src_ = PO[:, hh * 129:hh * 129 + 128] if hh < 3 else PO[:, 512:640]
                    dr = POr[0] if hh < 3 else POr[1]
                    S.add('dve', I('scalar_tensor_tensor', out=o_g[:, h * 128:(h + 1) * 128], in0=src_, scalar=rl[:, hh:hh + 1],
                                   in1=g_r[pa_c][:, h * 128:(h + 1) * 128], op0=ALU.mult, op1=ALU.mult),
                          reads=[dr, R("rl"), R("g_r%d" % pa_c)], writes=[R("o_g")])
                if g == 0:
                    S.add('sp', I('dma_start', out=mixr[0][:], in_=mixrd[c * 128:(c + 1) * 128, :]), reads=[R("mixr_d")], writes=[R("mixr")], dma="mixr")
                yield
            tr_group(1)
            for nb_ in range(2):
                for kt in range(8):
                    S.add('pe', I('matmul', out=PO[:, nb_ * 512:(nb_ + 1) * 512], lhsT=o_gT[:, kt, :], rhs=W2[:, kt, nb_ * 512:(nb_ + 1) * 512],
                                  start=(kt == 0), stop=(kt == 7)),
                          reads=[R("o_gT"), R("W2")], writes=[POr[nb_]])
            for nb_ in range(2):
                cs_ = slice(nb_ * 512, (nb_ + 1) * 512)
                S.add('dve', I('scalar_tensor_tensor', out=mixa[:, cs_], in0=sg_r[pa_c][:, cs_], scalar=1.0, in1=PO[:, cs_], op0=ALU.add, op1=ALU.mult),
                      reads=[R("sg_r%d" % pa_c), POr[nb_]], writes=[R("S_f")])
            S.add('dve', I('tensor_tensor', out=mixb[:], in0=mixa[:, 0:1024], in1=mixr[0][:], op=ALU.add), reads=[R("S_f"), R("mixr")], writes=[R("o_n")])
            yield
            tr_to(mixb, "o_n", 8, S_fbf[0][:], "S_fbf0", scale=0.5)
            yield
            for nb_ in range(2):
                for kt in range(8):
                    S.add('pe', I('matmul', out=PO[:, nb_ * 512:(nb_ + 1) * 512], lhsT=mixT[:, kt, :], rhs=W3[:, kt, nb_ * 512:(nb_ + 1) * 512],
                                  start=(kt == 0), stop=(kt == 7)),
                          reads=[R("S_fbf0"), R("W3")], writes=[POr[nb_]])
            yield

        own_info = {}
        ctxs = {0: front_a(xm[0:128, :], csm[0:128, :], ring=4)}
        front_b(ctxs[0])
        for idx in range(a_steps):
            if idx + 1 < 19:
                ctxs[idx + 1] = front_a(xm[(idx + 1) * 128:(idx + 2) * 128, :], csm[(idx + 1) * 128:(idx + 2) * 128, :], ring=4)
            tail = None
            if idx >= 3:
                c_ = idx - 3
                post = (c_ - 1, own_info[c_ - 1][0]) if c_ >= 1 else None
                tail = a_tail(c_, *own_info[c_], post=post)
            hd = a_head(idx, ctxs[idx], ctxs.get(idx + 1))
            if PIPE_A:
                interleave(tail, hd, k=1, lead=2)
            else:
                interleave(None, hd)
                interleave(tail, None)

        if a_steps == 19:
            a_post(NOWN - 1, own_info[NOWN - 1][0])
        S.add('sp', None, extra=final_dma)
        S.build()
        print("sched stats", S.stats, flush=True)
    return nc


def _host_prep(inputs):
    x = np.asarray(inputs["x"], np.float32)
    meta = np.asarray(inputs["meta_tokens"], np.float32)
    import jax
    import jax.numpy as jnp
    _cpu = jax.devices("cpu")[0]
    with jax.default_device(_cpu):
        inv_j = 10000.0 ** (-jnp.arange(64, dtype=jnp.float32) * 2.0 / 128)

    def chunk_data(b, n):
        if n == 0:
            z = np.zeros((128, 1024), np.float32)
            z[112:] = meta
            return z
        if n > 64:
            return np.zeros((128, 1024), np.float32)
        return x[b, (n - 1) * 128:n * 128]

    def cs_of(pos):
        with jax.default_device(_cpu):
            ang = jnp.asarray(np.asarray(pos, np.float32))[:, None] * inv_j[None, :]
            return np.concatenate([np.asarray(jnp.cos(ang), np.float32), np.asarray(jnp.sin(ang), np.float32)], axis=1)

    def chunk_pos(n):
        return np.maximum(n * 128 + np.arange(128) - 112, 0)

    jj = np.arange(128)[:, None].astype(np.float64)
    ii = np.arange(128)[None, :].astype(np.float64)
    sc = 128.0 ** -0.5
    rtab = np.concatenate([np.maximum(ii - jj, 0), (ii >= jj) * sc, np.maximum(jj - ii, 0), (jj > ii) * sc], axis=1).astype(np.float32)
    ctab = np.concatenate([np.broadcast_to(ii + 1, (128, 128)), np.broadcast_to(128 - ii, (128, 128))], axis=1).astype(np.float32)
    jtab = np.concatenate([127 - jj, jj], axis=1).astype(np.float32)
    bf = ml_dtypes.bfloat16
    m_prev = np.where(jj >= ii, 0.0, NEG).astype(np.float32)
    m_next = np.where(jj <= ii, 0.0, NEG).astype(np.float32)
    m_all = np.full((128, 128), NEG, np.float32)
    ident = np.eye(128, dtype=np.float32).astype(bf)

    def bc(v, n):
        return np.ascontiguousarray(np.broadcast_to(np.asarray(v, np.float32).reshape(1, n), (128, n)))

    common = dict(
        w_in=np.ascontiguousarray(inputs["w_in"][0], dtype=np.float32),
        w_rb=np.ascontiguousarray(inputs["w_ret_branch"][0], dtype=np.float32),
        w_ab=np.ascontiguousarray(inputs["w_attn_branch"][0], dtype=np.float32),
        w_o=np.ascontiguousarray(inputs["w_out"][0], dtype=np.float32),
        prew=bc(inputs["pre_norm_w"][0], 1024), postw=bc(inputs["post_norm_w"][0], 1024), nwb=bc(inputs["ret_norm_w"][0], 1024),
        dec8=bc(np.concatenate([np.asarray(inputs["ret_decay_fwd"][0]), np.asarray(inputs["ret_decay_bwd"][0])]), 8),
        sink8=bc(inputs["attn_sink"][0], 8),
        rtab=rtab, ctab=ctab, jtab=jtab, ident=ident,
    )
    in_maps = []
    for core in range(8):
        b, s = divmod(core, 4)
        fwd = list(range(0, 16 * s + 1))
        bwd = list(range(64, 16 * s + 16, -1))
        own = [16 * s + 1 + c for c in range(15, 0, -1)]
        slots = fwd + bwd + own
        assert len(slots) == NSLOT
        xs = np.concatenate([chunk_data(b, n) for n in slots], axis=0)
        css = np.concatenate([cs_of(chunk_pos(n)) for n in slots], axis=0)
        actf = np.array([1.0] * len(fwd) + [0.0] * (NSLOT - len(fwd)), np.float32)
        acttab = bc(np.concatenate([actf, 1.0 - actf]), 128)
        metac = np.zeros((128, 1024), np.float32)
        metac[:16] = meta
        mpos = np.zeros(128)
        mpos[:16] = np.arange(16)
        mchunks = [16 * s] + [16 * s + 1 + c for c in range(16)] + [16 * s + 17]
        xm = np.concatenate([metac] + [chunk_data(b, n) for n in mchunks], axis=0)
        csm = np.concatenate([cs_of(mpos)] + [cs_of(chunk_pos(n)) for n in mchunks], axis=0)
        mk = [m_prev, m_next, m_all if s == 0 else m_prev, m_all if s == 3 else m_next]
        masks = np.concatenate([np.tile(m, (1, 4)) for m in mk], axis=1).astype(bf)
        d = dict(common)
        d.update(xs=xs, css=css, xm=xm, csm=csm, acttab=acttab, masks=masks)
        in_maps.append(d)
    return in_maps


_NC_CACHE = {}


def kernel(**inputs):
    in_maps = _host_prep(inputs)
    if 'nc' not in _NC_CACHE:
        _NC_CACHE['nc'] = build_nc()
    nc = _NC_CACHE['nc']
    res = run_bass_kernel_spmd(nc, in_maps, core_ids=list(range(8)))
    out = np.empty((2, SEQ, D_MODEL), np.float32)
    for core in range(8):
        b, s = divmod(core, 4)
        out[b, s * 2048:(s + 1) * 2048] = np.asarray(res.results[core]["out"], np.float32)
    return out
```

```python
import contextlib
import numpy as np
import ml_dtypes
import concourse.bass as bass
import concourse.mybir as mybir
from concourse.bass_utils import run_bass_kernel_spmd

F32 = mybir.dt.float32
BF16 = mybir.dt.bfloat16
AF = mybir.ActivationFunctionType
ALU = mybir.AluOpType

SAME_ENGINE_SYNC = True
DEFER_W = True
PIPE_A = True
ENGS = ['pe', 'dve', 'act', 'pool', 'sp']

D_MODEL = 1024
SEQ = 8192
N_META = 16
NSLOT = 64
NOWN = 16
EPS = 1e-6
SCALE = 128 ** -0.5
LN_SCALE = float(-0.5 * np.log(128.0))
NEG = -30000.0


class Res:
    __slots__ = ('name', 'last_w', 'readers', 'excl')

    def __init__(self, name):
        self.name = name
        self.last_w = None
        self.readers = []
        self.excl = False


class Sched:
    def __init__(self, nc):
        self.nc = nc
        self.ops = []

    def add(self, eng, fn, reads=(), writes=(), dma=None, extra=()):
        self.ops.append((eng, fn, tuple(reads), tuple(writes), dma, tuple(extra)))
        return len(self.ops) - 1

    def build(self):
        nc = self.nc
        last = {}
        for i, op in enumerate(self.ops):
            if op[4] is not None:
                last[op[4]] = i
        self.add('sp', None, extra=sorted(last.values()))
        ops = self.ops
        n = len(ops)
        deps = [None] * n
        signaled = [False] * n
        for i, (eng, fn, reads, writes, dk, extra) in enumerate(ops):
            d = set(extra)
            for r in reads:
                if r.last_w is not None:
                    d.add(r.last_w)
                if r.excl:
                    d.update(j for j in r.readers if ops[j][0] != eng)
            for w in writes:
                if w.last_w is not None:
                    d.add(w.last_w)
                d.update(w.readers)
            d.discard(i)
            dd = set()
            latest = {}
            for j in d:
                ej = ops[j][0]
                dmaj = ops[j][4] is not None
                if dmaj and dk is not None and ops[j][4] == dk and dk.startswith('T:'):
                    continue
                if (not dmaj) and ej == eng and dk is None and (eng == 'pe' or not SAME_ENGINE_SYNC):
                    continue
                if dmaj:
                    dd.add(j)
                else:
                    latest[ej] = max(latest.get(ej, -1), j)
            dd.update(latest.values())
            deps[i] = dd
            for j in dd:
                signaled[j] = True
            for r in reads:
                r.readers.append(i)
            for w in writes:
                w.last_w = i
                w.readers = []
        cnt = {e: 0 for e in ENGS}
        val = [0] * n
        dmacnt = {}
        for i, op in enumerate(ops):
            if op[4] is not None:
                dmacnt[op[4]] = dmacnt.get(op[4], 0) + 1
                val[i] = 16 * dmacnt[op[4]]
            elif signaled[i]:
                cnt[op[0]] += 1
                val[i] = cnt[op[0]]
        for i, op in enumerate(ops):
            if op[4] is not None and op[4].startswith('T:'):
                val[i] = 16 * dmacnt[op[4]]
        self.stats = dict(n_ops=n, milestones=dict(cnt), dma_keys=len(dmacnt))
        with contextlib.ExitStack() as st:
            sem_eng = {e: st.enter_context(nc.semaphore('s_' + e)) for e in ENGS}
            dma_sem = {k: st.enter_context(nc.semaphore('d_' + k.replace(':', '_'))) for k in dmacnt}

            def emit(engname, e):
                waited = {}
                for i, op in enumerate(ops):
                    if op[0] != engname:
                        continue
                    need = {}
                    for j in deps[i]:
                        if ops[j][4] is not None:
                            key = ('d', ops[j][4])
                            sem = dma_sem[ops[j][4]]
                        else:
                            key = ('e', ops[j][0])
                            sem = sem_eng[ops[j][0]]
                        if val[j] > need.get(key, (None, 0))[1]:
                            need[key] = (sem, val[j])
                    for key, (sem, v) in need.items():
                        if waited.get(key, 0) >= v:
                            continue
                        e.wait_ge(sem, v)
                        waited[key] = v
                    if op[1] is None:
                        continue
                    inst = op[1](e)
                    if op[4] is not None:
                        inst.then_inc(dma_sem[op[4]], 16)
                    elif signaled[i]:
                        inst.then_inc(sem_eng[engname], 1)

            with nc.Block() as block:
                @block.tensor
                def _(e):
                    emit('pe', e)

                @block.vector
                def _(e):
                    emit('dve', e)

                @block.scalar
                def _(e):
                    emit('act', e)

                @block.gpsimd
                def _(e):
                    emit('pool', e)

                @block.sync
                def _(e):
                    emit('sp', e)


def I(name, *a, **kw):
    return lambda e: getattr(e, name)(*a, **kw)


def build_nc(debug=False, stop_after=None, a_steps=19):
    nc = bass.Bass("TRN2", target_bir_lowering=False)

    def din(name, shape, dt=F32):
        return nc.dram_tensor(name, shape, dt, kind="ExternalInput").ap()

    xs = din("xs", [NSLOT * 128, 1024])
    css = din("css", [NSLOT * 128, 128])
    xm = din("xm", [19 * 128, 1024])
    csm = din("csm", [19 * 128, 128])
    w_in = din("w_in", [1024, 7680])
    w_rb = din("w_rb", [1024, 1024])
    w_ab = din("w_ab", [1024, 1024])
    w_o = din("w_o", [1024, 1024])
    prew = din("prew", [128, 1024])
    postw = din("postw", [128, 1024])
    nwb = din("nwb", [128, 1024])
    dec8 = din("dec8", [128, 8])
    sink8 = din("sink8", [128, 8])
    rtab = din("rtab", [128, 512])
    ctab = din("ctab", [128, 256])
    jtab = din("jtab", [128, 2])
    acttab = din("acttab", [128, 128])
    masks = din("masks", [128, 4 * 512], BF16)
    identd = din("ident", [128, 128], BF16)
    out = nc.dram_tensor("out", [NOWN * 128, 1024], F32, kind="ExternalOutput").ap()
    sbst = nc.dram_tensor("sbst", [NOWN * 128, 1024], BF16, kind="Internal").ap()
    mixrd = nc.dram_tensor("mixrd", [NOWN * 128, 1024], BF16, kind="Internal").ap()
    wbf = nc.dram_tensor("wbf", [1024, 3584], BF16, kind="Internal").ap()
    wabf = nc.dram_tensor("wabf", [1024, 1024], BF16, kind="Internal").ap()
    dbg = {}
    if debug:
        dbg['sf'] = nc.dram_tensor("dbg_sf", [128, 1024], F32, kind="ExternalOutput").ap()
        dbg['sb'] = nc.dram_tensor("dbg_sb", [128, 1024], F32, kind="ExternalOutput").ap()
        dbg['mixr'] = nc.dram_tensor("dbg_mixr", [NOWN * 128, 1024], BF16, kind="ExternalOutput").ap()

    with contextlib.ExitStack() as st:
        RES = {}

        def R(name):
            if name not in RES:
                RES[name] = Res(name)
            return RES[name]

        def sb(name, shape, dt):
            t = st.enter_context(nc.sbuf_tensor(name, shape, dt))
            R(name)
            return t

        def ps(name, shape, dt):
            return st.enter_context(nc.psum_tensor(name, shape, dt))

        S = Sched(nc)

        WW = sb("WW", [128, 8, 4096], BF16)
        W2 = sb("W2", [128, 8, 1024], BF16)
        W3 = sb("W3", [128, 8, 1024], BF16)
        prew_t = sb("prew_t", [128, 1024], F32)
        nw_t = sb("nw_t", [128, 1024], F32)
        ident = sb("ident_t", [128, 128], BF16)
        mask_t = sb("mask_t", [128, 4 * 512], BF16)
        dec_t = sb("dec_t", [128, 8], F32)
        sink_t = sb("sink_t", [128, 8], F32)
        jtab_t = sb("jtab_t", [128, 2], F32)
        act_t = sb("act_t", [128, 128], F32)
        e8 = sb("e8", [128, 8], F32)
        lg8 = sb("lg8", [128, 8], F32)
        lg128 = sb("lg128", [128, 8], F32)
        DT = sb("DT", [128, 512], BF16)
        QDF = sb("QDF", [128, 512], F32)
        QDB = sb("QDB", [128, 512], F32)
        KD8 = sb("KD8", [128, 8], F32)
        CD8 = sb("CD8", [128, 8], F32)
        KDSf = sb("KDSf", [128, 256], F32)
        KDSb = sb("KDSb", [128, 256], F32)
        AFt = sb("AFt", [128, 256], F32)
        ABt = sb("ABt", [128, 256], F32)
        zrow = sb("zrow", [128, 128], F32)
        S_f = sb("S_f", [128, 1024], F32)
        S_b = sb("S_b", [128, 1024], F32)
        S_fbf1 = sb("S_fbf0", [128, 1024], BF16)
        S_fbf = [S_fbf1, S_fbf1]
        sbst2 = sb("sbst2", [128, 2048], BF16)
        sbstg = [sbst2[:, 0:1024], sbst2[:, 1024:2048]]
        R("sbstg0")
        R("sbstg1")
        XB = [sb("XB%d" % i, [128, 1024], F32) for i in range(3)]
        XB.append(sbst2.bitcast(F32))
        R("XB3")
        CS = [sb("CS%d" % i, [128, 128], F32) for i in range(4)]
        ssq = [sb("ssq%d" % i, [128, 1], F32) for i in range(2)]
        vv = [sb("vv%d" % i, [128, 1], F32) for i in range(2)]
        rstd = [sb("rstd%d" % i, [128, 1], F32) for i in range(2)]
        mhalf = sb("mhalf", [128, 8], F32)
        u0 = sb("u0", [128, 1024], F32)
        ub = [sb("ub%d" % i, [128, 1024], BF16) for i in range(2)]
        uT = [sb("uT%d" % i, [128, 8, 128], BF16) for i in range(2)]
        rotA = [sb("rotA%d" % i, [128, 512], F32) for i in range(2)]
        rotB = [sb("rotB%d" % i, [128, 512], F32) for i in range(2)]
        k_rot = [sb("k_rot%d" % i, [128, 512], BF16) for i in range(2)]
        q_rot1 = sb("q_rot", [128, 1024], BF16)
        q_rot = [q_rot1, q_rot1]
        kdf = [sb("kdf%d" % i, [128, 512], BF16) for i in range(2)]
        kdb = [sb("kdb%d" % i, [128, 512], BF16) for i in range(2)]
        v_tok = [sb("v_tok%d" % i, [128, 1024], BF16) for i in range(2)]
        qT = [sb("qT%d" % i, [128, 512], BF16) for i in range(2)]
        kT1 = sb("kT", [128, 512], BF16)
        qdf1 = sb("qdf", [128, 512], BF16)
        qdb1 = sb("qdb", [128, 512], BF16)
        sd1 = sb("sd", [128, 512], BF16)
        kT2 = sb("kT2", [128, 512], BF16)
        kT, qdf, qdb, sd = [kT1, kT2], [qdf1] * 2, [qdb1] * 2, [sd1] * 2
        kTR = ["kT", "kT2"]
        g_r = [sb("g_r%d" % i, [128, 1024], BF16) for i in range(2)]
        sg_r = [sb("sg_r%d" % i, [128, 1024], BF16) for i in range(2)]
        gnst = sb("gnst", [128, 24], F32)
        gnmv = sb("gnmv", [128, 8], F32)
        gnve = sb("gnve", [128, 4], F32)
        gnrs = sb("gnrs", [128, 4], F32)
        gnnm = sb("gnnm", [128, 4], F32)
        o_n = sb("o_n", [128, 1024], BF16)
        o_g = sb("o_g", [128, 1024], BF16)
        o_gT = sb("o_gT", [128, 8, 128], BF16)
        mixr1 = sb("mixr", [128, 1024], BF16)
        mixr = [mixr1, mixr1]
        KT = [sb("KT%d" % i, [128, 2, 128], BF16) for i in range(4)]
        VA = [sb("VA%d" % i, [128, 2, 130], BF16) for i in range(4)]
        KTm = sb("KTm", [128, 2, 128], BF16)
        Vm = sb("Vm", [128, 2, 130], BF16)
        PTb = [[kdf[0], kdf[1], kdb[0]], [kdb[1], qT[0], qT[1]]]
        PTbR = [["kdf0", "kdf1", "kdb0"], ["kdb1", "qT0", "qT1"]]
        PTm = [[kT1, qdf1], [qdb1, sd1]]
        PTmR = [["kT", "qdf"], ["qdb", "sd"]]
        th_t = QDF
        rl = sb("rl", [128, 4], F32)
        mixa = S_f.bitcast(BF16)
        mixb = o_n
        mixT = S_fbf[0][:].rearrange("p (a b) -> p a b", a=8, b=128)
        tfin = S_b

        rtab_t = rotA[0]
        RES["rtab_t"] = RES["rotA0"]
        ctab_t = u0[:, 256:512]
        tmpA = u0[:, 0:256]
        tmp1 = u0[:, 512:640]
        tmp2 = u0[:, 640:768]
        for nm_ in ("ctab_t", "tmpA", "tmp1", "tmp2"):
            RES[nm_] = RES["u0"]
        print("sbuf remaining", nc.sbuf_bytes_remaining() if callable(getattr(nc, 'sbuf_bytes_remaining', None)) else nc.sbuf_bytes_remaining, flush=True)
        PA = ps("PA", [128, 1024], F32)
        PTr = ps("PTr", [128, 2048], BF16)
        PS = ps("PS", [128, 1024], F32)
        PO = ps("PO", [128, 1024], F32)
        for nm in ["PA0", "PA1", "PT0", "PT1", "PS0", "PS1", "PO0", "PO1", "sbst_d", "mixr_d"]:
            R(nm)
        for nm in ["PA0", "PA1", "PT0", "PT1", "PS0", "PS1", "PO0", "PO1"]:
            R(nm).excl = True
        PTf = PTr.bitcast(F32)
        PA_all = {
            'state': [(PA[:, 0:512], R("PA0")), (PA[:, 512:1024], R("PA1"))],
            'main': [(PA[:, 0:512], R("PA0")), (PA[:, 512:1024], R("PA1"))],
        }
        PT_all = {
            'state': [(PTr[:, 0:1024], R("PT0")), (PTr[:, 1024:2048], R("PT1"))],
            'main': [(PTr[:, 0:1024], R("PT0")), (PTr[:, 1024:2048], R("PT1"))],
        }
        mode = ['state']
        cnt = dict(pa=0, pt=0, x=0, xa=0, par=0)

        def nxt(key, mod):
            v = cnt[key]
            cnt[key] = v + 1
            return v % mod

        def ld(dst, src, rname, q='sp', key='T:setup'):
            S.add(q, I('dma_start', out=dst, in_=src), writes=[R(rname)], dma=key)

        ld(dec_t[:], dec8, "dec_t")
        ld(sink_t[:], sink8, "sink_t")
        ld(rtab_t[:], rtab, "rtab_t")
        ld(ctab_t, ctab, "ctab_t")
        ld(jtab_t[:], jtab, "jtab_t")
        ld(act_t[:], acttab, "act_t")
        ld(ident[:], identd, "ident_t")
        ld(mask_t[:], masks, "mask_t", key='T:setup2')
        ld(prew_t[:], prew, "prew_t")
        ld(nw_t[:], nwb, "nw_t", key='T:setup2')

        def wload(dst_tile, rname, src, c0, n, d0, key):
            rn = rname if isinstance(rname, (list, tuple)) else [rname]
            for kt in range(8):
                S.add('pool', I('dma_start', out=dst_tile[:, kt, d0:d0 + n], in_=src[kt * 128:(kt + 1) * 128, c0:c0 + n]),
                      writes=[R(r_) for r_ in rn], dma=key)

        wload(WW, "WW", w_in, 512, 512, 512, 'T:w1')
        wload(WW, "WWv", w_in, 1024, 1024, 1024, 'T:w1v')

        S.add('pool', I('memset', mhalf[:], -0.5), writes=[R("mhalf")])
        S.add('pool', I('memset', zrow[:], 0.0), writes=[R("zrow")])
        S.add('pool', I('memset', S_f[:], 0.0), writes=[R("S_f")])
        S.add('pool', I('memset', S_b[:], 0.0), writes=[R("S_b")])
        for i in range(4):
            S.add('pool', I('memset', VA[i][:], 1.0), writes=[R("VA%d" % i)])
        S.add('pool', I('memset', Vm[:], 0.0), writes=[R("Vm")])
        S.add('pool', I('memset', Vm[0:16, :, 128:129], 1.0), writes=[R("Vm")])
        S.add('pool', I('memset', Vm[32:33, :, 128:129], 1.0), writes=[R("Vm")])
        S.add('act', I('activation', out=e8[:], in_=dec_t[:], func=AF.Exp, scale=-1.0), reads=[R("dec_t")], writes=[R("e8")])
        S.add('act', I('activation', out=e8[:], in_=e8[:], func=AF.Ln, bias=1.0), reads=[R("e8")], writes=[R("e8")])
        S.add('dve', I('tensor_scalar', out=lg8[:], in0=e8[:], scalar1=-1.0, scalar2=None, op0=ALU.mult), reads=[R("e8")], writes=[R("lg8")])
        S.add('dve', I('tensor_scalar', out=lg128[:], in0=e8[:], scalar1=-128.0, scalar2=None, op0=ALU.mult), reads=[R("e8")], writes=[R("lg128")])
        for h in range(4):
            hs = slice(h * 128, (h + 1) * 128)
            S.add('act', I('activation', out=tmp1, in_=rtab_t[:, 0:128], func=AF.Exp, scale=lg8[:, h:h + 1]),
                  reads=[R("rtab_t"), R("lg8")], writes=[R("tmp1")])
            S.add('act', I('activation', out=tmp2, in_=rtab_t[:, 256:384], func=AF.Exp, scale=lg8[:, 4 + h:5 + h]),
                  reads=[R("rtab_t"), R("lg8")], writes=[R("tmp2")])
            S.add('dve', I('tensor_tensor', out=tmp1, in0=tmp1, in1=rtab_t[:, 128:256], op=ALU.mult), reads=[R("tmp1"), R("rtab_t")], writes=[R("tmp1")])
            S.add('dve', I('tensor_tensor', out=tmp2, in0=tmp2, in1=rtab_t[:, 384:512], op=ALU.mult), reads=[R("tmp2"), R("rtab_t")], writes=[R("tmp2")])
            S.add('dve', I('tensor_tensor', out=DT[:, hs], in0=tmp1, in1=tmp2, op=ALU.add), reads=[R("tmp1"), R("tmp2")], writes=[R("DT")])
            S.add('act', I('activation', out=QDF[:, hs], in_=ctab_t[:, 0:128], func=AF.Exp, scale=lg8[:, h:h + 1]),
                  reads=[R("ctab_t"), R("lg8")], writes=[R("QDF")])
            S.add('act', I('activation', out=QDB[:, hs], in_=ctab_t[:, 128:256], func=AF.Exp, scale=lg8[:, 4 + h:5 + h]),
                  reads=[R("ctab_t"), R("lg8")], writes=[R("QDB")])
            S.add('act', I('activation', out=KD8[:, h:h + 1], in_=jtab_t[:, 0:1], func=AF.Exp, scale=lg8[:, h:h + 1], bias=LN_SCALE),
                  reads=[R("jtab_t"), R("lg8")], writes=[R("KD8")])
            S.add('act', I('activation', out=KD8[:, 4 + h:5 + h], in_=jtab_t[:, 1:2], func=AF.Exp, scale=lg8[:, 4 + h:5 + h], bias=LN_SCALE),
                  reads=[R("jtab_t"), R("lg8")], writes=[R("KD8")])
        S.add('act', I('activation', out=CD8[:], in_=lg128[:], func=AF.Exp), reads=[R("lg128")], writes=[R("CD8")])

        def bc_slot(tile, off):
            return bass.AP(tile, off, [[tile.shape[1], 128], [0, NSLOT], [1, 4]])

        def bc_head(tile, off):
            return bass.AP(tile, off, [[tile.shape[1], 128], [1, NSLOT], [0, 4]])

        def v3(tile):
            ap_ = tile if isinstance(tile, bass.AP) else tile[:]
            return ap_.rearrange("p (s h) -> p s h", s=NSLOT, h=4)

        S.add('dve', I('tensor_tensor', out=v3(KDSf), in0=bc_slot(KD8, 0), in1=bc_head(act_t, 0), op=ALU.mult),
              reads=[R("KD8"), R("act_t")], writes=[R("KDSf")])
        S.add('dve', I('tensor_tensor', out=v3(KDSb), in0=bc_slot(KD8, 4), in1=bc_head(act_t, 64), op=ALU.mult),
              reads=[R("KD8"), R("act_t")], writes=[R("KDSb")])
        S.add('dve', I('tensor_tensor', out=v3(tmpA), in0=bc_slot(lg128, 0), in1=bc_head(act_t, 0), op=ALU.mult),
              reads=[R("lg128"), R("act_t")], writes=[R("tmpA")])
        S.add('act', I('activation', out=AFt[:], in_=tmpA, func=AF.Exp), reads=[R("tmpA")], writes=[R("AFt")])
        S.add('dve', I('tensor_tensor', out=v3(tmpA), in0=bc_slot(lg128, 4), in1=bc_head(act_t, 64), op=ALU.mult),
              reads=[R("lg128"), R("act_t"), R("AFt")], writes=[R("tmpA")])
        S.add('act', I('activation', out=ABt[:], in_=tmpA, func=AF.Exp), reads=[R("tmpA")], writes=[R("ABt")])
        deferred = []

        def wload_def(dst_tile, rname, src, c0, n, d0, key):
            for kt in range(8):
                deferred.append(lambda kt=kt: S.add('pool', I('dma_start', out=dst_tile[:, kt, d0:d0 + n], in_=src[kt * 128:(kt + 1) * 128, c0:c0 + n]),
                                                   writes=[R(rname)], dma=key))

        wload_def(WW, "WWq", w_in, 0, 512, 0, 'T:w2')
        wload_def(WW, "WWq", w_in, 2048, 1024, 2048, 'T:w2')
        wload_def(WW, "WWq", w_in, 5632, 1024, 3072, 'T:w2')
        wload_def(W2, "W2", w_rb, 0, 1024, 0, 'T:w2')
        wload_def(W3, "W3", w_o, 0, 1024, 0, 'T:w3')

        def precast_def(dst, rname, src, c0, n, d0, key):
            for kt in range(8):
                deferred.append(lambda kt=kt: S.add('pool', I('dma_start', out=dst[kt * 128:(kt + 1) * 128, d0:d0 + n], in_=src[kt * 128:(kt + 1) * 128, c0:c0 + n]),
                                                   writes=[R(rname)], dma=key))

        precast_def(wbf, "wbf_kv", w_in, 4096, 512, 1024, 'T:pc1')
        precast_def(wbf, "wbf_q", w_in, 3072, 1024, 0, 'T:pc2')
        precast_def(wbf, "wbf_g", w_in, 4608, 1024, 1536, 'T:pc3')
        precast_def(wbf, "wbf_s", w_in, 6656, 1024, 2560, 'T:pc4')
        precast_def(wabf, "wabf", w_ab, 0, 1024, 0, 'T:pc5')
        if not DEFER_W:
            while deferred:
                deferred.pop(0)()

        def interleave(tail, filler, k=1, lead=0):
            t_done = tail is None
            f_done = filler is None

            def step(gen):
                try:
                    next(gen)
                    return False
                except StopIteration:
                    return True
            for _ in range(lead):
                if not f_done:
                    f_done = step(filler)
            while not (t_done and f_done):
                if not t_done:
                    t_done = step(tail)
                for _ in range(k):
                    if not f_done:
                        f_done = step(filler)

        def front_a(xsrc, cssrc, ring=3):
            xi = nxt('x' if ring == 3 else 'xa', ring)
            par = nxt('par', 2)
            xb, cs = XB[xi], CS[xi]
            S.add('sp', I('dma_start', out=xb[:], in_=xsrc), writes=[R("XB%d" % xi)] + ([R("sbstg0"), R("sbstg1")] if xi == 3 else []), dma='x%d' % xi)
            S.add('sp', I('dma_start', out=cs[:], in_=cssrc), writes=[R("CS%d" % xi)], dma='c%d' % xi)
            S.add('act', I('activation', out=ub[par][:], in_=xb[:], func=AF.Square, accum_out=ssq[par][:]),
                  reads=[R("XB%d" % xi)], writes=[R("ub%d" % par), R("ssq%d" % par)])
            S.add('dve', I('tensor_scalar', out=vv[par][:], in0=ssq[par][:], scalar1=1.0 / D_MODEL, scalar2=EPS, op0=ALU.mult, op1=ALU.add),
                  reads=[R("ssq%d" % par)], writes=[R("vv%d" % par)])
            S.add('pool', I('tensor_tensor', out=rstd[par][:], in0=vv[par][:], in1=mhalf[:, 0:1], op=ALU.pow),
                  reads=[R("vv%d" % par), R("mhalf")], writes=[R("rstd%d" % par)])
            S.add('pool', I('tensor_tensor', out=u0[:], in0=xb[:], in1=prew_t[:], op=ALU.mult),
                  reads=[R("XB%d" % xi), R("prew_t")], writes=[R("u0")])
            S.add('act', I('activation', out=ub[par][:], in_=u0[:], func=AF.Copy, scale=rstd[par][:]),
                  reads=[R("u0"), R("rstd%d" % par)], writes=[R("ub%d" % par)])
            return dict(xi=xi, par=par)

        def front_b(ctx):
            par = ctx['par']
            tr_to(ub[par], "ub%d" % par, 8, uT[par][:].rearrange("p a b -> p (a b)"), "uT%d" % par)

        def tr_to(src_tile, src_res, ntile, dst_ap, dst_res, scale=None, src_off=0):
            pts = PT_all[mode[0]]
            pt, ptr = pts[nxt('pt', len(pts))]
            for t in range(ntile):
                S.add('pe', I('transpose', out=pt[:, t * 128:(t + 1) * 128], in_=src_tile[:, src_off + t * 128:src_off + (t + 1) * 128], identity=ident[:]),
                      reads=[R(src_res), R("ident_t")], writes=[ptr])
            if scale is None:
                S.add('act', I('activation', out=dst_ap, in_=pt[:, 0:ntile * 128], func=AF.Copy), reads=[ptr], writes=[R(dst_res)])
            else:
                S.add('act', I('activation', out=dst_ap, in_=pt[:, 0:ntile * 128], func=AF.Copy, scale=scale), reads=[ptr], writes=[R(dst_res)])

        def proj(par, wt, wres, c0, n=512):
            pas = PA_all[mode[0]]
            bank, bres = pas[nxt('pa', len(pas))]
            for kt in range(8):
                S.add('pe', I('matmul', out=bank[:, 0:n], lhsT=uT[par][:, kt, :], rhs=wt[:, kt, c0:c0 + n], start=(kt == 0), stop=(kt == 7)),
                      reads=[R("uT%d" % par)] + [R(r_) for r_ in (wres if isinstance(wres, (list, tuple)) else [wres])], writes=[bres])
            return bank[:, 0:n], bres

        def rotary(zb, zres, cs, csres, nh, dst_ap, dst_res, ri):
            n = nh * 128
            zv = zb.rearrange("p (h t f) -> p h t f", h=nh, t=2, f=64)
            A = rotA[ri][:, 0:n].rearrange("p (h t f) -> p h t f", h=nh, t=2, f=64)
            B = rotB[ri][:, 0:n].rearrange("p (h t f) -> p h t f", h=nh, t=2, f=64)
            dv = dst_ap.rearrange("p (h t f) -> p h t f", h=nh, t=2, f=64)
            cosb = bass.AP(cs, 0, [[128, 128], [0, nh], [0, 2], [1, 64]])
            sinb = bass.AP(cs, 64, [[128, 128], [0, nh], [1, 64]])
            rA, rB = R("rotA%d" % ri), R("rotB%d" % ri)
            S.add('dve', I('tensor_tensor', out=A, in0=zv, in1=cosb, op=ALU.mult), reads=[zres, R(csres)], writes=[rA])
            S.add('dve', I('tensor_tensor', out=B[:, :, 0, :], in0=zv[:, :, 1, :], in1=sinb, op=ALU.mult), reads=[zres, R(csres)], writes=[rB])
            S.add('dve', I('tensor_tensor', out=B[:, :, 1, :], in0=zv[:, :, 0, :], in1=sinb, op=ALU.mult), reads=[zres, R(csres)], writes=[rB])
            S.add('pool', I('tensor_tensor', out=dv[:, :, 0, :], in0=A[:, :, 0, :], in1=B[:, :, 0, :], op=ALU.subtract), reads=[rA, rB], writes=[R(dst_res)])
            S.add('pool', I('tensor_tensor', out=dv[:, :, 1, :], in0=A[:, :, 1, :], in1=B[:, :, 1, :], op=ALU.add), reads=[rA, rB], writes=[R(dst_res)])

        def bc_d(tile, off):
            return bass.AP(tile, off, [[tile.shape[1], 128], [1, 4], [0, 128]])

        def h3(ap):
            return ap.rearrange("p (h d) -> p h d", h=4, d=128)

        def kv_update(par, kd_tile, kd_res, bank_tile, bres, Stile, Sres, coef_tile, coef_off):
            for h in range(4):
                S.add('pe', I('matmul', out=bank_tile[:, h * 256:(h + 1) * 256], lhsT=kd_tile[:, h * 128:(h + 1) * 128],
                              rhs=v_tok[par][:, h * 256:(h + 1) * 256], start=True, stop=True),
                      reads=[R(kd_res), R("v_tok%d" % par)], writes=[bres[h // 2]])
            for h in range(4):
                S.add('dve', I('scalar_tensor_tensor',
                               out=Stile[:, h * 256:(h + 1) * 256], in0=Stile[:, h * 256:(h + 1) * 256], scalar=coef_tile[:, coef_off + h:coef_off + h + 1],
                               in1=bank_tile[:, h * 256:(h + 1) * 256], op0=ALU.mult, op1=ALU.add),
                      reads=[R(Sres), bres[h // 2], R(coef_tile.name)], writes=[R(Sres)])

        PSr = [R("PS0"), R("PS1")]
        POr = [R("PO0"), R("PO1")]
        final_dma = []

        nstg = [0]

        def st_head(t, ctx):
            xi, par = ctx['xi'], ctx['par']
            zk, zkr = proj(par, WW, "WW", 512)
            rotary(zk, zkr, CS[xi], "CS%d" % xi, 4, k_rot[par][:], "k_rot%d" % par, par)
            yield
            for hb in range(2):
                zv_, zvr = proj(par, WW, "WWv", 1024 + hb * 512)
                S.add('act', I('activation', out=v_tok[par][:, hb * 512:(hb + 1) * 512], in_=zv_, func=AF.Copy),
                      reads=[zvr], writes=[R("v_tok%d" % par)])
                if hb == 0:
                    S.add('dve', I('tensor_tensor', out=h3(kdf[par][:]), in0=h3(k_rot[par][:]), in1=bc_d(KDSf, t * 4), op=ALU.mult),
                          reads=[R("k_rot%d" % par), R("KDSf")], writes=[R("kdf%d" % par)])
                    S.add('pool', I('tensor_tensor', out=h3(kdb[par][:]), in0=h3(k_rot[par][:]), in1=bc_d(KDSb, t * 4), op=ALU.mult),
                          reads=[R("k_rot%d" % par), R("KDSb")], writes=[R("kdb%d" % par)])
                yield

        def st_tail(t, ctx):
            par = ctx['par']
            if t >= 49:
                c = NSLOT - t
                sg = sbstg[nstg[0] % 2]
                sgr = "sbstg%d" % (nstg[0] % 2)
                nstg[0] += 1
                S.add('act', I('activation', out=sg[:], in_=S_b[:], func=AF.Copy), reads=[R("S_b")], writes=[R(sgr)])
                S.add('sp', I('dma_start', out=sbst[c * 128:(c + 1) * 128, :], in_=sg[:]), reads=[R(sgr)], writes=[R("sbst_d")], dma=sgr)
            kv_update(par, kdf[par], "kdf%d" % par, PS, PSr, S_f, "S_f", AFt, t * 4)
            yield
            kv_update(par, kdb[par], "kdb%d" % par, PO, POr, S_b, "S_b", ABt, t * 4)
            yield

        ctxs = {0: front_a(xs[0:128, :], css[0:128, :])}
        front_b(ctxs[0])
        prev_tail = None
        for t in range(NSLOT):
            if t + 1 < NSLOT:
                ctxs[t + 1] = front_a(xs[(t + 1) * 128:(t + 2) * 128, :], css[(t + 1) * 128:(t + 2) * 128, :])
            for _ in range(2):
                if deferred and DEFER_W:
                    deferred.pop(0)()
            interleave(prev_tail, st_head(t, ctxs[t]), k=2)
            if t + 1 < NSLOT:
                front_b(ctxs[t + 1])
            prev_tail = st_tail(t, ctxs[t])
        interleave(prev_tail, None)
        while deferred:
            deferred.pop(0)()
        sg = sbstg[nstg[0] % 2]
        sgr = "sbstg%d" % (nstg[0] % 2)
        S.add('act', I('activation', out=sg[:], in_=S_b[:], func=AF.Copy), reads=[R("S_b")], writes=[R(sgr)])
        S.add('sp', I('dma_start', out=sbst[0:128, :], in_=sg[:]), reads=[R(sgr)], writes=[R("sbst_d")], dma=sgr)
        S.add('act', I('activation', out=S_fbf[0][:], in_=S_f[:], func=AF.Copy), reads=[R("S_f")], writes=[R("S_fbf0")])
        if debug:
            final_dma.append(S.add('sp', I('dma_start', out=dbg['sf'], in_=S_f[:]), reads=[R("S_f")], dma='dbgsf'))
            final_dma.append(S.add('sp', I('dma_start', out=dbg['sb'], in_=S_b[:]), reads=[R("S_b")], dma='dbgsb'))

        if stop_after == 'state':
            S.add('sp', None, extra=final_dma)
            S.build()
            return nc
        mode[0] = 'main'

        def r_front(c):
            return front_a(xm[(2 + c) * 128:(3 + c) * 128, :], csm[(2 + c) * 128:(3 + c) * 128, :])

        def load_sb(c):
            sbc = sbstg[c % 2]
            sbr = "sbstg%d" % (c % 2)
            S.add('sp', I('dma_start', out=sbc[:], in_=sbst[c * 128:(c + 1) * 128, :]), reads=[R("sbst_d")], writes=[R(sbr)], dma=sbr)

        def r_head(c, ctx):
            xi, par = ctx['xi'], ctx['par']
            zq, zqr = proj(par, WW, "WWq", 0)
            rotary(zq, zqr, CS[xi], "CS%d" % xi, 4, q_rot[par][:, 0:512], "q_rot", 0)
            yield
            zk, zkr = proj(par, WW, "WW", 512)
            rotary(zk, zkr, CS[xi], "CS%d" % xi, 4, k_rot[par][:], "k_rot%d" % par, 1)
            tr_to(q_rot[par], "q_rot", 4, qT[par][:], "qT%d" % par)
            yield
            for hb in range(2):
                zv_, zvr = proj(par, WW, "WWv", 1024 + hb * 512)
                S.add('act', I('activation', out=v_tok[par][:, hb * 512:(hb + 1) * 512], in_=zv_, func=AF.Copy),
                      reads=[zvr], writes=[R("v_tok%d" % par)])
                if hb == 0:
                    tr_to(k_rot[par], "k_rot%d" % par, 4, kT[par][:], kTR[par])
                    S.add('dve', I('tensor_tensor', out=h3(kdf[par][:]), in0=h3(k_rot[par][:]), in1=bc_d(KD8, 0), op=ALU.mult),
                          reads=[R("k_rot%d" % par), R("KD8")], writes=[R("kdf%d" % par)])
                yield
            for hb in range(2):
                zg, zgr = proj(par, WW, "WWq", 2048 + hb * 512)
                S.add('act', I('activation', out=g_r[par][:, hb * 512:(hb + 1) * 512], in_=zg, func=AF.Silu),
                      reads=[zgr], writes=[R("g_r%d" % par)])
                yield
            S.add('pool', I('tensor_tensor', out=g_r[par][:], in0=g_r[par][:], in1=nw_t[:], op=ALU.mult),
                  reads=[R("g_r%d" % par), R("nw_t")], writes=[R("g_r%d" % par)])
            for hb in range(2):
                zg, zgr = proj(par, WW, "WWq", 3072 + hb * 512)
                S.add('act', I('activation', out=sg_r[par][:, hb * 512:(hb + 1) * 512], in_=zg, func=AF.Tanh, scale=0.5),
                      reads=[zgr], writes=[R("sg_r%d" % par)])
                yield

        def r_tail_a(c, ctx):
            par = ctx['par']
            sbc = sbstg[c % 2]
            sbr = "sbstg%d" % (c % 2)
            for h in range(4):
                hs = slice(h * 128, (h + 1) * 128)
                S.add('pe', I('matmul', out=PS[:, hs], lhsT=kT[par][:, hs], rhs=qT[par][:, hs], start=True, stop=True),
                      reads=[R(kTR[par]), R("qT%d" % par)], writes=[PSr[0]])
            S.add('dve', I('tensor_tensor', out=sd[par][:], in0=PS[:, 0:512], in1=DT[:], op=ALU.mult), reads=[PSr[0], R("DT")], writes=[R("sd")])
            S.add('dve', I('tensor_tensor', out=qdf[par][:], in0=qT[par][:], in1=QDF[:], op=ALU.mult), reads=[R("qT%d" % par), R("QDF")], writes=[R("qdf")])
            S.add('pool', I('tensor_tensor', out=qdb[par][:], in0=qT[par][:], in1=QDB[:], op=ALU.mult), reads=[R("qT%d" % par), R("QDB")], writes=[R("qdb")])
            yield
            sfb = S_fbf[0]
            sfr = "S_fbf0"
            for h in range(4):
                hs = slice(h * 128, (h + 1) * 128)
                vs = slice(h * 256, (h + 1) * 256)
                S.add('pe', I('matmul', out=PO[:, vs], lhsT=qdf[par][:, hs], rhs=sfb[:, vs], start=True, stop=False),
                      reads=[R("qdf"), R(sfr)], writes=[POr[h // 2]])
                S.add('pe', I('matmul', out=PO[:, vs], lhsT=qdb[par][:, hs], rhs=sbc[:, vs], start=False, stop=False),
                      reads=[R("qdb"), R(sbr)], writes=[POr[h // 2]])
                S.add('pe', I('matmul', out=PO[:, vs], lhsT=sd[par][:, hs], rhs=v_tok[par][:, vs], start=False, stop=True),
                      reads=[R("sd"), R("v_tok%d" % par)], writes=[POr[h // 2]])
            if c < NOWN - 1:
                kv_update(par, kdf[par], "kdf%d" % par, PS, PSr, S_f, "S_f", CD8, 0)
                S.add('dve', I('tensor_copy', out=S_fbf[0][:], in_=S_f[:]), reads=[R("S_f")], writes=[R("S_fbf0")])
            for h in range(4):
                S.add('dve', I('bn_stats', out=gnst[:, h * 6:(h + 1) * 6], in_=PO[:, h * 256:(h + 1) * 256]), reads=[POr[h // 2]], writes=[R("gnst")])
            for h in range(4):
                S.add('dve', I('bn_aggr', out=gnmv[:, h * 2:(h + 1) * 2], in_=gnst[:, h * 6:(h + 1) * 6]), reads=[R("gnst")], writes=[R("gnmv")])
            mvv = gnmv[:].rearrange("p (h t) -> p h t", h=4, t=2)
            S.add('dve', I('tensor_scalar', out=gnve[:], in0=mvv[:, :, 1], scalar1=EPS, scalar2=None, op0=ALU.add), reads=[R("gnmv")], writes=[R("gnve")])
            S.add('pool', I('tensor_tensor', out=gnrs[:], in0=gnve[:], in1=mhalf[:, 0:4], op=ALU.pow), reads=[R("gnve"), R("mhalf")], writes=[R("gnrs")])
            S.add('dve', I('scalar_tensor_tensor', out=gnnm[:], in0=mvv[:, :, 0], scalar=-1.0, in1=gnrs[:], op0=ALU.mult, op1=ALU.mult),
                  reads=[R("gnmv"), R("gnrs")], writes=[R("gnnm")])
            for h in range(4):
                vs = slice(h * 256, (h + 1) * 256)
                S.add('dve', I('tensor_scalar', out=o_n[:, vs], in0=PO[:, vs], scalar1=gnrs[:, h:h + 1], scalar2=gnnm[:, h:h + 1], op0=ALU.mult, op1=ALU.add),
                      reads=[POr[h // 2], R("gnrs"), R("gnnm")], writes=[R("o_n")])
            S.add('dve', I('tensor_tensor', out=o_g[:], in0=o_n[:], in1=g_r[par][:], op=ALU.mult), reads=[R("o_n"), R("g_r%d" % par)], writes=[R("o_g")])
            yield

        def r_tail_b(c, ctx):
            par = ctx['par']
            tr_to(o_g, "o_g", 8, o_gT[:].rearrange("p a b -> p (a b)"), "o_gT")
            yield
            for nb_ in range(2):
                for kt in range(8):
                    S.add('pe', I('matmul', out=PO[:, nb_ * 512:(nb_ + 1) * 512], lhsT=o_gT[:, kt, :], rhs=W2[:, kt, nb_ * 512:(nb_ + 1) * 512],
                                  start=(kt == 0), stop=(kt == 7)),
                          reads=[R("o_gT"), R("W2")], writes=[POr[nb_]])
            mr = mixr[0]
            mrr = "mixr"
            for nb_ in range(2):
                S.add('dve', I('scalar_tensor_tensor', out=mr[:, nb_ * 512:(nb_ + 1) * 512], in0=sg_r[par][:, nb_ * 512:(nb_ + 1) * 512], scalar=1.0,
                               in1=PO[:, nb_ * 512:(nb_ + 1) * 512], op0=ALU.add, op1=ALU.mult),
                      reads=[R("sg_r%d" % par), POr[nb_]], writes=[R(mrr)])
            S.add('sp', I('dma_start', out=mixrd[c * 128:(c + 1) * 128, :], in_=mr[:]), reads=[R(mrr)], writes=[R("mixr_d")], dma=mrr)
            if debug:
                final_dma.append(S.add('sp', I('dma_start', out=dbg['mixr'][c * 128:(c + 1) * 128, :], in_=mr[:]), reads=[R(mrr)], dma='dbgm'))
            yield

        ctxs = {0: r_front(0)}
        front_b(ctxs[0])
        def chain(*gens):
            for g_ in gens:
                if g_ is not None:
                    yield from g_

        def alt(ga, gb):
            live = [g_ for g_ in (ga, gb) if g_ is not None]
            while live:
                for g_ in list(live):
                    try:
                        next(g_)
                        yield
                    except StopIteration:
                        live.remove(g_)

        ctxs[1] = r_front(1)
        for c in range(NOWN + 2):
            if c < NOWN:
                load_sb(c)
            if c + 2 < NOWN:
                ctxs[c + 2] = r_front(c + 2)
            if c + 1 < NOWN:
                front_b(ctxs[c + 1])
            if c == NOWN:
                ld(nw_t[:], postw, "nw_t", key='pw')
                def reload(rn, sres, d0, n, key):
                    S.add('sp', I('dma_start', out=WW[:, :, d0:d0 + n], in_=wbf[:, d0:d0 + n].rearrange("(kt p) c -> p kt c", p=128)),
                          reads=[R(sres)], writes=[R("WW"), R("WWv"), R("WWq"), R(rn)], dma=key)
                reload("WAkv", "wbf_kv", 1024, 512, 'w4a')
                reload("WAq", "wbf_q", 0, 1024, 'w4b')
                reload("WAg", "wbf_g", 1536, 1024, 'w4c')
                reload("WAs", "wbf_s", 2560, 1024, 'w4d')
            tb = r_tail_b(c - 2, ctxs[c - 2]) if 0 <= c - 2 < NOWN else None
            ta = r_tail_a(c - 1, ctxs[c - 1]) if 0 <= c - 1 < NOWN else None
            hd = r_head(c, ctxs[c]) if c < NOWN else None
            interleave(alt(tb, ta), hd, k=2, lead=1)

        if stop_after == 'R':
            S.add('sp', None, extra=final_dma)
            S.build()
            return nc
        S.add('sp', I('dma_start', out=W2[:], in_=wabf.rearrange("(kt p) c -> p kt c", p=128)), reads=[R("wabf")], writes=[R("W2")], dma='w6')
        while deferred:
            deferred.pop(0)()

        for p in range(2):
            for g in range(2):
                S.add('pool', I('memset', PTm[p][g][:], 0.0), writes=[R(PTmR[p][g])])
        for p in range(2):
            for g in range(2):
                for hh in range(4):
                    h = g * 4 + hh
                    S.add('act', I('activation', out=PTm[p][g][32:33, hh * 128:(hh + 1) * 128], in_=zrow[32:33, :], func=AF.Exp,
                                   bias=sink_t[32:33, h:h + 1], scale=1.0),
                          reads=[R("zrow"), R("sink_t")], writes=[R(PTmR[p][g])])

        def a_head(idx, ctx, nctx=None):
            xi, par = ctx['xi'], ctx['par']
            zkv, zkvr = proj(par, WW, "WAkv", 1024)
            if idx == 0:
                ktile, kres, vtile, vres = KTm, "KTm", Vm, "Vm"
            else:
                ktile, kres, vtile, vres = KT[idx % 4], "KT%d" % (idx % 4), VA[idx % 4], "VA%d" % (idx % 4)
            if idx == 0:
                S.add('act', I('activation', out=Vm[0:16, :, 0:128], in_=zkv[0:16, 256:512].rearrange("p (g d) -> p g d", g=2, d=128), func=AF.Copy),
                      reads=[zkvr], writes=[R("Vm")])
            else:
                S.add('act', I('activation', out=vtile[:, :, 0:128], in_=zkv[:, 256:512].rearrange("p (g d) -> p g d", g=2, d=128), func=AF.Copy),
                      reads=[zkvr], writes=[R(vres)])
            rotary(zkv[:, 0:256], zkvr, CS[xi], "CS%d" % xi, 2, k_rot[par][:, 0:256], "k_rot%d" % par, 0)
            yield
            if 2 <= idx <= 17:
                c = idx - 2
                pa_c = c % 2
                own_info[c] = (xi, pa_c)
                for hb in range(2):
                    zq, zqr = proj(par, WW, "WAq", hb * 512)
                    rotary(zq, zqr, CS[xi], "CS%d" % xi, 4, q_rot[par][:, hb * 512:(hb + 1) * 512], "q_rot", 1)
                    if hb == 0:
                        tr_to(k_rot[par], "k_rot%d" % par, 2, ktile[:].rearrange("p a b -> p (a b)"), kres)
                    yield
                for hb in range(2):
                    zg, zgr = proj(par, WW, "WAg", 1536 + hb * 512)
                    S.add('act', I('activation', out=th_t[:], in_=zg, func=AF.Tanh, scale=0.5), reads=[zgr], writes=[R("QDF")])
                    S.add('dve', I('scalar_tensor_tensor', out=g_r[pa_c][:, hb * 512:(hb + 1) * 512], in0=th_t[:], scalar=1.0, in1=zg, op0=ALU.add, op1=ALU.mult),
                          reads=[R("QDF"), zgr], writes=[R("g_r%d" % pa_c)])
                    if hb == 0:
                        tr_to(q_rot[par], "q_rot", 8, v_tok[pa_c][:], "v_tok%d" % pa_c)
                    yield
                for hb in range(2):
                    zg, zgr = proj(par, WW, "WAs", 2560 + hb * 512)
                    S.add('act', I('activation', out=sg_r[pa_c][:, hb * 512:(hb + 1) * 512], in_=zg, func=AF.Tanh, scale=0.5),
                          reads=[zgr], writes=[R("sg_r%d" % pa_c)])
                    yield
            else:
                tr_to(k_rot[par], "k_rot%d" % par, 2, ktile[:].rearrange("p a b -> p (a b)"), kres)
                yield
            if nctx is not None:
                front_b(nctx)
                yield

        def a_post(c, xi_c):
            q = c % 2
            S.add('act', I('activation', out=o_n[:], in_=PO[:, :], func=AF.Square, accum_out=ssq[q][:]), reads=POr, writes=[R("o_n"), R("ssq%d" % q)])
            S.add('dve', I('tensor_scalar', out=vv[q][:], in0=ssq[q][:], scalar1=1.0 / D_MODEL, scalar2=EPS, op0=ALU.mult, op1=ALU.add),
                  reads=[R("ssq%d" % q)], writes=[R("vv%d" % q)])
            S.add('pool', I('tensor_tensor', out=rstd[q][:], in0=vv[q][:], in1=mhalf[:, 0:1], op=ALU.pow), reads=[R("vv%d" % q), R("mhalf")], writes=[R("rstd%d" % q)])
            for nb_ in range(2):
                cs_ = slice(nb_ * 512, (nb_ + 1) * 512)
                S.add('dve', I('scalar_tensor_tensor', out=tfin[:, cs_], in0=PO[:, cs_], scalar=rstd[q][:], in1=nw_t[:, cs_], op0=ALU.mult, op1=ALU.mult),
                      reads=[POr[nb_], R("rstd%d" % q), R("nw_t")], writes=[R("S_b")])
            S.add('pool', I('tensor_tensor', out=tfin[:], in0=tfin[:], in1=XB[xi_c][:], op=ALU.add), reads=[R("S_b"), R("XB%d" % xi_c)], writes=[R("S_b")])
            final_dma.append(S.add('sp', I('dma_start', out=out[c * 128:(c + 1) * 128, :], in_=tfin[:]), reads=[R("S_b")], dma="resb"))

        def a_tail(c, xi_c, pa_c, post=None):
            if post is not None:
                a_post(*post)
            blocks = []
            for bi, idx in enumerate((c + 1, c + 2, c + 3)):
                if bi == 0:
                    mk = 2 if c == 0 else 0
                elif bi == 2:
                    mk = 3 if c == NOWN - 1 else 1
                else:
                    mk = None
                blocks.append((idx % 4, mk))
            pm = c % 2
            def tr_group(g):
                tr_to(o_g, "o_g", 4, o_gT[:, g * 4:(g + 1) * 4, :].rearrange("p a b -> p (a b)"), "o_gT", src_off=g * 512)

            for g in range(2):
                qg = v_tok[pa_c][:, g * 512:(g + 1) * 512]
                for bi, (sl, mk) in enumerate(blocks):
                    bank = PS[:, (bi % 2) * 512:(bi % 2 + 1) * 512]
                    br = PSr[bi % 2]
                    S.add('pe', I('matmul', out=bank, lhsT=KT[sl][:, g, :], rhs=qg, start=True, stop=(mk is None)),
                          reads=[R("KT%d" % sl), R("v_tok%d" % pa_c)], writes=[br])
                    if mk is not None:
                        S.add('pe', I('matmul', out=bank, lhsT=ident[:], rhs=mask_t[:, mk * 512:(mk + 1) * 512], start=False, stop=True),
                              reads=[R("ident_t"), R("mask_t")], writes=[br])
                    S.add('act', I('activation', out=PTb[g][bi][:], in_=bank, func=AF.Exp, scale=SCALE),
                          reads=[br], writes=[R(PTbR[g][bi])])
                bank = PS[0:16, 512:1024]
                S.add('pe', I('matmul', out=bank, lhsT=KTm[:, g, 0:16], rhs=qg, start=True, stop=True),
                      reads=[R("KTm"), R("v_tok%d" % pa_c)], writes=[PSr[1]])
                S.add('act', I('activation', out=PTm[pm][g][0:16, :], in_=bank, func=AF.Exp, scale=SCALE),
                      reads=[PSr[1]], writes=[R(PTmR[pm][g])])
                yield
                for hh in range(4):
                    dst = PO[:, hh * 129:(hh + 1) * 129] if hh < 3 else PO[:, 512:641]
                    dr = POr[0] if hh < 3 else POr[1]
                    for bi, (sl, mk) in enumerate(blocks):
                        S.add('pe', I('matmul', out=dst, lhsT=PTb[g][bi][:, hh * 128:(hh + 1) * 128], rhs=VA[sl][:, g, 0:129],
                                      start=(bi == 0), stop=False),
                              reads=[R(PTbR[g][bi]), R("VA%d" % sl)], writes=[dr])
                    S.add('pe', I('matmul', out=dst, lhsT=PTm[pm][g][0:33, hh * 128:(hh + 1) * 128], rhs=Vm[0:33, g, 0:129], start=False, stop=True),
                          reads=[R(PTmR[pm][g]), R("Vm")], writes=[dr])
                if g == 1:
                    tr_group(0)
                l3 = PO[:, 0:387].rearrange("p (h c) -> p h c", h=3, c=129)[:, :, 128]
                S.add('dve', I('reciprocal', out=rl[:, 0:3], in_=l3), reads=[POr[0]], writes=[R("rl")])
                S.add('dve', I('reciprocal', out=rl[:, 3:4], in_=PO[:, 640:641]), reads=[POr[1], R("rl")], writes=[R("rl")])
                S.add('dve', I('tensor_scalar', out=rl[:], in0=rl[:], scalar1=0.5, scalar2=None, op0=ALU.mult), reads=[R("rl")], writes=[R("rl")])
                for hh in range(4):
                    h = g * 4 + hh
                    src_ = PO[:, hh * 129:hh * 129 + 128] if hh < 3 else PO[:, 512:640]
                    dr = POr[0] if hh < 3 else POr[1]
                    S.add('dve', I('scalar_tensor_tensor', out=o_g[:, h * 128:(h + 1) * 128], in0=src_, scalar=rl[:, hh:hh + 1],
                                   in1=g_r[pa_c][:, h * 128:(h + 1) * 128], op0=ALU.mult, op1=ALU.mult),
                          reads=[dr, R("rl"), R("g_r%d" % pa_c)], writes=[R("o_g")])
                if g == 0:
                    S.add('sp', I('dma_start', out=mixr[0][:], in_=mixrd[c * 128:(c + 1) * 128, :]), reads=[R("mixr_d")], writes=[R("mixr")], dma="mixr")
                yield
            tr_group(1)
            for nb_ in range(2):
                for kt in range(8):
                    S.add('pe', I('matmul', out=PO[:, nb_ * 512:(nb_ + 1) * 512], lhsT=o_gT[:, kt, :], rhs=W2[:, kt, nb_ * 512:(nb_ + 1) * 512],
                                  start=(kt == 0), stop=(kt == 7)),
                          reads=[R("o_gT"), R("W2")], writes=[POr[nb_]])
            for nb_ in range(2):
                cs_ = slice(nb_ * 512, (nb_ + 1) * 512)
                S.add('dve', I('scalar_tensor_tensor', out=mixa[:, cs_], in0=sg_r[pa_c][:, cs_], scalar=1.0, in1=PO[:, cs_], op0=ALU.add, op1=ALU.mult),
                      reads=[R("sg_r%d" % pa_c), POr[nb_]], writes=[R("S_f")])
            S.add('dve', I('tensor_tensor', out=mixb[:], in0=mixa[:, 0:1024], in1=mixr[0][:], op=ALU.add), reads=[R("S_f"), R("mixr")], writes=[R("o_n")])
            yield
            tr_to(mixb, "o_n", 8, S_fbf[0][:], "S_fbf0", scale=0.5)
            yield
            for nb_ in range(2):
                for kt in range(8):
                    S.add('pe', I('matmul', out=PO[:, nb_ * 512:(nb_ + 1) * 512], lhsT=mixT[:, kt, :], rhs=W3[:, kt, nb_ * 512:(nb_ + 1) * 512],
                                  start=(kt == 0), stop=(kt == 7)),
                          reads=[R("S_fbf0"), R("W3")], writes=[POr[nb_]])
            yield

        own_info = {}
        ctxs = {0: front_a(xm[0:128, :], csm[0:128, :], ring=4)}
        front_b(ctxs[0])
        for idx in range(a_steps):
            if idx + 1 < 19:
                ctxs[idx + 1] = front_a(xm[(idx + 1) * 128:(idx + 2) * 128, :], csm[(idx + 1) * 128:(idx + 2) * 128, :], ring=4)
            tail = None
            if idx >= 3:
                c_ = idx - 3
                post = (c_ - 1, own_info[c_ - 1][0]) if c_ >= 1 else None
                tail = a_tail(c_, *own_info[c_], post=post)
            hd = a_head(idx, ctxs[idx], ctxs.get(idx + 1))
            if PIPE_A:
                interleave(tail, hd, k=1, lead=2)
            else:
                interleave(None, hd)
                interleave(tail, None)

        if a_steps == 19:
            a_post(NOWN - 1, own_info[NOWN - 1][0])
        S.add('sp', None, extra=final_dma)
        S.build()
        print("sched stats", S.stats, flush=True)
    return nc


def _host_prep(inputs):
    x = np.asarray(inputs["x"], np.float32)
    meta = np.asarray(inputs["meta_tokens"], np.float32)
    import jax
    import jax.numpy as jnp
    _cpu = jax.devices("cpu")[0]
    with jax.default_device(_cpu):
        inv_j = 10000.0 ** (-jnp.arange(64, dtype=jnp.float32) * 2.0 / 128)

    def chunk_data(b, n):
        if n == 0:
            z = np.zeros((128, 1024), np.float32)
            z[112:] = meta
            return z
        if n > 64:
            return np.zeros((128, 1024), np.float32)
        return x[b, (n - 1) * 128:n * 128]

    def cs_of(pos):
        with jax.default_device(_cpu):
            ang = jnp.asarray(np.asarray(pos, np.float32))[:, None] * inv_j[None, :]
            return np.concatenate([np.asarray(jnp.cos(ang), np.float32), np.asarray(jnp.sin(ang), np.float32)], axis=1)

    def chunk_pos(n):
        return np.maximum(n * 128 + np.arange(128) - 112, 0)

    jj = np.arange(128)[:, None].astype(np.float64)
    ii = np.arange(128)[None, :].astype(np.float64)
    sc = 128.0 ** -0.5
    rtab = np.concatenate([np.maximum(ii - jj, 0), (ii >= jj) * sc, np.maximum(jj - ii, 0), (jj > ii) * sc], axis=1).astype(np.float32)
    ctab = np.concatenate([np.broadcast_to(ii + 1, (128, 128)), np.broadcast_to(128 - ii, (128, 128))], axis=1).astype(np.float32)
    jtab = np.concatenate([127 - jj, jj], axis=1).astype(np.float32)
    bf = ml_dtypes.bfloat16
    m_prev = np.where(jj >= ii, 0.0, NEG).astype(np.float32)
    m_next = np.where(jj <= ii, 0.0, NEG).astype(np.float32)
    m_all = np.full((128, 128), NEG, np.float32)
    ident = np.eye(128, dtype=np.float32).astype(bf)

    def bc(v, n):
        return np.ascontiguousarray(np.broadcast_to(np.asarray(v, np.float32).reshape(1, n), (128, n)))

    common = dict(
        w_in=np.ascontiguousarray(inputs["w_in"][0], dtype=np.float32),
        w_rb=np.ascontiguousarray(inputs["w_ret_branch"][0], dtype=np.float32),
        w_ab=np.ascontiguousarray(inputs["w_attn_branch"][0], dtype=np.float32),
        w_o=np.ascontiguousarray(inputs["w_out"][0], dtype=np.float32),
        prew=bc(inputs["pre_norm_w"][0], 1024), postw=bc(inputs["post_norm_w"][0], 1024), nwb=bc(inputs["ret_norm_w"][0], 1024),
        dec8=bc(np.concatenate([np.asarray(inputs["ret_decay_fwd"][0]), np.asarray(inputs["ret_decay_bwd"][0])]), 8),
        sink8=bc(inputs["attn_sink"][0], 8),
        rtab=rtab, ctab=ctab, jtab=jtab, ident=ident,
    )
    in_maps = []
    for core in range(8):
        b, s = divmod(core, 4)
        fwd = list(range(0, 16 * s + 1))
        bwd = list(range(64, 16 * s + 16, -1))
        own = [16 * s + 1 + c for c in range(15, 0, -1)]
        slots = fwd + bwd + own
        assert len(slots) == NSLOT
        xs = np.concatenate([chunk_data(b, n) for n in slots], axis=0)
        css = np.concatenate([cs_of(chunk_pos(n)) for n in slots], axis=0)
        actf = np.array([1.0] * len(fwd) + [0.0] * (NSLOT - len(fwd)), np.float32)
        acttab = bc(np.concatenate([actf, 1.0 - actf]), 128)
        metac = np.zeros((128, 1024), np.float32)
        metac[:16] = meta
        mpos = np.zeros(128)
        mpos[:16] = np.arange(16)
        mchunks = [16 * s] + [16 * s + 1 + c for c in range(16)] + [16 * s + 17]
        xm = np.concatenate([metac] + [chunk_data(b, n) for n in mchunks], axis=0)
        csm = np.concatenate([cs_of(mpos)] + [cs_of(chunk_pos(n)) for n in mchunks], axis=0)
        mk = [m_prev, m_next, m_all if s == 0 else m_prev, m_all if s == 3 else m_next]
        masks = np.concatenate([np.tile(m, (1, 4)) for m in mk], axis=1).astype(bf)
        d = dict(common)
        d.update(xs=xs, css=css, xm=xm, csm=csm, acttab=acttab, masks=masks)
        in_maps.append(d)
    return in_maps


_NC_CACHE = {}


def kernel(**inputs):
    in_maps = _host_prep(inputs)
    if 'nc' not in _NC_CACHE:
        _NC_CACHE['nc'] = build_nc()
    nc = _NC_CACHE['nc']
    res = run_bass_kernel_spmd(nc, in_maps, core_ids=list(range(8)))
    out = np.empty((2, SEQ, D_MODEL), np.float32)
    for core in range(8):
        b, s = divmod(core, 4)
        out[b, s * 2048:(s + 1) * 2048] = np.asarray(res.results[core]["out"], np.float32)
    return out
```
